# Optimizing a Trainium2 kernel written in Bass

```python
import jax, jax.numpy as jnp
from jax import lax
import numpy as np

D_MODEL = 1024
BATCH = 8
SEQ = 4096
DEPTH = 2
DEC_BATCH = 1
DEC_SEQ = 16384
PAST_LEN = 128

EPS = 1e-6
N_BRANCH = 4
BRANCH_WIDTH = 256
A_WIDTH = BRANCH_WIDTH
A_GROUPS = 4
CONV_WIDTH = 3
POOL_WIDTH = BRANCH_WIDTH
POOL_WINDOWS = (2, 4, 8, 16)
POOL_GROUPS = len(POOL_WINDOWS)
POOL_GDIM = POOL_WIDTH // POOL_GROUPS
SG_WIDTH = BRANCH_WIDTH
SG_GROUPS = 4
SG_GDIM = SG_WIDTH // SG_GROUPS
CHUNK = 128
N_Q_HEADS = 4
N_KV_HEADS = 2
GQA_GROUP = N_Q_HEADS // N_KV_HEADS
HEAD_DIM = 64
Q_DIM = N_Q_HEADS * HEAD_DIM
KV_DIM = N_KV_HEADS * HEAD_DIM
WINDOW = 128
ATTN_BLOCK = 128
NEG_BIG = -1e30
IN_A = 3 * A_WIDTH
IN_POOL = POOL_WIDTH
IN_SG = 2 * SG_WIDTH
IN_ATTN = Q_DIM + 2 * KV_DIM
D_IN = IN_A + IN_POOL + IN_SG + IN_ATTN
D_FF = ((8 * D_MODEL + 3 * 256 - 1) // (3 * 256)) * 256

kernel_name = "hybrid_gated_parallel_encoder"


def rms_norm(x, gain):
    xf = x.astype(jnp.float32)
    y = xf * lax.rsqrt(jnp.mean(xf * xf, axis=-1, keepdims=True) + EPS)
    return (y * gain.astype(jnp.float32)).astype(x.dtype)


def layer_norm(x, gain):
    xf = x.astype(jnp.float32)
    mu = jnp.mean(xf, axis=-1, keepdims=True)
    xc = xf - mu
    y = xc * lax.rsqrt(jnp.mean(xc * xc, axis=-1, keepdims=True) + EPS)
    return (y * gain.astype(jnp.float32)).astype(x.dtype)


def alibi_slopes():
    return jnp.exp2(-8.0 * (jnp.arange(N_Q_HEADS, dtype=jnp.float32) + 1.0) / N_Q_HEADS)


def short_conv_mixer(z, conv_w):
    b_gate, c_gate, xin = jnp.split(z, 3, axis=-1)
    u = c_gate * xin
    s = u.shape[1]
    up = jnp.pad(u, ((0, 0), (1, 1), (0, 0)))
    conv = conv_w[0] * up[:, :s] + conv_w[1] * up[:, 1:s + 1] + conv_w[2] * up[:, 2:]
    return b_gate * conv


def pool_mixer(z, pool_w, pool_scale):
    b, s, _ = z.shape
    zf = z.astype(jnp.float32)
    cs = jnp.concatenate([jnp.zeros((b, 1, POOL_WIDTH), jnp.float32), jnp.cumsum(zf, axis=1)], axis=1)
    pos = jnp.arange(s)
    outs = []
    for g, w in enumerate(POOL_WINDOWS):
        lo = jnp.clip(pos - w // 2, 0, s)
        hi = jnp.clip(pos + w // 2, 0, s)
        csg = cs[..., g * POOL_GDIM:(g + 1) * POOL_GDIM]
        total = jnp.take(csg, hi, axis=1) - jnp.take(csg, lo, axis=1)
        cnt = (hi - lo).astype(jnp.float32)[None, :, None]
        outs.append(total / cnt)
    pooled = (jnp.concatenate(outs, axis=-1) - zf).astype(z.dtype).reshape(b, s, POOL_GROUPS, POOL_GDIM)
    mixed = jnp.einsum('bsgc,gcd->bsgd', pooled, pool_w).reshape(b, s, POOL_WIDTH)
    return mixed * pool_scale


def spatial_gating_mixer(z, sg_norm, sg_w, sg_b):
    b, s, _ = z.shape
    u, v = jnp.split(z, 2, axis=-1)
    v = layer_norm(v, sg_norm).reshape(b, s // CHUNK, CHUNK, SG_GROUPS, SG_GDIM)
    mixed = jnp.einsum('gts,bnsgc->bntgc', sg_w, v) + jnp.transpose(sg_b)[None, None, :, :, None]
    return u * mixed.reshape(b, s, SG_WIDTH)


def windowed_gqa(z, sink):
    b, s, _ = z.shape
    q, k, v = jnp.split(z, [Q_DIM, Q_DIM + KV_DIM], axis=-1)
    nb = s // ATTN_BLOCK
    q = q.reshape(b, nb, ATTN_BLOCK, N_KV_HEADS, GQA_GROUP, HEAD_DIM)

    def band(t):
        tp = jnp.pad(t, ((0, 0), (ATTN_BLOCK, ATTN_BLOCK), (0, 0)))
        tp = tp.reshape(b, nb + 2, ATTN_BLOCK, N_KV_HEADS, HEAD_DIM)
        return jnp.concatenate([tp[:, :-2], tp[:, 1:-1], tp[:, 2:]], axis=2)

    kb, vb = band(k), band(v)
    scores = jnp.einsum('bnqkgd,bnskd->bnkgqs', q, kb).astype(jnp.float32) * (HEAD_DIM ** -0.5)
    qi = jnp.arange(ATTN_BLOCK)[:, None]
    kj = jnp.arange(3 * ATTN_BLOCK)[None, :]
    rel = qi - kj + ATTN_BLOCK
    key_pos = jnp.arange(nb)[:, None] * ATTN_BLOCK - ATTN_BLOCK + kj
    valid = (jnp.abs(rel) <= WINDOW)[None] & ((key_pos >= 0) & (key_pos < s))[:, None, :]
    slopes = alibi_slopes().reshape(N_KV_HEADS, GQA_GROUP)
    alibi = -slopes[:, :, None, None] * jnp.abs(rel).astype(jnp.float32)[None, None]
    scores = jnp.where(valid[None, :, None, None], scores + alibi[None, None], NEG_BIG)
    sink_l = sink.astype(jnp.float32).reshape(N_KV_HEADS, GQA_GROUP)[None, None, :, :, None, None]
    m = jnp.maximum(jnp.max(scores, axis=-1, keepdims=True), sink_l)
    p = jnp.exp(scores - m)
    denom = jnp.sum(p, axis=-1, keepdims=True) + jnp.exp(sink_l - m)
    probs = (p / denom).astype(vb.dtype)
    out = jnp.einsum('bnkgqs,bnskd->bnqkgd', probs, vb)
    return out.reshape(b, s, Q_DIM)


def trunk(x, norm_mix_pre, norm_mix_post, norm_ffn_pre, norm_ffn_post, w_in, conv_w,
          pool_w, pool_scale, sg_norm, sg_w, sg_b, attn_sink, w_branch, w_gate, b_gate,
          w_out, w_ffn_in, w_ffn_out):
    for l in range(DEPTH):
        h = rms_norm(x, norm_mix_pre[l])
        z = h @ w_in[l]
        z_a, z_p, z_s, z_d = jnp.split(z, [IN_A, IN_A + IN_POOL, IN_A + IN_POOL + IN_SG], axis=-1)
        branches = (
            short_conv_mixer(z_a, conv_w[l]),
            pool_mixer(z_p, pool_w[l], pool_scale[l]),
            spatial_gating_mixer(z_s, sg_norm[l], sg_w[l], sg_b[l]),
            windowed_gqa(z_d, attn_sink[l]),
        )
        merged = None
        for i, br in enumerate(branches):
            gate = jax.nn.sigmoid(h @ w_gate[l, i] + b_gate[l, i])
            term = gate * (br @ w_branch[l, i])
            merged = term if merged is None else merged + term
        x = x + rms_norm(merged @ w_out[l], norm_mix_post[l])
        h = rms_norm(x, norm_ffn_pre[l])
        a, g = jnp.split(h @ w_ffn_in[l], 2, axis=-1)
        x = x + rms_norm((jax.nn.silu(a) * g) @ w_ffn_out[l], norm_ffn_post[l])
    return x


def setup_inputs(seed: int = 0) -> dict:
    key = jax.random.key(seed)
    ks = jax.random.split(key, 24)
    f32 = jnp.float32

    def nrm(k, shape, scale):
        return jax.random.normal(k, shape, f32) * scale

    def gain(k, shape):
        return 1.0 + 0.1 * jax.random.normal(k, shape, f32)

    return {
        "x_prompt": nrm(ks[0], (BATCH, SEQ, D_MODEL), 1.0),
        "x_sample": nrm(ks[1], (DEC_BATCH, DEC_SEQ, D_MODEL), 1.0),
        "norm_mix_pre": gain(ks[2], (DEPTH, D_MODEL)),
        "norm_mix_post": gain(ks[3], (DEPTH, D_MODEL)),
        "norm_ffn_pre": gain(ks[4], (DEPTH, D_MODEL)),
        "norm_ffn_post": gain(ks[5], (DEPTH, D_MODEL)),
        "w_in": nrm(ks[6], (DEPTH, D_MODEL, D_IN), D_MODEL ** -0.5),
        "conv_w": nrm(ks[7], (DEPTH, CONV_WIDTH, A_WIDTH), CONV_WIDTH ** -0.5),
        "pool_w": nrm(ks[8], (DEPTH, POOL_GROUPS, POOL_GDIM, POOL_GDIM), POOL_GDIM ** -0.5),
        "pool_scale": gain(ks[9], (DEPTH, POOL_WIDTH)),
        "sg_norm": gain(ks[10], (DEPTH, SG_WIDTH)),
        "sg_w": nrm(ks[11], (DEPTH, SG_GROUPS, CHUNK, CHUNK), CHUNK ** -0.5),
        "sg_b": gain(ks[12], (DEPTH, SG_GROUPS, CHUNK)),
        "attn_sink": nrm(ks[13], (DEPTH, N_Q_HEADS), 1.0),
        "w_branch": nrm(ks[14], (DEPTH, N_BRANCH, BRANCH_WIDTH, D_MODEL), BRANCH_WIDTH ** -0.5),
        "w_gate": nrm(ks[15], (DEPTH, N_BRANCH, D_MODEL, D_MODEL), D_MODEL ** -0.5),
        "b_gate": nrm(ks[16], (DEPTH, N_BRANCH, D_MODEL), 0.1),
        "w_out": nrm(ks[17], (DEPTH, D_MODEL, D_MODEL), D_MODEL ** -0.5),
        "w_ffn_in": nrm(ks[18], (DEPTH, D_MODEL, 2 * D_FF), D_MODEL ** -0.5),
        "w_ffn_out": nrm(ks[19], (DEPTH, D_FF, D_MODEL), D_FF ** -0.5),
    }


def reference(x_prompt, x_sample, norm_mix_pre, norm_mix_post, norm_ffn_pre, norm_ffn_post,
              w_in, conv_w, pool_w, pool_scale, sg_norm, sg_w, sg_b, attn_sink, w_branch,
              w_gate, b_gate, w_out, w_ffn_in, w_ffn_out):
    y_prompt = trunk(x_prompt, norm_mix_pre, norm_mix_post, norm_ffn_pre, norm_ffn_post, w_in, conv_w,
                     pool_w, pool_scale, sg_norm, sg_w, sg_b, attn_sink, w_branch, w_gate, b_gate,
                     w_out, w_ffn_in, w_ffn_out)
    y_sample = trunk(x_sample, norm_mix_pre, norm_mix_post, norm_ffn_pre, norm_ffn_post, w_in, conv_w,
                     pool_w, pool_scale, sg_norm, sg_w, sg_b, attn_sink, w_branch, w_gate, b_gate,
                     w_out, w_ffn_in, w_ffn_out)
    return (y_prompt, y_sample)
```

```python
import numpy as np
import concourse.bass as bass
import concourse.mybir as mybir
from concourse.bass_utils import run_bass_kernel_spmd

F32 = mybir.dt.float32
BF16 = mybir.dt.bfloat16
U8 = mybir.dt.uint8
AF = mybir.ActivationFunctionType
ALU = mybir.AluOpType
AX = mybir.AxisListType

NCORES = 8
D = 1024
SEQ_P = 4096
SEQ_S = 16384
NITEM = 6
NTOK_IN = 1536
NX = 1280
NOUT = 1024
EPS = 1e-6
DFF = 2816
NPIECE = 42
PIECE_E = 4096
NSLOT = 4

B_G = 0
B_BG = 64
B_CW = 128
B_PS = 140
B_SN = 144
B_IW = 148
B_SK = 150
B_FL = 158
B_PT = 182
B_NSK = 374
NCP = 382
CB_SGW = 0
CB_PW = 1024
CB_ID = 1536
CB_ON = 1664
CB_DG = 1792
NCB = 3328


def piece_sizes():
    s = [4096, 4096, 4096, 4096]
    s += [2560] * 16
    s += [4096, 4096]
    for _ in range(2):
        s += [4096] * 5 + [2048]
        s += [2816] * 4
    assert len(s) == NPIECE
    return s


class Tracker:
    def __init__(self):
        self.streams = {}
        self.sems = {}
        self.count = {}
        self.known = {}
        self.lastw = {}
        self.readers = {}
        self.semh = {}
        self.dcount = {}

    def add_engine(self, name, sem=None):
        self.streams[name] = []
        self.known[name] = {}
        if sem is not None:
            self.sems[name] = sem
            self.semh[name] = sem
            self.count[name] = 0

    def add_dma_sem(self, name, sem):
        self.semh[name] = sem
        self.dcount[name] = 0

    def _waits(self, eng, r, w):
        need = {}

        def req(ev, same_ok):
            sn, val, en = ev
            if en == eng and same_ok and eng == "pe":
                return
            if need.get(sn, 0) < val:
                need[sn] = val
        for k in r:
            ev = self.lastw.get(k)
            if ev is not None:
                req(ev, False)
        for k in w:
            ev = self.lastw.get(k)
            if ev is not None:
                req(ev, True)
            for ev in self.readers.get(k, {}).values():
                req(ev, True)
        kn = self.known[eng]
        out = []
        for sn, val in need.items():
            if kn.get(sn, 0) < val:
                kn[sn] = val
                out.append((sn, val))
        return out

    def _commit(self, ev, r, w):
        for k in w:
            self.lastw[k] = ev
            self.readers[k] = {}
        for k in r:
            self.readers.setdefault(k, {})[ev[0]] = ev

    def op(self, eng, name, kw, r=(), w=(), inc=True):
        w = list(w) + [k for k in r if k[0] == "ps"]
        waits = self._waits(eng, r, w)
        if inc:
            self.count[eng] += 1
            ev = (eng, self.count[eng], eng)
        else:
            ev = (eng, self.count[eng] + 1, eng)
        self.streams[eng].append((waits, name, kw, self.sems[eng] if inc else None, 1))
        self._commit(ev, r, w)

    def dma(self, q, dsem, kw, r=(), w=()):
        waits = self._waits(q, r, w)
        self.dcount[dsem] += 16
        ev = (dsem, self.dcount[dsem], "dma:" + dsem)
        self.streams[q].append((waits, "dma_start", kw, self.semh[dsem], 16))
        self._commit(ev, r, w)

    def barrier(self, engs):
        for e in engs:
            waits = []
            for o in engs:
                if o == e:
                    continue
                val = self.count[o]
                if self.known[e].get(o, 0) < val:
                    self.known[e][o] = val
                    waits.append((o, val))
            self.streams[e].append((waits, None, None, None, 0))

    def final_wait(self, q, semnames):
        vals = [(sn, self.dcount[sn]) for sn in semnames if self.dcount[sn] > 0]
        self.streams[q].append((vals, None, None, None, 0))

    def emit(self, eng, h):
        semh = self.semh
        for waits, name, kw, isem, amt in self.streams[eng]:
            for sn, val in waits:
                h.wait_ge(semh[sn], val)
            if name is not None:
                ins = getattr(h, name)(**kw)
                if isem is not None:
                    ins.then_inc(isem, amt)


def tiles_of(t0, t1, step=512):
    out = []
    a = t0
    while a < t1:
        b = min(a + step, t1)
        out.append((a, b))
        a = b
    return out


def build_program(debug=None):
    nc = bass.Bass("TRN2", target_bir_lowering=False)
    psz = piece_sizes()
    xin = nc.dram_tensor("xin", [NITEM, 128, 8, NTOK_IN], F32, kind="ExternalInput").ap()
    wst = nc.dram_tensor("wst", [2, NPIECE, 128, PIECE_E], F32, kind="ExternalInput").ap()
    cpd = nc.dram_tensor("cp", [128, NCP], F32, kind="ExternalInput").ap()
    cbd = nc.dram_tensor("cb", [128, NCB], F32, kind="ExternalInput").ap()
    biasd = nc.dram_tensor("abias", [128, 4 * 384], F32, kind="ExternalInput").ap()
    sgbd = nc.dram_tensor("sgb", [128, 2 * 2 * 128], F32, kind="ExternalInput").ap()
    yout = nc.dram_tensor("yout", [NITEM, 128, 8, NOUT], F32, kind="ExternalOutput").ap()
    dbg_t = {}
    if debug:
        for name, spec in debug.items():
            if name.startswith("_"):
                continue
            dbg_t[name] = nc.dram_tensor("dbg_" + name, list(spec[0]), spec[1], kind="ExternalOutput").ap()
    n_items = NITEM if not (debug and debug.get("_items")) else 1

    from contextlib import ExitStack
    with ExitStack() as es:
        def sb(name, shape, dt):
            return es.enter_context(nc.sbuf_tensor(name, shape, dt))

        Xt = sb("X", [128, 8, NX], F32)
        Ht = sb("H", [128, 8, NTOK_IN], BF16)
        RSZ = 69632
        Rt = sb("R", [128, RSZ], U8)
        WRt = sb("WR", [128, NSLOT, PIECE_E], BF16)
        CP = sb("CP", [128, NCP], F32)
        CB = sb("CB", [128, NCB], BF16)
        BIAS = sb("BIAS", [128, 4, 384], BF16)
        SGB = sb("SGB", [128, 2, 2, 128], F32)
        NSCR = 18
        SCRB = 1536
        SCRt = sb("SCR", [128, NSCR * SCRB], U8)
        SMt = sb("SM", [128, 32, 8], F32)
        PSt = es.enter_context(nc.psum_tensor("PS", [128, 8, 512], F32))

        def sem(name):
            return es.enter_context(nc.semaphore(name))

        T = Tracker()
        for en in ("pe", "act", "dve"):
            T.add_engine(en, sem("s_" + en))
        T.add_engine("sync")
        T.add_engine("gq")
        for s_ in range(NSLOT):
            T.add_dma_sem("w%d" % s_, sem("s_w%d" % s_))
        for n_ in ("xa0", "xa1", "xa2", "xh", "out0", "out1", "cst", "cstg", "dbg"):
            T.add_dma_sem(n_, sem("s_" + n_))

        def PE(name, r, w, inc=True, **kw):
            T.op("pe", name, kw, r, w, inc)

        def ACT(name, r, w, **kw):
            T.op("act", name, kw, r, w)

        def DVE(name, r, w, **kw):
            T.op("dve", name, kw, r, w)

        def rview(off, nbytes, dt, pat=None, **kw):
            v = Rt[:, off:off + nbytes].bitcast(dt)
            if pat:
                v = v.rearrange(pat, **kw)
            return v
        Zc = {}
        Zc[12] = rview(0, 3072, BF16)
        VT = rview(3072, 3072, BF16, "p (b d) -> p b d", b=12)
        VS = rview(6144, 5120, BF16, "p (b d) -> p b d", b=10)
        for zi_ in range(8, 12):
            Zc[zi_] = rview(11264 + (zi_ - 8) * 3072, 3072, BF16)
        for zi_ in range(8):
            Zc[zi_] = rview(23552 + zi_ * 3072, 3072, BF16)
        BR = rview(48128, 20480, BF16, "p (c t) -> p c t", c=8)
        Mv = rview(0, 20480, BF16, "p (c t) -> p c t", c=8)
        Yv = rview(20480, 40960, F32, "p (c t) -> p c t", c=8)
        HID = rview(0, 28160, BF16, "p (c t) -> p c t", c=11)
        Y2 = rview(28160, 40960, F32, "p (c t) -> p c t", c=8)

        def scr(i, dt, n=None, nslots=1):
            v = SCRt[:, i * SCRB:(i + nslots) * SCRB].bitcast(dt)
            if n is not None:
                v = v[:, 0:n]
            return v

        def scrk(i, nslots=1):
            return [("scr", i + j) for j in range(nslots)]

        def bk(name, chunks, t0, t1):
            return [(name, c, b) for c in chunks for b in range(t0 // 128, (t1 + 127) // 128)]

        def cpc(j):
            return CP[:, j:j + 1]

        KC = [("const",), ("constb",)]
        ident = CB[:, CB_ID:CB_ID + 128]
        onesm = CB[:, CB_ON:CB_ON + 128]

        bank_ctr = [0]

        held = set()

        def nbank():
            for _ in range(8):
                b = bank_ctr[0] % 8
                bank_ctr[0] += 1
                if b not in held:
                    return b
            raise RuntimeError("no free PSUM bank")

        def try_hold(n):
            if len(held) + n > 6:
                return None
            step = 2 if n == 2 else 1
            for b in range(0, 8, step):
                if all((b + i) not in held for i in range(n)):
                    for i in range(n):
                        held.add(b + i)
                    return b
            return None

        def release(b, n=1):
            for i in range(n):
                held.discard(b + i)

        def psk(b):
            return [("ps", b)]

        sm_ctr = [0]

        def nsm():
            i = sm_ctr[0] % 32
            sm_ctr[0] += 1
            return i

        alt = [0]

        def evac_copy(out_ap, in_ap, r, w):
            alt[0] += 1
            if alt[0] % 2 == 0:
                ACT("activation", r, w, out=out_ap, in_=in_ap, func=AF.Copy)
            else:
                DVE("tensor_copy", r, w, out=out_ap, in_=in_ap)

        T.dma("sync", "cst", dict(out=CP[:], in_=cpd[:, :]), w=[("const",)])
        T.dma("sync", "cst", dict(out=SGB[:].rearrange("p l c t -> p (l c t)"), in_=sgbd[:, :]), w=[("const",)])
        T.dma("gq", "cstg", dict(out=CB[:, 0:1792], in_=cbd[:, 0:1792]), w=[("constb",)])
        T.dma("gq", "cstg", dict(out=CB[:, 1792:NCB], in_=cbd[:, 1792:NCB]), w=[("constb",)])
        T.dma("gq", "cstg", dict(out=BIAS[:].rearrange("p h k -> p (h k)"), in_=biasd[:, :]), w=[("constb",)])

        DVE("tensor_scalar", KC, [("nsk",)], out=CP[:, B_NSK:B_NSK + 8], in0=CP[:, B_SK:B_SK + 8], scalar1=-1.0,
            scalar2=None, op0=ALU.mult)

        gp_state = {"next": 0}
        total_pieces = n_items * 2 * NPIECE

        def issue_piece(g):
            l = (g // NPIECE) % 2
            j = g % NPIECE
            slot = g % NSLOT
            n = psz[j]
            sp = {4096: 1024, 2048: 1024, 2816: 1408, 2560: 1280}[n]
            dst = WRt[:, slot, 0:n].rearrange("p (a b) -> p a b", b=sp)
            src = wst[l, j, :, 0:n].rearrange("p (a b) -> p a b", b=sp)
            T.dma("gq", "w%d" % slot, dict(out=dst, in_=src), w=[("w", slot)])

        def acquire_piece(g, hold_from=None):
            if hold_from is None:
                hold_from = g
            while gp_state["next"] < min(hold_from + NSLOT, total_pieces):
                issue_piece(gp_state["next"])
                gp_state["next"] += 1
            assert gp_state["next"] > g
            return g % NSLOT

        def stats_rstd(src_fn, n):
            b1 = nbank()
            for c in range(8):
                ap, keys = src_fn(c)
                si = 6 + (c % 2)
                sq = scr(si, BF16, n)
                ACT("activation", keys, scrk(si), out=sq, in_=ap, func=AF.Square)
                PE("matmul", scrk(si) + KC, psk(b1), out=PSt[:, b1, 0:n], lhsT=onesm, rhs=sq,
                   start=(c == 0), stop=(c == 7))
            sr = scr(8, F32, n, nslots=2)
            ACT("activation", psk(b1) + KC, scrk(8, 2), out=sr, in_=PSt[:, b1, 0:n], func=AF.Sqrt, bias=epsc, scale=1.0)
            b2 = nbank()
            DVE("reciprocal", scrk(8, 2), psk(b2), out=PSt[:, b2, 0:n], in_=sr)
            return b2

        def prenorm(l, w, a, b, src_fn):
            n = b - a
            b2 = stats_rstd(src_fn, n)
            for c in range(8):
                ap, keys = src_fn(c)
                DVE("scalar_tensor_tensor", keys + psk(b2) + KC, bk("H", [c], a, b),
                    out=Ht[:, c, a:b], in0=ap, scalar=cpc(B_G + (l * 4 + w) * 8 + c), in1=PSt[:, b2, 0:n],
                    op0=ALU.mult, op1=ALU.mult)

        def xsrc(a, b):
            def f(c):
                return Xt[:, c, a - 128:b - 128], bk("X", [c], a, b)
            return f

        def postnorm_residual(l, w, a, b, Ybuf, yname):
            n = b - a

            def ysrc(c):
                return Ybuf[:, c, a - 128:b - 128], bk(yname, [c], a, b)
            b2 = stats_rstd(ysrc, n)
            tms = {}
            for c in range(9):
                if c < 8:
                    tb = nbank()
                    while tb == b2 or tb in [v for k_, v in tms.items() if k_ >= c - 1]:
                        tb = nbank()
                    tms[c] = tb
                    DVE("scalar_tensor_tensor", bk(yname, [c], a, b) + psk(b2) + KC, psk(tb),
                        out=PSt[:, tb, 0:n], in0=Ybuf[:, c, a - 128:b - 128], scalar=cpc(B_G + (l * 4 + w) * 8 + c),
                        in1=PSt[:, b2, 0:n], op0=ALU.mult, op1=ALU.mult)
                if c >= 1:
                    cp_ = c - 1
                    xs = Xt[:, cp_, a - 128:b - 128]
                    DVE("tensor_tensor", psk(tms[cp_]) + bk("X", [cp_], a, b), bk("X", [cp_], a, b),
                        out=xs, in0=xs, in1=PSt[:, tms[cp_], 0:n], op=ALU.add)

        def fm_proj(slot, woff, nk, rhs_fn, a, b, rkeys):
            n = b - a
            bnk = nbank()
            for k in range(nk):
                PE("matmul", [("w", slot)] + rkeys, psk(bnk), inc=(k == nk - 1),
                   out=PSt[:, bnk, 0:n], lhsT=WRt[:, slot, woff + k * 128: woff + (k + 1) * 128],
                   rhs=rhs_fn(k), start=(k == 0), stop=(k == nk - 1))
            return bnk

        class _Stop(Exception):
            pass

        def chk(name):
            if debug and debug.get("_stop") == name:
                raise _Stop()

        def dump(name, ap, keys):
            if name in dbg_t:
                T.dma("sync", "dbg", dict(out=dbg_t[name], in_=ap), r=keys)

        def layer(it, l, KV0, KV1, C0, C1, gbase, after_p9=None):
            kv_tiles = tiles_of(KV0, KV1)
            c_tiles = tiles_of(C0, C1)
            kvb = list(range(KV0 // 128, KV1 // 128))
            cb_ = list(range(C0 // 128, C1 // 128))
            g = gbase

            WORDER = [3, 2, 0, 1]

            def gen_win(pjs):
                for pj in pjs:
                    slot = acquire_piece(g + WORDER.index(pj))
                    nchunks = 4 if pj < 3 else 1
                    for ci in range(nchunks):
                        zi = pj * 4 + ci
                        for (a, b) in kv_tiles:
                            bnk = fm_proj(slot, ci * 1024, 8, lambda k, a=a, b=b: Ht[:, k, a:b], a, b,
                                          bk("H", range(8), a, b))
                            evac_copy(Zc[zi][:, a:b], PSt[:, bnk, 0:b - a], psk(bnk), bk("Z", [zi], a, b))
                            yield
                    if pj == 3:
                        for blk in kvb:
                            t0 = blk * 128
                            inC = blk in cb_
                            ncol = 384 if inC else 128
                            bnk = nbank()
                            for k in range(8):
                                PE("matmul", [("w", slot)] + bk("H", range(8), t0, t0 + 128), psk(bnk), inc=(k == 7),
                                   out=PSt[:, bnk, 0:ncol], lhsT=Ht[:, k, t0:t0 + 128],
                                   rhs=WRt[:, slot, 1024 + k * 384: 1024 + k * 384 + ncol],
                                   start=(k == 0), stop=(k == 7))
                            ACT("activation", psk(bnk), [("VT", blk)], out=VT[:, blk, :], in_=PSt[:, bnk, 0:128], func=AF.Copy)
                            if inC:
                                s1 = nsm()
                                s2 = nsm()
                                DVE("bn_stats", psk(bnk), [("sm", s1)], out=SMt[:, s1, 0:6], in_=PSt[:, bnk, 128:384])
                                DVE("bn_aggr", [("sm", s1)], [("sm", s2)], out=SMt[:, s2, 0:2], in_=SMt[:, s1, 0:6])
                                ACT("activation", [("sm", s2)] + KC, [("sm", s2)], out=SMt[:, s2, 2:3], in_=SMt[:, s2, 1:2],
                                    func=AF.Sqrt, bias=epsc, scale=1.0)
                                DVE("reciprocal", [("sm", s2)], [("sm", s2)], out=SMt[:, s2, 3:4], in_=SMt[:, s2, 2:3])
                                DVE("tensor_scalar", psk(bnk) + [("sm", s2), ("sm", s2)], [("VS", blk)],
                                    out=VS[:, blk - 1, :], in0=PSt[:, bnk, 128:384], scalar1=SMt[:, s2, 0:1],
                                    scalar2=SMt[:, s2, 3:4], op0=ALU.subtract, op1=ALU.mult)
                            yield

            def gen_conv():
                for cc in range(2):
                    for (a, b) in c_tiles:
                        n = b - a
                        U = scr(12, BF16, n + 2)
                        DVE("tensor_tensor", bk("Z", [2 + cc, 4 + cc], a - 1, b + 1), scrk(12),
                            out=U, in0=Zc[2 + cc][:, a - 1:b + 1], in1=Zc[4 + cc][:, a - 1:b + 1], op=ALU.mult)
                        bnk = nbank()
                        for tap in range(3):
                            o = CB_DG + ((l * 3 + tap) * 2 + cc) * 128
                            PE("matmul", scrk(12) + KC, psk(bnk), inc=(tap == 2), out=PSt[:, bnk, 0:n],
                               lhsT=CB[:, o:o + 128], rhs=U[:, tap:tap + n], start=(tap == 0), stop=(tap == 2))
                        DVE("tensor_tensor", psk(bnk) + bk("Z", [cc], a, b), bk("BR", [cc], a, b),
                            out=BR[:, cc, a - 128:b - 128], in0=Zc[cc][:, a:b], in1=PSt[:, bnk, 0:n], op=ALU.mult)
                        yield

            def gen_pool():
                for cc in range(2):
                    zp = 6 + cc
                    for (a, b) in c_tiles:
                        n = b - a
                        T1 = scr(14, BF16)
                        T2 = scr(15, BF16)
                        T3 = scr(16, BF16)
                        T4 = scr(17, BF16)
                        PL = scr(13, BF16, n)
                        zk = bk("Z", [zp], a - 8, b + 8)
                        DVE("tensor_tensor", zk, scrk(14), out=T1[:, 0:n + 14], in0=Zc[zp][:, a - 8:b + 6],
                            in1=Zc[zp][:, a - 7:b + 7], op=ALU.add)
                        DVE("tensor_tensor", scrk(14), scrk(15), out=T2[:, 0:n + 12], in0=T1[:, 0:n + 12],
                            in1=T1[:, 2:n + 14], op=ALU.add)
                        if cc == 0:
                            sel = [(T1, 7, 14), (T2, 6, 15)]
                        else:
                            yield
                            DVE("tensor_tensor", scrk(15), scrk(16), out=T3[:, 0:n + 8], in0=T2[:, 0:n + 8],
                                in1=T2[:, 4:n + 12], op=ALU.add)
                            DVE("tensor_tensor", scrk(16), scrk(17), out=T4[:, 0:n], in0=T3[:, 0:n],
                                in1=T3[:, 8:n + 8], op=ALU.add)
                            sel = [(T3, 4, 16), (T4, 0, 17)]
                        yield
                        for hf in range(2):
                            Ts, off, si = sel[hf]
                            r0 = hf * 64
                            DVE("scalar_tensor_tensor", scrk(si) + bk("Z", [zp], a, b) + KC, scrk(13),
                                out=PL[r0:r0 + 64, :], in0=Ts[r0:r0 + 64, off:off + n],
                                scalar=CP[r0:r0 + 64, B_IW + cc:B_IW + cc + 1],
                                in1=Zc[zp][r0:r0 + 64, a:b], op0=ALU.mult, op1=ALU.subtract)
                            for side, e0 in ((0, 256), (1, 1272)):
                                if a <= e0 and e0 + 8 <= b:
                                    j0 = e0 - a
                                    tb = B_PT + ((it * 2 + side) * 2 + cc) * 8
                                    sm = nsm()
                                    DVE("tensor_tensor", scrk(si) + KC, [("sm", sm)],
                                        out=SMt[r0:r0 + 64, sm, :], in0=Ts[r0:r0 + 64, off + j0:off + j0 + 8],
                                        in1=CP[r0:r0 + 64, tb:tb + 8], op=ALU.mult)
                                    DVE("tensor_tensor", [("sm", sm)] + bk("Z", [zp], e0, e0 + 8), scrk(13),
                                        out=PL[r0:r0 + 64, j0:j0 + 8], in0=SMt[r0:r0 + 64, sm, :],
                                        in1=Zc[zp][r0:r0 + 64, e0:e0 + 8], op=ALU.subtract)
                        bnk = nbank()
                        PE("matmul", scrk(13) + KC, psk(bnk), out=PSt[:, bnk, 0:n],
                           lhsT=CB[:, CB_PW + (l * 2 + cc) * 128: CB_PW + (l * 2 + cc + 1) * 128],
                           rhs=PL, start=True, stop=True)
                        ACT("activation", psk(bnk) + KC, bk("BR", [2 + cc], a, b),
                            out=BR[:, 2 + cc, a - 128:b - 128], in_=PSt[:, bnk, 0:n], func=AF.Copy,
                            scale=cpc(B_PS + l * 2 + cc))
                        yield

            def gen_sg():
                for blk in cb_:
                    t0 = blk * 128
                    for cc in range(2):
                        bnk = nbank()
                        PE("matmul", [("VS", blk)] + KC, psk(bnk), out=PSt[:, bnk, 0:256],
                           lhsT=VS[:, blk - 1, cc * 128:(cc + 1) * 128],
                           rhs=CB[:, CB_SGW + (l * 2 + cc) * 256: CB_SGW + (l * 2 + cc + 1) * 256],
                           start=True, stop=True)
                        for hf in range(2):
                            r0 = hf * 64
                            tm = PSt[r0:r0 + 64, bnk, 256 + hf * 128:256 + (hf + 1) * 128]
                            DVE("scalar_tensor_tensor", psk(bnk) + KC, psk(bnk),
                                out=tm, in0=PSt[r0:r0 + 64, bnk, hf * 128:(hf + 1) * 128],
                                scalar=CP[r0:r0 + 64, B_SN + l * 2 + cc:B_SN + l * 2 + cc + 1],
                                in1=SGB[r0:r0 + 64, l, cc, :], op0=ALU.mult, op1=ALU.add)
                            DVE("tensor_tensor", psk(bnk) + bk("Z", [8 + cc], t0, t0 + 128), bk("BR", [4 + cc], t0, t0 + 128),
                                out=BR[r0:r0 + 64, 4 + cc, t0 - 128:t0], in0=Zc[8 + cc][r0:r0 + 64, t0:t0 + 128],
                                in1=tm, op=ALU.mult)
                        yield

            sinkv = CP[:, B_SK + l * 4:B_SK + l * 4 + 4]
            nsinkv = CP[:, B_NSK + l * 4:B_NSK + l * 4 + 4]

            def attn_unit(blk, hp, u):
                t0 = blk * 128
                k0 = t0 - 128
                sb0 = (u % 6) * 2
                Pb = SCRt[:, sb0 * SCRB:sb0 * SCRB + 1536].bitcast(BF16).rearrange("p (h k) -> p h k", h=2)
                Pk = scrk(sb0)
                PTs = SCRt[:, (sb0 + 1) * SCRB:(sb0 + 1) * SCRB + 1536].bitcast(BF16).rearrange("p (h k) -> p h k", h=2)
                PTk = scrk(sb0 + 1)
                hds = (2 * hp, 2 * hp + 1)
                r0 = hp * 64
                b0 = try_hold(2)
                while b0 is None:
                    yield
                    b0 = try_hold(2)
                Sk = psk(b0) + psk(b0 + 1)
                for i2, hd in enumerate(hds):
                    qc = 10 + (hd % 2)
                    PE("matmul", bk("Z", [qc], t0, t0 + 128) + bk("Z", [12], k0, k0 + 384), psk(b0 + i2), inc=False,
                       out=PSt[:, b0 + i2, 0:384], lhsT=Zc[qc][r0:r0 + 64, t0:t0 + 128],
                       rhs=Zc[12][r0:r0 + 64, k0:k0 + 384], start=True, stop=False)
                    PE("matmul", KC, psk(b0 + i2), out=PSt[:, b0 + i2, 0:384], lhsT=ident, rhs=BIAS[:, hd, :],
                       start=False, stop=True)
                S2 = PSt[:, b0:b0 + 2, 0:384]
                if blk == 2:
                    DVE("tensor_scalar", Sk + KC, Sk, out=PSt[:, b0:b0 + 2, 0:128], in0=PSt[:, b0:b0 + 2, 0:128],
                        scalar1=cpc(B_FL + it * 4 + 0), scalar2=None, op0=ALU.add)
                if blk == 9:
                    DVE("tensor_scalar", Sk + KC, Sk, out=PSt[:, b0:b0 + 2, 256:384], in0=PSt[:, b0:b0 + 2, 256:384],
                        scalar1=cpc(B_FL + it * 4 + 1), scalar2=None, op0=ALU.add)
                yield
                s1 = nsm()
                s2 = nsm()
                s3 = nsm()
                s4 = nsm()
                SMr = SMt[:, s1, :]
                SMq = SMt[:, s2, :]
                SMs = SMt[:, s3, :]
                SMd = SMt[:, s4, :]
                sk2 = sinkv[:, 2 * hp:2 * hp + 2]
                DVE("tensor_reduce", Sk, [("sm", s1)], out=SMr[:, 0:2], in_=S2, axis=AX.X, op=ALU.max, negate=True)
                DVE("scalar_tensor_tensor", [("sm", s1), ("nsk",)] + KC, [("sm", s2)], out=SMq[:, 0:2], in0=SMr[:, 0:2],
                    scalar=0.125, in1=nsinkv[:, 2 * hp:2 * hp + 2], op0=ALU.mult, op1=ALU.min)
                yield
                for i2 in range(2):
                    ACT("activation", psk(b0 + i2) + [("sm", s2)], Pk + [("sm", s3)],
                        out=Pb[:, i2, :], in_=PSt[:, b0 + i2, 0:384], func=AF.Exp, bias=SMq[:, i2:i2 + 1], scale=0.125,
                        accum_out=SMs[:, i2:i2 + 1])
                    ACT("activation", [("sm", s2)] + KC, [("sm", s3)], out=SMs[:, 4 + i2:5 + i2],
                        in_=sinkv[:, 2 * hp + i2:2 * hp + i2 + 1], func=AF.Exp, bias=SMq[:, i2:i2 + 1], scale=1.0)
                release(b0, 2)
                yield
                DVE("tensor_tensor", [("sm", s3), ("sm", s3), ("sm", s3), ("sm", s3)], [("sm", s4)],
                    out=SMd[:, 0:2], in0=SMs[:, 0:2], in1=SMs[:, 4:6], op=ALU.add)
                DVE("reciprocal", [("sm", s4)], [("sm", s4)], out=SMd[:, 4:6], in_=SMd[:, 0:2])
                yield
                for i2 in range(2):
                    ACT("activation", Pk + [("sm", s4)], Pk, out=Pb[:, i2, :], in_=Pb[:, i2, :], func=AF.Copy,
                        scale=SMd[:, 4 + i2:5 + i2])
                yield
                tbk = try_hold(1)
                while tbk is None:
                    yield
                    tbk = try_hold(1)
                ptv = PSt[:, tbk, :].bitcast(BF16)
                for i2 in range(2):
                    for j in range(3):
                        c0 = i2 * 384 + j * 128
                        PE("transpose", Pk + KC, psk(tbk), out=ptv[:, c0:c0 + 128],
                           in_=Pb[:, i2, j * 128:(j + 1) * 128], identity=ident)
                yield
                evac_copy(PTs.rearrange("p h k -> p (h k)"), ptv[:, 0:768], psk(tbk), PTk)
                release(tbk)
                yield
                ob = try_hold(1)
                while ob is None:
                    yield
                    ob = try_hold(1)
                for i2 in range(2):
                    for j in range(3):
                        PE("matmul", PTk + [("VT", blk - 1 + j)], psk(ob), inc=(j == 2),
                           out=PSt[i2 * 64:i2 * 64 + 64, ob, 0:128],
                           lhsT=VT[:, blk - 1 + j, hp * 64:(hp + 1) * 64], rhs=PTs[:, i2, j * 128:(j + 1) * 128],
                           start=(j == 0), stop=(j == 2))
                yield
                ACT("activation", psk(ob), bk("BR", [6 + hp], t0, t0 + 128),
                    out=BR[:, 6 + hp, t0 - 128:t0], in_=PSt[:, ob, 0:128], func=AF.Copy)
                release(ob)

            for _ in gen_win([3, 2]):
                pass
            import itertools
            others = itertools.chain(gen_win([0, 1]), gen_sg(), gen_conv(), gen_pool())
            others_done = False
            units = [(blk, hp) for blk in cb_ for hp in range(2)]
            active = []
            nxt = 0
            while nxt < len(units) or active or not others_done:
                if nxt < len(units) and len(active) < 6:
                    active.append(attn_unit(units[nxt][0], units[nxt][1], nxt))
                    nxt += 1
                for gen in list(active):
                    try:
                        next(gen)
                    except StopIteration:
                        active.remove(gen)
                for _ in range(4):
                    if not others_done:
                        try:
                            next(others)
                        except StopIteration:
                            others_done = True
            g += 4

            if it == 0 and ("BR%d" % l) in dbg_t:
                dump("BR%d" % l, BR, bk("BR", range(8), 128, 1408))

            chk("p3d")
            T.barrier(["pe", "act", "dve"])
            GB = [0, 1, 2]
            PB = [3, 4, 5]
            MB = [6, 7]
            gctr = [0]
            mctr = [0]
            pend = []

            def flush_pend(keep):
                while len(pend) > keep:
                    pend.pop(0)()
            for c in range(8):
                slots2 = [acquire_piece(g, hold_from=g), acquire_piece(g + 1, hold_from=g)]
                g += 2
                for (a, b) in c_tiles:
                    n = b - a
                    mb = MB[mctr[0] % 2]
                    mctr[0] += 1
                    for i in range(4):
                        gslot = slots2[i // 2]
                        i2 = i % 2
                        gb = GB[gctr[0] % 3]
                        pb = PB[gctr[0] % 3]
                        gi = gctr[0] % 2
                        gctr[0] += 1
                        for k in range(8):
                            wo = (i2 * 10 + k) * 128
                            PE("matmul", [("w", gslot)] + bk("H", range(8), a, b), psk(gb), inc=(k == 7),
                               out=PSt[:, gb, 0:n], lhsT=WRt[:, gslot, wo:wo + 128],
                               rhs=Ht[:, k, a:b], start=(k == 0), stop=(k == 7))
                        for kk in range(2):
                            wo = (i2 * 10 + 8 + kk) * 128
                            PE("matmul", [("w", gslot)] + bk("BR", [2 * i, 2 * i + 1], a, b), psk(pb), inc=(kk == 1),
                               out=PSt[:, pb, 0:n], lhsT=WRt[:, gslot, wo:wo + 128],
                               rhs=BR[:, 2 * i + kk, a - 128:b - 128], start=(kk == 0), stop=(kk == 1))
                        gsb = scr(2 * gi, F32, n, nslots=2)
                        tsb = scr(4 + gi, BF16, n)
                        ACT("activation", psk(gb) + KC, scrk(2 * gi, 2), out=gsb, in_=PSt[:, gb, 0:n], func=AF.Sigmoid,
                            bias=cpc(B_BG + (l * 4 + i) * 8 + c), scale=1.0)
                        DVE("tensor_tensor", scrk(2 * gi, 2) + psk(pb), scrk(4 + gi), out=tsb, in0=gsb, in1=PSt[:, pb, 0:n],
                            op=ALU.mult)

                        def acc(i=i, mb=mb, tsb=tsb, n=n, gi=gi, a=a, b=b, c=c):
                            PE("matmul", scrk(4 + gi) + KC, psk(mb), out=PSt[:, mb, 0:n], lhsT=ident, rhs=tsb,
                               start=(i == 0), stop=(i == 3))
                            if i == 3:
                                ACT("activation", psk(mb), bk("M", [c], a, b), out=Mv[:, c, a - 128:b - 128],
                                    in_=PSt[:, mb, 0:n], func=AF.Copy)
                        pend.append(acc)
                        flush_pend(1)
            flush_pend(0)

            if l == 1 and it + 1 < n_items:
                load_halo(it + 1)
            chk("p4")
            wslots = [acquire_piece(g, hold_from=g), acquire_piece(g + 1, hold_from=g)]
            g += 2
            for (a, b) in c_tiles:
                for op_ in range(2):
                    for c4 in range(4):
                        c = op_ * 4 + c4
                        bnk = fm_proj(wslots[op_], c4 * 1024, 8, lambda k, a=a, b=b: Mv[:, k, a - 128:b - 128], a, b,
                                      bk("M", range(8), a, b))
                        evac_copy(Yv[:, c, a - 128:b - 128], PSt[:, bnk, 0:b - a], psk(bnk), bk("Y", [c], a, b))
                postnorm_residual(l, 1, a, b, Yv, "Y")
                prenorm(l, 2, a, b, xsrc(a, b))
            chk("p6")
            for half in range(2):
                for pp in range(6):
                    slot = acquire_piece(g)
                    g += 1
                    for jj in range(2 if pp < 5 else 1):
                        j = pp * 2 + jj
                        for (a, b) in c_tiles:
                            n = b - a
                            ba = fm_proj(slot, (jj * 2 + 0) * 1024, 8, lambda k, a=a, b=b: Ht[:, k, a:b], a, b,
                                         bk("H", range(8), a, b))
                            bg = fm_proj(slot, (jj * 2 + 1) * 1024, 8, lambda k, a=a, b=b: Ht[:, k, a:b], a, b,
                                         bk("H", range(8), a, b))
                            si = j % 2
                            ssb = scr(2 * si, F32, n, nslots=2)
                            ACT("activation", psk(ba), scrk(2 * si, 2), out=ssb, in_=PSt[:, ba, 0:n], func=AF.Silu)
                            DVE("tensor_tensor", scrk(2 * si, 2) + psk(bg), bk("HID", [j], a, b),
                                out=HID[:, j, a - 128:b - 128], in0=ssb, in1=PSt[:, bg, 0:n], op=ALU.mult)
                for pp in range(4):
                    slot = acquire_piece(g)
                    g += 1
                    for c2 in range(2):
                        c = pp * 2 + c2
                        for (a, b) in c_tiles:
                            n = b - a
                            bnk = fm_proj(slot, c2 * 1408, 11, lambda k, a=a, b=b: HID[:, k, a - 128:b - 128], a, b,
                                          bk("HID", range(11), a, b))
                            y2s = Y2[:, c, a - 128:b - 128]
                            if half == 0:
                                evac_copy(y2s, PSt[:, bnk, 0:n], psk(bnk), bk("Y2", [c], a, b))
                            else:
                                DVE("tensor_tensor", psk(bnk) + bk("Y2", [c], a, b), bk("Y2", [c], a, b),
                                    out=y2s, in0=y2s, in1=PSt[:, bnk, 0:n], op=ALU.add)
            chk("p8")
            for ti, (a, b) in enumerate(c_tiles):
                postnorm_residual(l, 3, a, b, Y2, "Y2")
                if after_p9 is not None:
                    after_p9(ti, a, b)
            assert g == gbase + NPIECE
            if it == 0:
                dump("X%d" % l, Xt[:], bk("X", range(8), 128, 1408))

        epsc = EPS
        XH = SCRt[:, 12 * SCRB:12 * SCRB + 8192].bitcast(F32).rearrange("p (c t) -> p c t", c=8)
        def load_halo(it_):
            T.dma("sync", "xh", dict(out=XH[:, :, 0:128], in_=xin[it_, :, :, 0:128]), w=scrk(12, 6))
            T.dma("sync", "xh", dict(out=XH[:, :, 128:256], in_=xin[it_, :, :, 1408:1536]), w=scrk(12, 6))

        def main_body():
            for it in range(n_items):
                for ti, (a, b) in enumerate(tiles_of(128, 1408)):
                    T.dma("sync", "xa%d" % ti, dict(out=Xt[:, :, a - 128:b - 128], in_=xin[it, :, :, a:b]),
                          w=bk("X", range(8), a, b))
                if it == 0:
                    load_halo(0)
                gbase = (it * 2) * NPIECE

                def hsrc0(c):
                    return XH[:, c, 0:128], scrk(12, 6)

                def hsrc1(c):
                    return XH[:, c, 128:256], scrk(12, 6)
                chk("p0")
                tl = tiles_of(128, 1408)
                prenorm(0, 0, 0, 128, hsrc0)
                prenorm(0, 0, tl[0][0], tl[0][1], xsrc(*tl[0]))
                prenorm(0, 0, tl[1][0], tl[1][1], xsrc(*tl[1]))
                prenorm(0, 0, tl[2][0], tl[2][1], xsrc(*tl[2]))
                prenorm(0, 0, 1408, 1536, hsrc1)
                chk("p1")
                layer(it, 0, 0, 1536, 128, 1408, gbase)
                for (a, b) in tiles_of(128, 1408):
                    prenorm(1, 0, a, b, xsrc(a, b))
                hk = bk("H", range(8), 128, 256)
                DVE("tensor_scalar", hk + KC, hk, out=Ht[:, :, 128:256], in0=Ht[:, :, 128:256],
                    scalar1=cpc(B_FL + it * 4 + 2), scalar2=None, op0=ALU.mult)
                hk = bk("H", range(8), 1280, 1408)
                DVE("tensor_scalar", hk + KC, hk, out=Ht[:, :, 1280:1408], in0=Ht[:, :, 1280:1408],
                    scalar1=cpc(B_FL + it * 4 + 3), scalar2=None, op0=ALU.mult)
                def store_tile(ti, a, b, it=it):
                    T.dma("sync", "out%d" % ti, dict(out=yout[it, :, :, a - 256:b - 256], in_=Xt[:, :, a - 128:b - 128]),
                          r=bk("X", range(8), a, b))
                layer(it, 1, 128, 1408, 256, 1280, gbase + NPIECE, after_p9=store_tile)
        try:
            main_body()
        except _Stop:
            pass
        T.barrier(["pe", "act", "dve"])
        T.final_wait("sync", list(T.dcount.keys()))

        with nc.Block() as block:
            @block.sync
            def _(h):
                T.emit("sync", h)

            @block.gpsimd
            def _(h):
                T.emit("gq", h)

            @block.tensor
            def _(h):
                T.emit("pe", h)

            @block.scalar
            def _(h):
                T.emit("act", h)

            @block.vector
            def _(h):
                T.emit("dve", h)
    return nc


def pack_weights(w_in, w_branch, w_gate, w_out, w_ffn_in, w_ffn_out):
    wst = np.zeros((2, NPIECE, 128, PIECE_E), np.float32)

    def fm(Wcols):
        K = Wcols.shape[0] // 128
        return Wcols.reshape(K, 128, 128).transpose(1, 0, 2)
    qperm = np.concatenate([np.arange(1536, 1600), np.arange(1664, 1728), np.arange(1600, 1664), np.arange(1728, 1792)])
    fmcols = [np.arange(0, 128), np.arange(128, 256), np.arange(256, 384), np.arange(384, 512),
              np.arange(512, 640), np.arange(640, 768), np.arange(768, 896), np.arange(896, 1024),
              np.arange(1024, 1152), np.arange(1152, 1280), qperm[:128], qperm[128:], np.arange(1792, 1920)]
    tmcols = np.concatenate([np.arange(1920, 2048), np.arange(1280, 1536)])
    for l in range(2):
        wp = []
        for pj in range(3):
            parts = [fm(w_in[l][:, fmcols[pj * 4 + ci]]) for ci in range(4)]
            wp.append(np.stack(parts, axis=1).reshape(128, 4096))
        kpart = fm(w_in[l][:, fmcols[12]]).reshape(128, 1024)
        tm = w_in[l][:, tmcols].reshape(8, 128, 384).transpose(1, 0, 2).reshape(128, 3072)
        wp.append(np.concatenate([kpart, tm], axis=1))
        for j, pj in enumerate([3, 2, 0, 1]):
            wst[l, j] = wp[pj]
        j = 4
        for c in range(8):
            for ih in range(2):
                arr = np.zeros((128, 2, 10, 128), np.float32)
                for i2 in range(2):
                    i = ih * 2 + i2
                    arr[:, i2, 0:8] = fm(w_gate[l, i][:, c * 128:(c + 1) * 128])
                    arr[:, i2, 8:10] = fm(w_branch[l, i][:, c * 128:(c + 1) * 128])
                wst[l, j, :, :2560] = arr.reshape(128, 2560)
                j += 1
        for op_ in range(2):
            arr = np.stack([fm(w_out[l][:, (op_ * 4 + c4) * 128:(op_ * 4 + c4 + 1) * 128]) for c4 in range(4)], axis=1)
            wst[l, j] = arr.reshape(128, 4096)
            j += 1
        for half in range(2):
            for pp in range(6):
                njj = 2 if pp < 5 else 1
                arr = np.zeros((128, njj, 2, 8, 128), np.float32)
                for jj in range(njj):
                    hj = half * 11 + pp * 2 + jj
                    for ag in range(2):
                        arr[:, jj, ag] = fm(w_ffn_in[l][:, ag * DFF + hj * 128: ag * DFF + (hj + 1) * 128])
                wst[l, j, :, :njj * 2048] = arr.reshape(128, njj * 2048)
                j += 1
            for pp in range(4):
                arr = np.zeros((128, 2, 11, 128), np.float32)
                for c2 in range(2):
                    c = pp * 2 + c2
                    arr[:, c2] = fm(w_ffn_out[l][half * 11 * 128:(half + 1) * 11 * 128, c * 128:(c + 1) * 128])
                wst[l, j, :, :2816] = arr.reshape(128, 2816)
                j += 1
        assert j == NPIECE
    return wst


def item_list(core):
    items = []
    for a in (0, 8, 16, 24):
        items.append(("p", core, a, 32))
    for a in (16 * core, 16 * core + 8):
        items.append(("s", 0, a, 128))
    return items


def build_inputs(inp):
    f32 = np.float32
    x_prompt = np.asarray(inp["x_prompt"], f32)
    x_sample = np.asarray(inp["x_sample"], f32)
    wst = pack_weights(np.asarray(inp["w_in"], f32), np.asarray(inp["w_branch"], f32), np.asarray(inp["w_gate"], f32),
                       np.asarray(inp["w_out"], f32), np.asarray(inp["w_ffn_in"], f32), np.asarray(inp["w_ffn_out"], f32))
    cp0 = np.zeros((128, NCP), f32)
    gains = [inp["norm_mix_pre"], inp["norm_mix_post"], inp["norm_ffn_pre"], inp["norm_ffn_post"]]
    for l in range(2):
        for w in range(4):
            cp0[:, B_G + (l * 4 + w) * 8: B_G + (l * 4 + w) * 8 + 8] = np.asarray(gains[w], f32)[l].reshape(8, 128).T
        for i in range(4):
            cp0[:, B_BG + (l * 4 + i) * 8: B_BG + (l * 4 + i) * 8 + 8] = np.asarray(inp["b_gate"], f32)[l, i].reshape(8, 128).T
        for tap in range(3):
            cp0[:, B_CW + (l * 3 + tap) * 2: B_CW + (l * 3 + tap) * 2 + 2] = np.asarray(inp["conv_w"], f32)[l, tap].reshape(2, 128).T
        cp0[:, B_PS + l * 2: B_PS + l * 2 + 2] = np.asarray(inp["pool_scale"], f32)[l].reshape(2, 128).T
        cp0[:, B_SN + l * 2: B_SN + l * 2 + 2] = np.asarray(inp["sg_norm"], f32)[l].reshape(2, 128).T
        cp0[:, B_SK + l * 4: B_SK + l * 4 + 4] = np.asarray(inp["attn_sink"], f32)[l][None, :]
    wins = np.zeros((128, 2), np.int64)
    wins[:64, 0] = 2
    wins[64:, 0] = 4
    wins[:64, 1] = 8
    wins[64:, 1] = 16
    cp0[:, B_IW:B_IW + 2] = (1.0 / wins).astype(f32)
    cb = np.zeros((128, NCB), f32)
    sg_w = np.asarray(inp["sg_w"], f32)
    pool_w = np.asarray(inp["pool_w"], f32)
    for l in range(2):
        for cc in range(2):
            for jg in range(2):
                o = CB_SGW + (l * 2 + cc) * 256 + jg * 128
                cb[:, o:o + 128] = sg_w[l, 2 * cc + jg].T
            blk = np.zeros((128, 128), f32)
            blk[:64, :64] = pool_w[l, 2 * cc]
            blk[64:, 64:] = pool_w[l, 2 * cc + 1]
            cb[:, CB_PW + (l * 2 + cc) * 128: CB_PW + (l * 2 + cc + 1) * 128] = blk
    conv_w = np.asarray(inp["conv_w"], f32)
    for l in range(2):
        for tap in range(3):
            for cc in range(2):
                o = CB_DG + ((l * 3 + tap) * 2 + cc) * 128
                dg = np.zeros((128, 128), f32)
                dg[np.arange(128), np.arange(128)] = conv_w[l, tap, cc * 128:(cc + 1) * 128]
                cb[:, o:o + 128] = dg
    cb[:, CB_ID:CB_ID + 128] = np.eye(128, dtype=f32)
    cb[:, CB_ON:CB_ON + 128] = 1.0 / 1024.0
    qi = np.arange(128)[:, None]
    kj = np.arange(384)[None, :]
    rel = np.abs(qi - kj + 128).astype(f32)
    slopes = np.exp2(-8.0 * (np.arange(4, dtype=f32) + 1.0) / 4.0).astype(f32)
    abias = np.where(rel[:, None, :] <= 128, -8.0 * slopes[None, :, None] * rel[:, None, :], f32(-8e30)).astype(f32)
    abias = np.ascontiguousarray(abias.reshape(128, 4 * 384))
    sg_b = np.asarray(inp["sg_b"], f32)
    sgb = np.zeros((128, 2, 2, 128), f32)
    for l in range(2):
        for cc in range(2):
            sgb[:64, l, cc, :] = sg_b[l, 2 * cc][None, :]
            sgb[64:, l, cc, :] = sg_b[l, 2 * cc + 1][None, :]
    sgb = sgb.reshape(128, 512)

    in_maps = []
    for core in range(NCORES):
        items = item_list(core)
        xin = np.zeros((NITEM, 128, 8, NTOK_IN), f32)
        cp = cp0.copy()
        for it, (kind, sidx, a, nblk) in enumerate(items):
            src = x_prompt[sidx] if kind == "p" else x_sample[0]
            S = src.shape[0]
            t0 = (a - 2) * 128
            lo = max(t0, 0)
            hi = min(t0 + NTOK_IN, S)
            seg = src[lo:hi]
            xin[it, :, :, lo - t0:hi - t0] = seg.reshape(-1, 8, 128).transpose(2, 1, 0)
            lvalid = a > 0
            rvalid = (a + 8) < nblk
            cp[:, B_FL + it * 4 + 0] = 0.0 if lvalid else -1e30
            cp[:, B_FL + it * 4 + 1] = 0.0 if rvalid else -1e30
            cp[:, B_FL + it * 4 + 2] = 1.0 if lvalid else 0.0
            cp[:, B_FL + it * 4 + 3] = 1.0 if rvalid else 0.0
            for cc in range(2):
                w = wins[:, cc]
                for jx in range(8):
                    cl = w if lvalid else (jx + w // 2 - np.maximum(jx - w // 2, 0))
                    cr = w if rvalid else (np.minimum(w // 2, 8 - jx) + w // 2)
                    cp[:, B_PT + ((it * 2 + 0) * 2 + cc) * 8 + jx] = (1.0 / cl).astype(f32)
                    cp[:, B_PT + ((it * 2 + 1) * 2 + cc) * 8 + jx] = (1.0 / cr).astype(f32)
        in_maps.append({"xin": xin, "wst": wst, "cp": cp, "cb": cb, "abias": abias, "sgb": sgb})
    return in_maps


def assemble(results):
    yp = np.zeros((8, SEQ_P, D), np.float32)
    ys = np.zeros((1, SEQ_S, D), np.float32)
    for core in range(NCORES):
        y = results[core]["yout"]
        for it, (kind, sidx, a, nblk) in enumerate(item_list(core)):
            blk = y[it].transpose(2, 1, 0).reshape(NOUT, D)
            if kind == "p":
                yp[sidx, a * 128:a * 128 + NOUT] = blk
            else:
                ys[0, a * 128:a * 128 + NOUT] = blk
    return yp, ys


_DEBUG = None
_LAST = {}


def kernel(**inputs):
    in_maps = build_inputs(inputs)
    nc = build_program(_DEBUG)
    res = run_bass_kernel_spmd(nc, in_maps, core_ids=list(range(NCORES)))
    _LAST["res"] = res
    yp, ys = assemble(res.results)
    return (yp, ys)
```

```python
import numpy as np
import concourse.bass as bass
import concourse.mybir as mybir
from concourse.bass_utils import run_bass_kernel_spmd

F32 = mybir.dt.float32
BF16 = mybir.dt.bfloat16
U8 = mybir.dt.uint8
AF = mybir.ActivationFunctionType
ALU = mybir.AluOpType
AX = mybir.AxisListType

NCORES = 8
D = 1024
SEQ_P = 4096
SEQ_S = 16384
NITEM = 6
NTOK_IN = 1536
NX = 1280
NOUT = 1024
EPS = 1e-6
DFF = 2816
NPIECE = 42
PIECE_E = 4096
NSLOT = 4

B_G = 0
B_BG = 64
B_CW = 128
B_PS = 140
B_SN = 144
B_IW = 148
B_SK = 150
B_FL = 158
B_PT = 182
B_NSK = 374
NCP = 382
CB_SGW = 0
CB_PW = 1024
CB_ID = 1536
CB_ON = 1664
CB_DG = 1792
NCB = 3328


def piece_sizes():
    s = [4096, 4096, 4096, 4096]
    s += [2560] * 16
    s += [4096, 4096]
    for _ in range(2):
        s += [4096] * 5 + [2048]
        s += [2816] * 4
    assert len(s) == NPIECE
    return s


class Tracker:
    def __init__(self):
        self.streams = {}
        self.sems = {}
        self.count = {}
        self.known = {}
        self.lastw = {}
        self.readers = {}
        self.semh = {}
        self.dcount = {}

    def add_engine(self, name, sem=None):
        self.streams[name] = []
        self.known[name] = {}
        if sem is not None:
            self.sems[name] = sem
            self.semh[name] = sem
            self.count[name] = 0

    def add_dma_sem(self, name, sem):
        self.semh[name] = sem
        self.dcount[name] = 0

    def _waits(self, eng, r, w):
        need = {}

        def req(ev, same_ok):
            sn, val, en = ev
            if en == eng and same_ok and eng == "pe":
                return
            if need.get(sn, 0) < val:
                need[sn] = val
        for k in r:
            ev = self.lastw.get(k)
            if ev is not None:
                req(ev, False)
        for k in w:
            ev = self.lastw.get(k)
            if ev is not None:
                req(ev, True)
            for ev in self.readers.get(k, {}).values():
                req(ev, True)
        kn = self.known[eng]
        out = []
        for sn, val in need.items():
            if kn.get(sn, 0) < val:
                kn[sn] = val
                out.append((sn, val))
        return out

    def _commit(self, ev, r, w):
        for k in w:
            self.lastw[k] = ev
            self.readers[k] = {}
        for k in r:
            self.readers.setdefault(k, {})[ev[0]] = ev

    def op(self, eng, name, kw, r=(), w=(), inc=True):
        w = list(w) + [k for k in r if k[0] == "ps"]
        waits = self._waits(eng, r, w)
        if inc:
            self.count[eng] += 1
            ev = (eng, self.count[eng], eng)
        else:
            ev = (eng, self.count[eng] + 1, eng)
        self.streams[eng].append((waits, name, kw, self.sems[eng] if inc else None, 1))
        self._commit(ev, r, w)

    def dma(self, q, dsem, kw, r=(), w=()):
        waits = self._waits(q, r, w)
        self.dcount[dsem] += 16
        ev = (dsem, self.dcount[dsem], "dma:" + dsem)
        self.streams[q].append((waits, "dma_start", kw, self.semh[dsem], 16))
        self._commit(ev, r, w)

    def barrier(self, engs):
        for e in engs:
            waits = []
            for o in engs:
                if o == e:
                    continue
                val = self.count[o]
                if self.known[e].get(o, 0) < val:
                    self.known[e][o] = val
                    waits.append((o, val))
            self.streams[e].append((waits, None, None, None, 0))

    def final_wait(self, q, semnames):
        vals = [(sn, self.dcount[sn]) for sn in semnames if self.dcount[sn] > 0]
        self.streams[q].append((vals, None, None, None, 0))

    def emit(self, eng, h):
        semh = self.semh
        for waits, name, kw, isem, amt in self.streams[eng]:
            for sn, val in waits:
                h.wait_ge(semh[sn], val)
            if name is not None:
                ins = getattr(h, name)(**kw)
                if isem is not None:
                    ins.then_inc(isem, amt)


def tiles_of(t0, t1, step=512):
    out = []
    a = t0
    while a < t1:
        b = min(a + step, t1)
        out.append((a, b))
        a = b
    return out


def build_program(debug=None):
    nc = bass.Bass("TRN2", target_bir_lowering=False)
    psz = piece_sizes()
    xin = nc.dram_tensor("xin", [NITEM, 128, 8, NTOK_IN], F32, kind="ExternalInput").ap()
    wst = nc.dram_tensor("wst", [2, NPIECE, 128, PIECE_E], F32, kind="ExternalInput").ap()
    cpd = nc.dram_tensor("cp", [128, NCP], F32, kind="ExternalInput").ap()
    cbd = nc.dram_tensor("cb", [128, NCB], F32, kind="ExternalInput").ap()
    biasd = nc.dram_tensor("abias", [128, 4 * 384], F32, kind="ExternalInput").ap()
    sgbd = nc.dram_tensor("sgb", [128, 2 * 2 * 128], F32, kind="ExternalInput").ap()
    yout = nc.dram_tensor("yout", [NITEM, 128, 8, NOUT], F32, kind="ExternalOutput").ap()
    dbg_t = {}
    if debug:
        for name, spec in debug.items():
            if name.startswith("_"):
                continue
            dbg_t[name] = nc.dram_tensor("dbg_" + name, list(spec[0]), spec[1], kind="ExternalOutput").ap()
    n_items = NITEM if not (debug and debug.get("_items")) else 1

    from contextlib import ExitStack
    with ExitStack() as es:
        def sb(name, shape, dt):
            return es.enter_context(nc.sbuf_tensor(name, shape, dt))

        Xt = sb("X", [128, 8, NX], F32)
        Ht = sb("H", [128, 8, NTOK_IN], BF16)
        RSZ = 69632
        Rt = sb("R", [128, RSZ], U8)
        WRt = sb("WR", [128, NSLOT, PIECE_E], BF16)
        CP = sb("CP", [128, NCP], F32)
        CB = sb("CB", [128, NCB], BF16)
        BIAS = sb("BIAS", [128, 4, 384], BF16)
        SGB = sb("SGB", [128, 2, 2, 128], F32)
        NSCR = 18
        SCRB = 1536
        SCRt = sb("SCR", [128, NSCR * SCRB], U8)
        SMt = sb("SM", [128, 32, 8], F32)
        PSt = es.enter_context(nc.psum_tensor("PS", [128, 8, 512], F32))

        def sem(name):
            return es.enter_context(nc.semaphore(name))

        T = Tracker()
        for en in ("pe", "act", "dve"):
            T.add_engine(en, sem("s_" + en))
        T.add_engine("sync")
        T.add_engine("gq")
        for s_ in range(NSLOT):
            T.add_dma_sem("w%d" % s_, sem("s_w%d" % s_))
        for n_ in ("xa0", "xa1", "xa2", "xh", "out0", "out1", "cst", "cstg", "dbg"):
            T.add_dma_sem(n_, sem("s_" + n_))

        def PE(name, r, w, inc=True, **kw):
            T.op("pe", name, kw, r, w, inc)

        def ACT(name, r, w, **kw):
            T.op("act", name, kw, r, w)

        def DVE(name, r, w, **kw):
            T.op("dve", name, kw, r, w)

        def rview(off, nbytes, dt, pat=None, **kw):
            v = Rt[:, off:off + nbytes].bitcast(dt)
            if pat:
                v = v.rearrange(pat, **kw)
            return v
        Zc = {}
        Zc[12] = rview(0, 3072, BF16)
        VT = rview(3072, 3072, BF16, "p (b d) -> p b d", b=12)
        VS = rview(6144, 5120, BF16, "p (b d) -> p b d", b=10)
        for zi_ in range(8, 12):
            Zc[zi_] = rview(11264 + (zi_ - 8) * 3072, 3072, BF16)
        for zi_ in range(8):
            Zc[zi_] = rview(23552 + zi_ * 3072, 3072, BF16)
        BR = rview(48128, 20480, BF16, "p (c t) -> p c t", c=8)
        Mv = rview(0, 20480, BF16, "p (c t) -> p c t", c=8)
        Yv = rview(20480, 40960, F32, "p (c t) -> p c t", c=8)
        HID = rview(0, 28160, BF16, "p (c t) -> p c t", c=11)
        Y2 = rview(28160, 40960, F32, "p (c t) -> p c t", c=8)

        def scr(i, dt, n=None, nslots=1):
            v = SCRt[:, i * SCRB:(i + nslots) * SCRB].bitcast(dt)
            if n is not None:
                v = v[:, 0:n]
            return v

        def scrk(i, nslots=1):
            return [("scr", i + j) for j in range(nslots)]

        def bk(name, chunks, t0, t1):
            return [(name, c, b) for c in chunks for b in range(t0 // 128, (t1 + 127) // 128)]

        def cpc(j):
            return CP[:, j:j + 1]

        KC = [("const",), ("constb",)]
        ident = CB[:, CB_ID:CB_ID + 128]
        onesm = CB[:, CB_ON:CB_ON + 128]

        bank_ctr = [0]

        held = set()

        def nbank():
            for _ in range(8):
                b = bank_ctr[0] % 8
                bank_ctr[0] += 1
                if b not in held:
                    return b
            raise RuntimeError("no free PSUM bank")

        def try_hold(n):
            if len(held) + n > 6:
                return None
            step = 2 if n == 2 else 1
            for b in range(0, 8, step):
                if all((b + i) not in held for i in range(n)):
                    for i in range(n):
                        held.add(b + i)
                    return b
            return None

        def release(b, n=1):
            for i in range(n):
                held.discard(b + i)

        def psk(b):
            return [("ps", b)]

        sm_ctr = [0]

        def nsm():
            i = sm_ctr[0] % 32
            sm_ctr[0] += 1
            return i

        alt = [0]

        def evac_copy(out_ap, in_ap, r, w):
            alt[0] += 1
            if alt[0] % 2 == 0:
                ACT("activation", r, w, out=out_ap, in_=in_ap, func=AF.Copy)
            else:
                DVE("tensor_copy", r, w, out=out_ap, in_=in_ap)

        T.dma("sync", "cst", dict(out=CP[:], in_=cpd[:, :]), w=[("const",)])
        T.dma("sync", "cst", dict(out=SGB[:].rearrange("p l c t -> p (l c t)"), in_=sgbd[:, :]), w=[("const",)])
        T.dma("gq", "cstg", dict(out=CB[:, 0:1792], in_=cbd[:, 0:1792]), w=[("constb",)])
        T.dma("gq", "cstg", dict(out=CB[:, 1792:NCB], in_=cbd[:, 1792:NCB]), w=[("constb",)])
        T.dma("gq", "cstg", dict(out=BIAS[:].rearrange("p h k -> p (h k)"), in_=biasd[:, :]), w=[("constb",)])

        DVE("tensor_scalar", KC, [("nsk",)], out=CP[:, B_NSK:B_NSK + 8], in0=CP[:, B_SK:B_SK + 8], scalar1=-1.0,
            scalar2=None, op0=ALU.mult)

        gp_state = {"next": 0}
        total_pieces = n_items * 2 * NPIECE

        def issue_piece(g):
            l = (g // NPIECE) % 2
            j = g % NPIECE
            slot = g % NSLOT
            n = psz[j]
            sp = {4096: 1024, 2048: 1024, 2816: 1408, 2560: 1280}[n]
            dst = WRt[:, slot, 0:n].rearrange("p (a b) -> p a b", b=sp)
            src = wst[l, j, :, 0:n].rearrange("p (a b) -> p a b", b=sp)
            T.dma("gq", "w%d" % slot, dict(out=dst, in_=src), w=[("w", slot)])

        def acquire_piece(g, hold_from=None):
            if hold_from is None:
                hold_from = g
            while gp_state["next"] < min(hold_from + NSLOT, total_pieces):
                issue_piece(gp_state["next"])
                gp_state["next"] += 1
            assert gp_state["next"] > g
            return g % NSLOT

        def stats_rstd(src_fn, n):
            b1 = nbank()
            for c in range(8):
                ap, keys = src_fn(c)
                si = 6 + (c % 2)
                sq = scr(si, BF16, n)
                ACT("activation", keys, scrk(si), out=sq, in_=ap, func=AF.Square)
                PE("matmul", scrk(si) + KC, psk(b1), out=PSt[:, b1, 0:n], lhsT=onesm, rhs=sq,
                   start=(c == 0), stop=(c == 7))
            sr = scr(8, F32, n, nslots=2)
            ACT("activation", psk(b1) + KC, scrk(8, 2), out=sr, in_=PSt[:, b1, 0:n], func=AF.Sqrt, bias=epsc, scale=1.0)
            b2 = nbank()
            DVE("reciprocal", scrk(8, 2), psk(b2), out=PSt[:, b2, 0:n], in_=sr)
            return b2

        def prenorm(l, w, a, b, src_fn):
            n = b - a
            b2 = stats_rstd(src_fn, n)
            for c in range(8):
                ap, keys = src_fn(c)
                DVE("scalar_tensor_tensor", keys + psk(b2) + KC, bk("H", [c], a, b),
                    out=Ht[:, c, a:b], in0=ap, scalar=cpc(B_G + (l * 4 + w) * 8 + c), in1=PSt[:, b2, 0:n],
                    op0=ALU.mult, op1=ALU.mult)

        def xsrc(a, b):
            def f(c):
                return Xt[:, c, a - 128:b - 128], bk("X", [c], a, b)
            return f

        def postnorm_residual(l, w, a, b, Ybuf, yname):
            n = b - a

            def ysrc(c):
                return Ybuf[:, c, a - 128:b - 128], bk(yname, [c], a, b)
            b2 = stats_rstd(ysrc, n)
            tms = {}
            for c in range(9):
                if c < 8:
                    tb = nbank()
                    while tb == b2 or tb in [v for k_, v in tms.items() if k_ >= c - 1]:
                        tb = nbank()
                    tms[c] = tb
                    DVE("scalar_tensor_tensor", bk(yname, [c], a, b) + psk(b2) + KC, psk(tb),
                        out=PSt[:, tb, 0:n], in0=Ybuf[:, c, a - 128:b - 128], scalar=cpc(B_G + (l * 4 + w) * 8 + c),
                        in1=PSt[:, b2, 0:n], op0=ALU.mult, op1=ALU.mult)
                if c >= 1:
                    cp_ = c - 1
                    xs = Xt[:, cp_, a - 128:b - 128]
                    DVE("tensor_tensor", psk(tms[cp_]) + bk("X", [cp_], a, b), bk("X", [cp_], a, b),
                        out=xs, in0=xs, in1=PSt[:, tms[cp_], 0:n], op=ALU.add)

        def fm_proj(slot, woff, nk, rhs_fn, a, b, rkeys):
            n = b - a
            bnk = nbank()
            for k in range(nk):
                PE("matmul", [("w", slot)] + rkeys, psk(bnk), inc=(k == nk - 1),
                   out=PSt[:, bnk, 0:n], lhsT=WRt[:, slot, woff + k * 128: woff + (k + 1) * 128],
                   rhs=rhs_fn(k), start=(k == 0), stop=(k == nk - 1))
            return bnk

        class _Stop(Exception):
            pass

        def chk(name):
            if debug and debug.get("_stop") == name:
                raise _Stop()

        def dump(name, ap, keys):
            if name in dbg_t:
                T.dma("sync", "dbg", dict(out=dbg_t[name], in_=ap), r=keys)

        def layer(it, l, KV0, KV1, C0, C1, gbase, after_p9=None):
            kv_tiles = tiles_of(KV0, KV1)
            c_tiles = tiles_of(C0, C1)
            kvb = list(range(KV0 // 128, KV1 // 128))
            cb_ = list(range(C0 // 128, C1 // 128))
            g = gbase

            WORDER = [3, 2, 0, 1]

            def gen_win(pjs):
                for pj in pjs:
                    slot = acquire_piece(g + WORDER.index(pj))
                    nchunks = 4 if pj < 3 else 1
                    for ci in range(nchunks):
                        zi = pj * 4 + ci
                        for (a, b) in kv_tiles:
                            bnk = fm_proj(slot, ci * 1024, 8, lambda k, a=a, b=b: Ht[:, k, a:b], a, b,
                                          bk("H", range(8), a, b))
                            evac_copy(Zc[zi][:, a:b], PSt[:, bnk, 0:b - a], psk(bnk), bk("Z", [zi], a, b))
                            yield
                    if pj == 3:
                        for blk in kvb:
                            t0 = blk * 128
                            inC = blk in cb_
                            ncol = 384 if inC else 128
                            bnk = nbank()
                            for k in range(8):
                                PE("matmul", [("w", slot)] + bk("H", range(8), t0, t0 + 128), psk(bnk), inc=(k == 7),
                                   out=PSt[:, bnk, 0:ncol], lhsT=Ht[:, k, t0:t0 + 128],
                                   rhs=WRt[:, slot, 1024 + k * 384: 1024 + k * 384 + ncol],
                                   start=(k == 0), stop=(k == 7))
                            ACT("activation", psk(bnk), [("VT", blk)], out=VT[:, blk, :], in_=PSt[:, bnk, 0:128], func=AF.Copy)
                            if inC:
                                s1 = nsm()
                                s2 = nsm()
                                DVE("bn_stats", psk(bnk), [("sm", s1)], out=SMt[:, s1, 0:6], in_=PSt[:, bnk, 128:384])
                                DVE("bn_aggr", [("sm", s1)], [("sm", s2)], out=SMt[:, s2, 0:2], in_=SMt[:, s1, 0:6])
                                ACT("activation", [("sm", s2)] + KC, [("sm", s2)], out=SMt[:, s2, 2:3], in_=SMt[:, s2, 1:2],
                                    func=AF.Sqrt, bias=epsc, scale=1.0)
                                DVE("reciprocal", [("sm", s2)], [("sm", s2)], out=SMt[:, s2, 3:4], in_=SMt[:, s2, 2:3])
                                DVE("tensor_scalar", psk(bnk) + [("sm", s2), ("sm", s2)], [("VS", blk)],
                                    out=VS[:, blk - 1, :], in0=PSt[:, bnk, 128:384], scalar1=SMt[:, s2, 0:1],
                                    scalar2=SMt[:, s2, 3:4], op0=ALU.subtract, op1=ALU.mult)
                            yield

            def gen_conv():
                for cc in range(2):
                    for (a, b) in c_tiles:
                        n = b - a
                        U = scr(12, BF16, n + 2)
                        DVE("tensor_tensor", bk("Z", [2 + cc, 4 + cc], a - 1, b + 1), scrk(12),
                            out=U, in0=Zc[2 + cc][:, a - 1:b + 1], in1=Zc[4 + cc][:, a - 1:b + 1], op=ALU.mult)
                        bnk = nbank()
                        for tap in range(3):
                            o = CB_DG + ((l * 3 + tap) * 2 + cc) * 128
                            PE("matmul", scrk(12) + KC, psk(bnk), inc=(tap == 2), out=PSt[:, bnk, 0:n],
                               lhsT=CB[:, o:o + 128], rhs=U[:, tap:tap + n], start=(tap == 0), stop=(tap == 2))
                        DVE("tensor_tensor", psk(bnk) + bk("Z", [cc], a, b), bk("BR", [cc], a, b),
                            out=BR[:, cc, a - 128:b - 128], in0=Zc[cc][:, a:b], in1=PSt[:, bnk, 0:n], op=ALU.mult)
                        yield

            def gen_pool():
                for cc in range(2):
                    zp = 6 + cc
                    for (a, b) in c_tiles:
                        n = b - a
                        T1 = scr(14, BF16)
                        T2 = scr(15, BF16)
                        T3 = scr(16, BF16)
                        T4 = scr(17, BF16)
                        PL = scr(13, BF16, n)
                        zk = bk("Z", [zp], a - 8, b + 8)
                        DVE("tensor_tensor", zk, scrk(14), out=T1[:, 0:n + 14], in0=Zc[zp][:, a - 8:b + 6],
                            in1=Zc[zp][:, a - 7:b + 7], op=ALU.add)
                        DVE("tensor_tensor", scrk(14), scrk(15), out=T2[:, 0:n + 12], in0=T1[:, 0:n + 12],
                            in1=T1[:, 2:n + 14], op=ALU.add)
                        if cc == 0:
                            sel = [(T1, 7, 14), (T2, 6, 15)]
                        else:
                            yield
                            DVE("tensor_tensor", scrk(15), scrk(16), out=T3[:, 0:n + 8], in0=T2[:, 0:n + 8],
                                in1=T2[:, 4:n + 12], op=ALU.add)
                            DVE("tensor_tensor", scrk(16), scrk(17), out=T4[:, 0:n], in0=T3[:, 0:n],
                                in1=T3[:, 8:n + 8], op=ALU.add)
                            sel = [(T3, 4, 16), (T4, 0, 17)]
                        yield
                        for hf in range(2):
                            Ts, off, si = sel[hf]
                            r0 = hf * 64
                            DVE("scalar_tensor_tensor", scrk(si) + bk("Z", [zp], a, b) + KC, scrk(13),
                                out=PL[r0:r0 + 64, :], in0=Ts[r0:r0 + 64, off:off + n],
                                scalar=CP[r0:r0 + 64, B_IW + cc:B_IW + cc + 1],
                                in1=Zc[zp][r0:r0 + 64, a:b], op0=ALU.mult, op1=ALU.subtract)
                            for side, e0 in ((0, 256), (1, 1272)):
                                if a <= e0 and e0 + 8 <= b:
                                    j0 = e0 - a
                                    tb = B_PT + ((it * 2 + side) * 2 + cc) * 8
                                    sm = nsm()
                                    DVE("tensor_tensor", scrk(si) + KC, [("sm", sm)],
                                        out=SMt[r0:r0 + 64, sm, :], in0=Ts[r0:r0 + 64, off + j0:off + j0 + 8],
                                        in1=CP[r0:r0 + 64, tb:tb + 8], op=ALU.mult)
                                    DVE("tensor_tensor", [("sm", sm)] + bk("Z", [zp], e0, e0 + 8), scrk(13),
                                        out=PL[r0:r0 + 64, j0:j0 + 8], in0=SMt[r0:r0 + 64, sm, :],
                                        in1=Zc[zp][r0:r0 + 64, e0:e0 + 8], op=ALU.subtract)
                        bnk = nbank()
                        PE("matmul", scrk(13) + KC, psk(bnk), out=PSt[:, bnk, 0:n],
                           lhsT=CB[:, CB_PW + (l * 2 + cc) * 128: CB_PW + (l * 2 + cc + 1) * 128],
                           rhs=PL, start=True, stop=True)
                        ACT("activation", psk(bnk) + KC, bk("BR", [2 + cc], a, b),
                            out=BR[:, 2 + cc, a - 128:b - 128], in_=PSt[:, bnk, 0:n], func=AF.Copy,
                            scale=cpc(B_PS + l * 2 + cc))
                        yield

            def gen_sg():
                for blk in cb_:
                    t0 = blk * 128
                    for cc in range(2):
                        bnk = nbank()
                        PE("matmul", [("VS", blk)] + KC, psk(bnk), out=PSt[:, bnk, 0:256],
                           lhsT=VS[:, blk - 1, cc * 128:(cc + 1) * 128],
                           rhs=CB[:, CB_SGW + (l * 2 + cc) * 256: CB_SGW + (l * 2 + cc + 1) * 256],
                           start=True, stop=True)
                        for hf in range(2):
                            r0 = hf * 64
                            tm = PSt[r0:r0 + 64, bnk, 256 + hf * 128:256 + (hf + 1) * 128]
                            DVE("scalar_tensor_tensor", psk(bnk) + KC, psk(bnk),
                                out=tm, in0=PSt[r0:r0 + 64, bnk, hf * 128:(hf + 1) * 128],
                                scalar=CP[r0:r0 + 64, B_SN + l * 2 + cc:B_SN + l * 2 + cc + 1],
                                in1=SGB[r0:r0 + 64, l, cc, :], op0=ALU.mult, op1=ALU.add)
                            DVE("tensor_tensor", psk(bnk) + bk("Z", [8 + cc], t0, t0 + 128), bk("BR", [4 + cc], t0, t0 + 128),
                                out=BR[r0:r0 + 64, 4 + cc, t0 - 128:t0], in0=Zc[8 + cc][r0:r0 + 64, t0:t0 + 128],
                                in1=tm, op=ALU.mult)
                        yield

            sinkv = CP[:, B_SK + l * 4:B_SK + l * 4 + 4]
            nsinkv = CP[:, B_NSK + l * 4:B_NSK + l * 4 + 4]

            def attn_unit(blk, hp, u):
                t0 = blk * 128
                k0 = t0 - 128
                sb0 = (u % 6) * 2
                Pb = SCRt[:, sb0 * SCRB:sb0 * SCRB + 1536].bitcast(BF16).rearrange("p (h k) -> p h k", h=2)
                Pk = scrk(sb0)
                PTs = SCRt[:, (sb0 + 1) * SCRB:(sb0 + 1) * SCRB + 1536].bitcast(BF16).rearrange("p (h k) -> p h k", h=2)
                PTk = scrk(sb0 + 1)
                hds = (2 * hp, 2 * hp + 1)
                r0 = hp * 64
                b0 = try_hold(2)
                while b0 is None:
                    yield
                    b0 = try_hold(2)
                Sk = psk(b0) + psk(b0 + 1)
                for i2, hd in enumerate(hds):
                    qc = 10 + (hd % 2)
                    PE("matmul", bk("Z", [qc], t0, t0 + 128) + bk("Z", [12], k0, k0 + 384), psk(b0 + i2), inc=False,
                       out=PSt[:, b0 + i2, 0:384], lhsT=Zc[qc][r0:r0 + 64, t0:t0 + 128],
                       rhs=Zc[12][r0:r0 + 64, k0:k0 + 384], start=True, stop=False)
                    PE("matmul", KC, psk(b0 + i2), out=PSt[:, b0 + i2, 0:384], lhsT=ident, rhs=BIAS[:, hd, :],
                       start=False, stop=True)
                S2 = PSt[:, b0:b0 + 2, 0:384]
                if blk == 2:
                    DVE("tensor_scalar", Sk + KC, Sk, out=PSt[:, b0:b0 + 2, 0:128], in0=PSt[:, b0:b0 + 2, 0:128],
                        scalar1=cpc(B_FL + it * 4 + 0), scalar2=None, op0=ALU.add)
                if blk == 9:
                    DVE("tensor_scalar", Sk + KC, Sk, out=PSt[:, b0:b0 + 2, 256:384], in0=PSt[:, b0:b0 + 2, 256:384],
                        scalar1=cpc(B_FL + it * 4 + 1), scalar2=None, op0=ALU.add)
                yield
                s1 = nsm()
                s2 = nsm()
                s3 = nsm()
                s4 = nsm()
                SMr = SMt[:, s1, :]
                SMq = SMt[:, s2, :]
                SMs = SMt[:, s3, :]
                SMd = SMt[:, s4, :]
                sk2 = sinkv[:, 2 * hp:2 * hp + 2]
                DVE("tensor_reduce", Sk, [("sm", s1)], out=SMr[:, 0:2], in_=S2, axis=AX.X, op=ALU.max, negate=True)
                DVE("scalar_tensor_tensor", [("sm", s1), ("nsk",)] + KC, [("sm", s2)], out=SMq[:, 0:2], in0=SMr[:, 0:2],
                    scalar=0.125, in1=nsinkv[:, 2 * hp:2 * hp + 2], op0=ALU.mult, op1=ALU.min)
                yield
                for i2 in range(2):
                    ACT("activation", psk(b0 + i2) + [("sm", s2)], Pk + [("sm", s3)],
                        out=Pb[:, i2, :], in_=PSt[:, b0 + i2, 0:384], func=AF.Exp, bias=SMq[:, i2:i2 + 1], scale=0.125,
                        accum_out=SMs[:, i2:i2 + 1])
                    ACT("activation", [("sm", s2)] + KC, [("sm", s3)], out=SMs[:, 4 + i2:5 + i2],
                        in_=sinkv[:, 2 * hp + i2:2 * hp + i2 + 1], func=AF.Exp, bias=SMq[:, i2:i2 + 1], scale=1.0)
                release(b0, 2)
                yield
                DVE("tensor_tensor", [("sm", s3), ("sm", s3), ("sm", s3), ("sm", s3)], [("sm", s4)],
                    out=SMd[:, 0:2], in0=SMs[:, 0:2], in1=SMs[:, 4:6], op=ALU.add)
                DVE("reciprocal", [("sm", s4)], [("sm", s4)], out=SMd[:, 4:6], in_=SMd[:, 0:2])
                yield
                for i2 in range(2):
                    ACT("activation", Pk + [("sm", s4)], Pk, out=Pb[:, i2, :], in_=Pb[:, i2, :], func=AF.Copy,
                        scale=SMd[:, 4 + i2:5 + i2])
                yield
                tbk = try_hold(1)
                while tbk is None:
                    yield
                    tbk = try_hold(1)
                ptv = PSt[:, tbk, :].bitcast(BF16)
                for i2 in range(2):
                    for j in range(3):
                        c0 = i2 * 384 + j * 128
                        PE("transpose", Pk + KC, psk(tbk), out=ptv[:, c0:c0 + 128],
                           in_=Pb[:, i2, j * 128:(j + 1) * 128], identity=ident)
                yield
                evac_copy(PTs.rearrange("p h k -> p (h k)"), ptv[:, 0:768], psk(tbk), PTk)
                release(tbk)
                yield
                ob = try_hold(1)
                while ob is None:
                    yield
                    ob = try_hold(1)
                for i2 in range(2):
                    for j in range(3):
                        PE("matmul", PTk + [("VT", blk - 1 + j)], psk(ob), inc=(j == 2),
                           out=PSt[i2 * 64:i2 * 64 + 64, ob, 0:128],
                           lhsT=VT[:, blk - 1 + j, hp * 64:(hp + 1) * 64], rhs=PTs[:, i2, j * 128:(j + 1) * 128],
                           start=(j == 0), stop=(j == 2))
                yield
                ACT("activation", psk(ob), bk("BR", [6 + hp], t0, t0 + 128),
                    out=BR[:, 6 + hp, t0 - 128:t0], in_=PSt[:, ob, 0:128], func=AF.Copy)
                release(ob)

            for _ in gen_win([3, 2]):
                pass
            import itertools
            others = itertools.chain(gen_win([0, 1]), gen_sg(), gen_conv(), gen_pool())
            others_done = False
            units = [(blk, hp) for blk in cb_ for hp in range(2)]
            active = []
            nxt = 0
            while nxt < len(units) or active or not others_done:
                if nxt < len(units) and len(active) < 6:
                    active.append(attn_unit(units[nxt][0], units[nxt][1], nxt))
                    nxt += 1
                for gen in list(active):
                    try:
                        next(gen)
                    except StopIteration:
                        active.remove(gen)
                for _ in range(1):
                    if not others_done:
                        try:
                            next(others)
                        except StopIteration:
                            others_done = True
            g += 4

            if it == 0 and ("BR%d" % l) in dbg_t:
                dump("BR%d" % l, BR, bk("BR", range(8), 128, 1408))

            chk("p3d")
            T.barrier(["pe", "act", "dve"])
            GB = [0, 1, 2]
            PB = [3, 4, 5]
            MB = [6, 7]
            gctr = [0]
            mctr = [0]
            pend = []

            def flush_pend(keep):
                while len(pend) > keep:
                    pend.pop(0)()
            for c in range(8):
                slots2 = [acquire_piece(g, hold_from=g), acquire_piece(g + 1, hold_from=g)]
                g += 2
                for (a, b) in c_tiles:
                    n = b - a
                    mb = MB[mctr[0] % 2]
                    mctr[0] += 1
                    for i in range(4):
                        gslot = slots2[i // 2]
                        i2 = i % 2
                        gb = GB[gctr[0] % 3]
                        pb = PB[gctr[0] % 3]
                        gi = gctr[0] % 2
                        gctr[0] += 1
                        for k in range(8):
                            wo = (i2 * 10 + k) * 128
                            PE("matmul", [("w", gslot)] + bk("H", range(8), a, b), psk(gb), inc=(k == 7),
                               out=PSt[:, gb, 0:n], lhsT=WRt[:, gslot, wo:wo + 128],
                               rhs=Ht[:, k, a:b], start=(k == 0), stop=(k == 7))
                        for kk in range(2):
                            wo = (i2 * 10 + 8 + kk) * 128
                            PE("matmul", [("w", gslot)] + bk("BR", [2 * i, 2 * i + 1], a, b), psk(pb), inc=(kk == 1),
                               out=PSt[:, pb, 0:n], lhsT=WRt[:, gslot, wo:wo + 128],
                               rhs=BR[:, 2 * i + kk, a - 128:b - 128], start=(kk == 0), stop=(kk == 1))
                        gsb = scr(2 * gi, F32, n, nslots=2)
                        tsb = scr(4 + gi, BF16, n)
                        ACT("activation", psk(gb) + KC, scrk(2 * gi, 2), out=gsb, in_=PSt[:, gb, 0:n], func=AF.Sigmoid,
                            bias=cpc(B_BG + (l * 4 + i) * 8 + c), scale=1.0)
                        DVE("tensor_tensor", scrk(2 * gi, 2) + psk(pb), scrk(4 + gi), out=tsb, in0=gsb, in1=PSt[:, pb, 0:n],
                            op=ALU.mult)

                        def acc(i=i, mb=mb, tsb=tsb, n=n, gi=gi, a=a, b=b, c=c):
                            PE("matmul", scrk(4 + gi) + KC, psk(mb), out=PSt[:, mb, 0:n], lhsT=ident, rhs=tsb,
                               start=(i == 0), stop=(i == 3))
                            if i == 3:
                                ACT("activation", psk(mb), bk("M", [c], a, b), out=Mv[:, c, a - 128:b - 128],
                                    in_=PSt[:, mb, 0:n], func=AF.Copy)
                        pend.append(acc)
                        flush_pend(1)
            flush_pend(0)

            if l == 1 and it + 1 < n_items:
                load_halo(it + 1)
            chk("p4")
            wslots = [acquire_piece(g, hold_from=g), acquire_piece(g + 1, hold_from=g)]
            g += 2
            for (a, b) in c_tiles:
                for op_ in range(2):
                    for c4 in range(4):
                        c = op_ * 4 + c4
                        bnk = fm_proj(wslots[op_], c4 * 1024, 8, lambda k, a=a, b=b: Mv[:, k, a - 128:b - 128], a, b,
                                      bk("M", range(8), a, b))
                        evac_copy(Yv[:, c, a - 128:b - 128], PSt[:, bnk, 0:b - a], psk(bnk), bk("Y", [c], a, b))
                postnorm_residual(l, 1, a, b, Yv, "Y")
                prenorm(l, 2, a, b, xsrc(a, b))
            chk("p6")
            for half in range(2):
                for pp in range(6):
                    slot = acquire_piece(g)
                    g += 1
                    for jj in range(2 if pp < 5 else 1):
                        j = pp * 2 + jj
                        for (a, b) in c_tiles:
                            n = b - a
                            ba = fm_proj(slot, (jj * 2 + 0) * 1024, 8, lambda k, a=a, b=b: Ht[:, k, a:b], a, b,
                                         bk("H", range(8), a, b))
                            bg = fm_proj(slot, (jj * 2 + 1) * 1024, 8, lambda k, a=a, b=b: Ht[:, k, a:b], a, b,
                                         bk("H", range(8), a, b))
                            si = j % 2
                            ssb = scr(2 * si, F32, n, nslots=2)
                            ACT("activation", psk(ba), scrk(2 * si, 2), out=ssb, in_=PSt[:, ba, 0:n], func=AF.Silu)
                            DVE("tensor_tensor", scrk(2 * si, 2) + psk(bg), bk("HID", [j], a, b),
                                out=HID[:, j, a - 128:b - 128], in0=ssb, in1=PSt[:, bg, 0:n], op=ALU.mult)
                for pp in range(4):
                    slot = acquire_piece(g)
                    g += 1
                    for c2 in range(2):
                        c = pp * 2 + c2
                        for (a, b) in c_tiles:
                            n = b - a
                            bnk = fm_proj(slot, c2 * 1408, 11, lambda k, a=a, b=b: HID[:, k, a - 128:b - 128], a, b,
                                          bk("HID", range(11), a, b))
                            y2s = Y2[:, c, a - 128:b - 128]
                            if half == 0:
                                evac_copy(y2s, PSt[:, bnk, 0:n], psk(bnk), bk("Y2", [c], a, b))
                            else:
                                DVE("tensor_tensor", psk(bnk) + bk("Y2", [c], a, b), bk("Y2", [c], a, b),
                                    out=y2s, in0=y2s, in1=PSt[:, bnk, 0:n], op=ALU.add)
            chk("p8")
            for ti, (a, b) in enumerate(c_tiles):
                postnorm_residual(l, 3, a, b, Y2, "Y2")
                if after_p9 is not None:
                    after_p9(ti, a, b)
            assert g == gbase + NPIECE
            if it == 0:
                dump("X%d" % l, Xt[:], bk("X", range(8), 128, 1408))

        epsc = EPS
        XH = SCRt[:, 12 * SCRB:12 * SCRB + 8192].bitcast(F32).rearrange("p (c t) -> p c t", c=8)
        def load_halo(it_):
            T.dma("sync", "xh", dict(out=XH[:, :, 0:128], in_=xin[it_, :, :, 0:128]), w=scrk(12, 6))
            T.dma("sync", "xh", dict(out=XH[:, :, 128:256], in_=xin[it_, :, :, 1408:1536]), w=scrk(12, 6))

        def main_body():
            for it in range(n_items):
                for ti, (a, b) in enumerate(tiles_of(128, 1408)):
                    T.dma("sync", "xa%d" % ti, dict(out=Xt[:, :, a - 128:b - 128], in_=xin[it, :, :, a:b]),
                          w=bk("X", range(8), a, b))
                if it == 0:
                    load_halo(0)
                gbase = (it * 2) * NPIECE

                def hsrc0(c):
                    return XH[:, c, 0:128], scrk(12, 6)

                def hsrc1(c):
                    return XH[:, c, 128:256], scrk(12, 6)
                chk("p0")
                tl = tiles_of(128, 1408)
                prenorm(0, 0, 0, 128, hsrc0)
                prenorm(0, 0, tl[0][0], tl[0][1], xsrc(*tl[0]))
                prenorm(0, 0, tl[1][0], tl[1][1], xsrc(*tl[1]))
                prenorm(0, 0, tl[2][0], tl[2][1], xsrc(*tl[2]))
                prenorm(0, 0, 1408, 1536, hsrc1)
                chk("p1")
                layer(it, 0, 0, 1536, 128, 1408, gbase)
                for (a, b) in tiles_of(128, 1408):
                    prenorm(1, 0, a, b, xsrc(a, b))
                hk = bk("H", range(8), 128, 256)
                DVE("tensor_scalar", hk + KC, hk, out=Ht[:, :, 128:256], in0=Ht[:, :, 128:256],
                    scalar1=cpc(B_FL + it * 4 + 2), scalar2=None, op0=ALU.mult)
                hk = bk("H", range(8), 1280, 1408)
                DVE("tensor_scalar", hk + KC, hk, out=Ht[:, :, 1280:1408], in0=Ht[:, :, 1280:1408],
                    scalar1=cpc(B_FL + it * 4 + 3), scalar2=None, op0=ALU.mult)
                def store_tile(ti, a, b, it=it):
                    T.dma("sync", "out%d" % ti, dict(out=yout[it, :, :, a - 256:b - 256], in_=Xt[:, :, a - 128:b - 128]),
                          r=bk("X", range(8), a, b))
                layer(it, 1, 128, 1408, 256, 1280, gbase + NPIECE, after_p9=store_tile)
        try:
            main_body()
        except _Stop:
            pass
        T.barrier(["pe", "act", "dve"])
        T.final_wait("sync", list(T.dcount.keys()))

        with nc.Block() as block:
            @block.sync
            def _(h):
                T.emit("sync", h)

            @block.gpsimd
            def _(h):
                T.emit("gq", h)

            @block.tensor
            def _(h):
                T.emit("pe", h)

            @block.scalar
            def _(h):
                T.emit("act", h)

            @block.vector
            def _(h):
                T.emit("dve", h)
    return nc


def pack_weights(w_in, w_branch, w_gate, w_out, w_ffn_in, w_ffn_out):
    wst = np.zeros((2, NPIECE, 128, PIECE_E), np.float32)

    def fm(Wcols):
        K = Wcols.shape[0] // 128
        return Wcols.reshape(K, 128, 128).transpose(1, 0, 2)
    qperm = np.concatenate([np.arange(1536, 1600), np.arange(1664, 1728), np.arange(1600, 1664), np.arange(1728, 1792)])
    fmcols = [np.arange(0, 128), np.arange(128, 256), np.arange(256, 384), np.arange(384, 512),
              np.arange(512, 640), np.arange(640, 768), np.arange(768, 896), np.arange(896, 1024),
              np.arange(1024, 1152), np.arange(1152, 1280), qperm[:128], qperm[128:], np.arange(1792, 1920)]
    tmcols = np.concatenate([np.arange(1920, 2048), np.arange(1280, 1536)])
    for l in range(2):
        wp = []
        for pj in range(3):
            parts = [fm(w_in[l][:, fmcols[pj * 4 + ci]]) for ci in range(4)]
            wp.append(np.stack(parts, axis=1).reshape(128, 4096))
        kpart = fm(w_in[l][:, fmcols[12]]).reshape(128, 1024)
        tm = w_in[l][:, tmcols].reshape(8, 128, 384).transpose(1, 0, 2).reshape(128, 3072)
        wp.append(np.concatenate([kpart, tm], axis=1))
        for j, pj in enumerate([3, 2, 0, 1]):
            wst[l, j] = wp[pj]
        j = 4
        for c in range(8):
            for ih in range(2):
                arr = np.zeros((128, 2, 10, 128), np.float32)
                for i2 in range(2):
                    i = ih * 2 + i2
                    arr[:, i2, 0:8] = fm(w_gate[l, i][:, c * 128:(c + 1) * 128])
                    arr[:, i2, 8:10] = fm(w_branch[l, i][:, c * 128:(c + 1) * 128])
                wst[l, j, :, :2560] = arr.reshape(128, 2560)
                j += 1
        for op_ in range(2):
            arr = np.stack([fm(w_out[l][:, (op_ * 4 + c4) * 128:(op_ * 4 + c4 + 1) * 128]) for c4 in range(4)], axis=1)
            wst[l, j] = arr.reshape(128, 4096)
            j += 1
        for half in range(2):
            for pp in range(6):
                njj = 2 if pp < 5 else 1
                arr = np.zeros((128, njj, 2, 8, 128), np.float32)
                for jj in range(njj):
                    hj = half * 11 + pp * 2 + jj
                    for ag in range(2):
                        arr[:, jj, ag] = fm(w_ffn_in[l][:, ag * DFF + hj * 128: ag * DFF + (hj + 1) * 128])
                wst[l, j, :, :njj * 2048] = arr.reshape(128, njj * 2048)
                j += 1
            for pp in range(4):
                arr = np.zeros((128, 2, 11, 128), np.float32)
                for c2 in range(2):
                    c = pp * 2 + c2
                    arr[:, c2] = fm(w_ffn_out[l][half * 11 * 128:(half + 1) * 11 * 128, c * 128:(c + 1) * 128])
                wst[l, j, :, :2816] = arr.reshape(128, 2816)
                j += 1
        assert j == NPIECE
    return wst


def item_list(core):
    items = []
    for a in (0, 8, 16, 24):
        items.append(("p", core, a, 32))
    for a in (16 * core, 16 * core + 8):
        items.append(("s", 0, a, 128))
    return items


def build_inputs(inp):
    f32 = np.float32
    x_prompt = np.asarray(inp["x_prompt"], f32)
    x_sample = np.asarray(inp["x_sample"], f32)
    wst = pack_weights(np.asarray(inp["w_in"], f32), np.asarray(inp["w_branch"], f32), np.asarray(inp["w_gate"], f32),
                       np.asarray(inp["w_out"], f32), np.asarray(inp["w_ffn_in"], f32), np.asarray(inp["w_ffn_out"], f32))
    cp0 = np.zeros((128, NCP), f32)
    gains = [inp["norm_mix_pre"], inp["norm_mix_post"], inp["norm_ffn_pre"], inp["norm_ffn_post"]]
    for l in range(2):
        for w in range(4):
            cp0[:, B_G + (l * 4 + w) * 8: B_G + (l * 4 + w) * 8 + 8] = np.asarray(gains[w], f32)[l].reshape(8, 128).T
        for i in range(4):
            cp0[:, B_BG + (l * 4 + i) * 8: B_BG + (l * 4 + i) * 8 + 8] = np.asarray(inp["b_gate"], f32)[l, i].reshape(8, 128).T
        for tap in range(3):
            cp0[:, B_CW + (l * 3 + tap) * 2: B_CW + (l * 3 + tap) * 2 + 2] = np.asarray(inp["conv_w"], f32)[l, tap].reshape(2, 128).T
        cp0[:, B_PS + l * 2: B_PS + l * 2 + 2] = np.asarray(inp["pool_scale"], f32)[l].reshape(2, 128).T
        cp0[:, B_SN + l * 2: B_SN + l * 2 + 2] = np.asarray(inp["sg_norm"], f32)[l].reshape(2, 128).T
        cp0[:, B_SK + l * 4: B_SK + l * 4 + 4] = np.asarray(inp["attn_sink"], f32)[l][None, :]
    wins = np.zeros((128, 2), np.int64)
    wins[:64, 0] = 2
    wins[64:, 0] = 4
    wins[:64, 1] = 8
    wins[64:, 1] = 16
    cp0[:, B_IW:B_IW + 2] = (1.0 / wins).astype(f32)
    cb = np.zeros((128, NCB), f32)
    sg_w = np.asarray(inp["sg_w"], f32)
    pool_w = np.asarray(inp["pool_w"], f32)
    for l in range(2):
        for cc in range(2):
            for jg in range(2):
                o = CB_SGW + (l * 2 + cc) * 256 + jg * 128
                cb[:, o:o + 128] = sg_w[l, 2 * cc + jg].T
            blk = np.zeros((128, 128), f32)
            blk[:64, :64] = pool_w[l, 2 * cc]
            blk[64:, 64:] = pool_w[l, 2 * cc + 1]
            cb[:, CB_PW + (l * 2 + cc) * 128: CB_PW + (l * 2 + cc + 1) * 128] = blk
    conv_w = np.asarray(inp["conv_w"], f32)
    for l in range(2):
        for tap in range(3):
            for cc in range(2):
                o = CB_DG + ((l * 3 + tap) * 2 + cc) * 128
                dg = np.zeros((128, 128), f32)
                dg[np.arange(128), np.arange(128)] = conv_w[l, tap, cc * 128:(cc + 1) * 128]
                cb[:, o:o + 128] = dg
    cb[:, CB_ID:CB_ID + 128] = np.eye(128, dtype=f32)
    cb[:, CB_ON:CB_ON + 128] = 1.0 / 1024.0
    qi = np.arange(128)[:, None]
    kj = np.arange(384)[None, :]
    rel = np.abs(qi - kj + 128).astype(f32)
    slopes = np.exp2(-8.0 * (np.arange(4, dtype=f32) + 1.0) / 4.0).astype(f32)
    abias = np.where(rel[:, None, :] <= 128, -8.0 * slopes[None, :, None] * rel[:, None, :], f32(-8e30)).astype(f32)
    abias = np.ascontiguousarray(abias.reshape(128, 4 * 384))
    sg_b = np.asarray(inp["sg_b"], f32)
    sgb = np.zeros((128, 2, 2, 128), f32)
    for l in range(2):
        for cc in range(2):
            sgb[:64, l, cc, :] = sg_b[l, 2 * cc][None, :]
            sgb[64:, l, cc, :] = sg_b[l, 2 * cc + 1][None, :]
    sgb = sgb.reshape(128, 512)

    in_maps = []
    for core in range(NCORES):
        items = item_list(core)
        xin = np.zeros((NITEM, 128, 8, NTOK_IN), f32)
        cp = cp0.copy()
        for it, (kind, sidx, a, nblk) in enumerate(items):
            src = x_prompt[sidx] if kind == "p" else x_sample[0]
            S = src.shape[0]
            t0 = (a - 2) * 128
            lo = max(t0, 0)
            hi = min(t0 + NTOK_IN, S)
            seg = src[lo:hi]
            xin[it, :, :, lo - t0:hi - t0] = seg.reshape(-1, 8, 128).transpose(2, 1, 0)
            lvalid = a > 0
            rvalid = (a + 8) < nblk
            cp[:, B_FL + it * 4 + 0] = 0.0 if lvalid else -1e30
            cp[:, B_FL + it * 4 + 1] = 0.0 if rvalid else -1e30
            cp[:, B_FL + it * 4 + 2] = 1.0 if lvalid else 0.0
            cp[:, B_FL + it * 4 + 3] = 1.0 if rvalid else 0.0
            for cc in range(2):
                w = wins[:, cc]
                for jx in range(8):
                    cl = w if lvalid else (jx + w // 2 - np.maximum(jx - w // 2, 0))
                    cr = w if rvalid else (np.minimum(w // 2, 8 - jx) + w // 2)
                    cp[:, B_PT + ((it * 2 + 0) * 2 + cc) * 8 + jx] = (1.0 / cl).astype(f32)
                    cp[:, B_PT + ((it * 2 + 1) * 2 + cc) * 8 + jx] = (1.0 / cr).astype(f32)
        in_maps.append({"xin": xin, "wst": wst, "cp": cp, "cb": cb, "abias": abias, "sgb": sgb})
    return in_maps


def assemble(results):
    yp = np.zeros((8, SEQ_P, D), np.float32)
    ys = np.zeros((1, SEQ_S, D), np.float32)
    for core in range(NCORES):
        y = results[core]["yout"]
        for it, (kind, sidx, a, nblk) in enumerate(item_list(core)):
            blk = y[it].transpose(2, 1, 0).reshape(NOUT, D)
            if kind == "p":
                yp[sidx, a * 128:a * 128 + NOUT] = blk
            else:
                ys[0, a * 128:a * 128 + NOUT] = blk
    return yp, ys


_DEBUG = None
_LAST = {}


def kernel(**inputs):
    in_maps = build_inputs(inputs)
    nc = build_program(_DEBUG)
    res = run_bass_kernel_spmd(nc, in_maps, core_ids=list(range(NCORES)))
    _LAST["res"] = res
    yp, ys = assemble(res.results)
    return (yp, ys)
```

```python
import numpy as np
import concourse.bass as bass
import concourse.mybir as mybir
from concourse.bass_utils import run_bass_kernel_spmd

F32 = mybir.dt.float32
BF16 = mybir.dt.bfloat16
U8 = mybir.dt.uint8
AF = mybir.ActivationFunctionType
ALU = mybir.AluOpType
AX = mybir.AxisListType

NCORES = 8
D = 1024
SEQ_P = 4096
SEQ_S = 16384
NITEM = 6
NTOK_IN = 1536
NX = 1280
NOUT = 1024
EPS = 1e-6
DFF = 2816
NPIECE = 42
PIECE_E = 4096
NSLOT = 4

B_G = 0
B_BG = 64
B_CW = 128
B_PS = 140
B_SN = 144
B_IW = 148
B_SK = 150
B_FL = 158
B_PT = 182
B_NSK = 374
NCP = 382
CB_SGW = 0
CB_PW = 1024
CB_ID = 1536
CB_ON = 1664
CB_DG = 1792
NCB = 3328


def piece_sizes():
    s = [4096, 4096, 4096, 4096]
    s += [2560] * 16
    s += [4096, 4096]
    for _ in range(2):
        s += [4096] * 5 + [2048]
        s += [2816] * 4
    assert len(s) == NPIECE
    return s


class Tracker:
    def __init__(self):
        self.streams = {}
        self.sems = {}
        self.count = {}
        self.known = {}
        self.lastw = {}
        self.readers = {}
        self.semh = {}
        self.dcount = {}

    def add_engine(self, name, sem=None):
        self.streams[name] = []
        self.known[name] = {}
        if sem is not None:
            self.sems[name] = sem
            self.semh[name] = sem
            self.count[name] = 0

    def add_dma_sem(self, name, sem):
        self.semh[name] = sem
        self.dcount[name] = 0

    def _waits(self, eng, r, w):
        need = {}

        def req(ev, same_ok):
            sn, val, en = ev
            if en == eng and same_ok and eng == "pe":
                return
            if need.get(sn, 0) < val:
                need[sn] = val
        for k in r:
            ev = self.lastw.get(k)
            if ev is not None:
                req(ev, False)
        for k in w:
            ev = self.lastw.get(k)
            if ev is not None:
                req(ev, True)
            for ev in self.readers.get(k, {}).values():
                req(ev, True)
        kn = self.known[eng]
        out = []
        for sn, val in need.items():
            if kn.get(sn, 0) < val:
                kn[sn] = val
                out.append((sn, val))
        return out

    def _commit(self, ev, r, w):
        for k in w:
            self.lastw[k] = ev
            self.readers[k] = {}
        for k in r:
            self.readers.setdefault(k, {})[ev[0]] = ev

    def op(self, eng, name, kw, r=(), w=(), inc=True):
        w = list(w) + [k for k in r if k[0] == "ps"]
        waits = self._waits(eng, r, w)
        if inc:
            self.count[eng] += 1
            ev = (eng, self.count[eng], eng)
        else:
            ev = (eng, self.count[eng] + 1, eng)
        self.streams[eng].append((waits, name, kw, self.sems[eng] if inc else None, 1))
        self._commit(ev, r, w)

    def dma(self, q, dsem, kw, r=(), w=()):
        waits = self._waits(q, r, w)
        self.dcount[dsem] += 16
        ev = (dsem, self.dcount[dsem], "dma:" + dsem)
        self.streams[q].append((waits, "dma_start", kw, self.semh[dsem], 16))
        self._commit(ev, r, w)

    def barrier(self, engs):
        for e in engs:
            waits = []
            for o in engs:
                if o == e:
                    continue
                val = self.count[o]
                if self.known[e].get(o, 0) < val:
                    self.known[e][o] = val
                    waits.append((o, val))
            self.streams[e].append((waits, None, None, None, 0))

    def final_wait(self, q, semnames):
        vals = [(sn, self.dcount[sn]) for sn in semnames if self.dcount[sn] > 0]
        self.streams[q].append((vals, None, None, None, 0))

    def emit(self, eng, h):
        semh = self.semh
        for waits, name, kw, isem, amt in self.streams[eng]:
            for sn, val in waits:
                h.wait_ge(semh[sn], val)
            if name is not None:
                ins = getattr(h, name)(**kw)
                if isem is not None:
                    ins.then_inc(isem, amt)


def tiles_of(t0, t1, step=512):
    out = []
    a = t0
    while a < t1:
        b = min(a + step, t1)
        out.append((a, b))
        a = b
    return out


def build_program(debug=None):
    nc = bass.Bass("TRN2", target_bir_lowering=False)
    psz = piece_sizes()
    xin = nc.dram_tensor("xin", [NITEM, 128, 8, NTOK_IN], F32, kind="ExternalInput").ap()
    wst = nc.dram_tensor("wst", [2, NPIECE, 128, PIECE_E], F32, kind="ExternalInput").ap()
    cpd = nc.dram_tensor("cp", [128, NCP], F32, kind="ExternalInput").ap()
    cbd = nc.dram_tensor("cb", [128, NCB], F32, kind="ExternalInput").ap()
    biasd = nc.dram_tensor("abias", [128, 4 * 384], F32, kind="ExternalInput").ap()
    sgbd = nc.dram_tensor("sgb", [128, 2 * 2 * 128], F32, kind="ExternalInput").ap()
    yout = nc.dram_tensor("yout", [NITEM, 128, 8, NOUT], F32, kind="ExternalOutput").ap()
    dbg_t = {}
    if debug:
        for name, spec in debug.items():
            if name.startswith("_"):
                continue
            dbg_t[name] = nc.dram_tensor("dbg_" + name, list(spec[0]), spec[1], kind="ExternalOutput").ap()
    n_items = NITEM if not (debug and debug.get("_items")) else 1

    from contextlib import ExitStack
    with ExitStack() as es:
        def sb(name, shape, dt):
            return es.enter_context(nc.sbuf_tensor(name, shape, dt))

        Xt = sb("X", [128, 8, NX], F32)
        Ht = sb("H", [128, 8, NTOK_IN], BF16)
        RSZ = 69632
        Rt = sb("R", [128, RSZ], U8)
        WRt = sb("WR", [128, NSLOT, PIECE_E], BF16)
        CP = sb("CP", [128, NCP], F32)
        CB = sb("CB", [128, NCB], BF16)
        BIAS = sb("BIAS", [128, 4, 384], BF16)
        SGB = sb("SGB", [128, 2, 2, 128], F32)
        NSCR = 18
        SCRB = 1536
        SCRt = sb("SCR", [128, NSCR * SCRB], U8)
        SMt = sb("SM", [128, 32, 8], F32)
        PSt = es.enter_context(nc.psum_tensor("PS", [128, 8, 512], F32))

        def sem(name):
            return es.enter_context(nc.semaphore(name))

        T = Tracker()
        for en in ("pe", "act", "dve"):
            T.add_engine(en, sem("s_" + en))
        T.add_engine("sync")
        T.add_engine("gq")
        for s_ in range(NSLOT):
            T.add_dma_sem("w%d" % s_, sem("s_w%d" % s_))
        for n_ in ("xa0", "xa1", "xa2", "xh", "out0", "out1", "cst", "cstg", "dbg"):
            T.add_dma_sem(n_, sem("s_" + n_))

        def PE(name, r, w, inc=True, **kw):
            T.op("pe", name, kw, r, w, inc)

        def ACT(name, r, w, **kw):
            T.op("act", name, kw, r, w)

        def DVE(name, r, w, **kw):
            T.op("dve", name, kw, r, w)

        def rview(off, nbytes, dt, pat=None, **kw):
            v = Rt[:, off:off + nbytes].bitcast(dt)
            if pat:
                v = v.rearrange(pat, **kw)
            return v
        Zc = {}
        Zc[12] = rview(0, 3072, BF16)
        VT = rview(3072, 3072, BF16, "p (b d) -> p b d", b=12)
        VS = rview(6144, 5120, BF16, "p (b d) -> p b d", b=10)
        for zi_ in range(8, 12):
            Zc[zi_] = rview(11264 + (zi_ - 8) * 3072, 3072, BF16)
        for zi_ in range(8):
            Zc[zi_] = rview(23552 + zi_ * 3072, 3072, BF16)
        BR = rview(48128, 20480, BF16, "p (c t) -> p c t", c=8)
        Mv = rview(0, 20480, BF16, "p (c t) -> p c t", c=8)
        Yv = rview(20480, 40960, F32, "p (c t) -> p c t", c=8)
        HID = rview(0, 28160, BF16, "p (c t) -> p c t", c=11)
        Y2 = rview(28160, 40960, F32, "p (c t) -> p c t", c=8)

        def scr(i, dt, n=None, nslots=1):
            v = SCRt[:, i * SCRB:(i + nslots) * SCRB].bitcast(dt)
            if n is not None:
                v = v[:, 0:n]
            return v

        def scrk(i, nslots=1):
            return [("scr", i + j) for j in range(nslots)]

        def bk(name, chunks, t0, t1):
            return [(name, c, b) for c in chunks for b in range(t0 // 128, (t1 + 127) // 128)]

        def cpc(j):
            return CP[:, j:j + 1]

        KC = [("const",), ("constb",)]
        ident = CB[:, CB_ID:CB_ID + 128]
        onesm = CB[:, CB_ON:CB_ON + 128]

        bank_ctr = [0]

        held = set()

        def nbank():
            for _ in range(8):
                b = bank_ctr[0] % 8
                bank_ctr[0] += 1
                if b not in held:
                    return b
            raise RuntimeError("no free PSUM bank")

        def try_hold(n):
            if len(held) + n > 6:
                return None
            step = 2 if n == 2 else 1
            for b in range(0, 8, step):
                if all((b + i) not in held for i in range(n)):
                    for i in range(n):
                        held.add(b + i)
                    return b
            return None

        def release(b, n=1):
            for i in range(n):
                held.discard(b + i)

        def psk(b):
            return [("ps", b)]

        sm_ctr = [0]

        def nsm():
            i = sm_ctr[0] % 32
            sm_ctr[0] += 1
            return i

        alt = [0]

        def evac_copy(out_ap, in_ap, r, w):
            alt[0] += 1
            if alt[0] % 2 == 0:
                ACT("activation", r, w, out=out_ap, in_=in_ap, func=AF.Copy)
            else:
                DVE("tensor_copy", r, w, out=out_ap, in_=in_ap)

        T.dma("sync", "cst", dict(out=CP[:], in_=cpd[:, :]), w=[("const",)])
        T.dma("sync", "cst", dict(out=SGB[:].rearrange("p l c t -> p (l c t)"), in_=sgbd[:, :]), w=[("const",)])
        T.dma("gq", "cstg", dict(out=CB[:, 0:1792], in_=cbd[:, 0:1792]), w=[("constb",)])
        T.dma("gq", "cstg", dict(out=CB[:, 1792:NCB], in_=cbd[:, 1792:NCB]), w=[("constb",)])
        T.dma("gq", "cstg", dict(out=BIAS[:].rearrange("p h k -> p (h k)"), in_=biasd[:, :]), w=[("constb",)])

        DVE("tensor_scalar", KC, [("nsk",)], out=CP[:, B_NSK:B_NSK + 8], in0=CP[:, B_SK:B_SK + 8], scalar1=-1.0,
            scalar2=None, op0=ALU.mult)

        gp_state = {"next": 0}
        total_pieces = n_items * 2 * NPIECE

        def issue_piece(g):
            l = (g // NPIECE) % 2
            j = g % NPIECE
            slot = g % NSLOT
            n = psz[j]
            sp = {4096: 1024, 2048: 1024, 2816: 1408, 2560: 1280}[n]
            dst = WRt[:, slot, 0:n].rearrange("p (a b) -> p a b", b=sp)
            src = wst[l, j, :, 0:n].rearrange("p (a b) -> p a b", b=sp)
            T.dma("gq", "w%d" % slot, dict(out=dst, in_=src), w=[("w", slot)])

        def acquire_piece(g, hold_from=None):
            if hold_from is None:
                hold_from = g
            while gp_state["next"] < min(hold_from + NSLOT, total_pieces):
                issue_piece(gp_state["next"])
                gp_state["next"] += 1
            assert gp_state["next"] > g
            return g % NSLOT

        def stats_rstd(src_fn, n):
            b1 = nbank()
            for c in range(8):
                ap, keys = src_fn(c)
                si = 6 + (c % 2)
                sq = scr(si, BF16, n)
                ACT("activation", keys, scrk(si), out=sq, in_=ap, func=AF.Square)
                PE("matmul", scrk(si) + KC, psk(b1), out=PSt[:, b1, 0:n], lhsT=onesm, rhs=sq,
                   start=(c == 0), stop=(c == 7))
            sr = scr(8, F32, n, nslots=2)
            ACT("activation", psk(b1) + KC, scrk(8, 2), out=sr, in_=PSt[:, b1, 0:n], func=AF.Sqrt, bias=epsc, scale=1.0)
            b2 = nbank()
            DVE("reciprocal", scrk(8, 2), psk(b2), out=PSt[:, b2, 0:n], in_=sr)
            return b2

        def prenorm(l, w, a, b, src_fn):
            n = b - a
            b2 = stats_rstd(src_fn, n)
            for c in range(8):
                ap, keys = src_fn(c)
                DVE("scalar_tensor_tensor", keys + psk(b2) + KC, bk("H", [c], a, b),
                    out=Ht[:, c, a:b], in0=ap, scalar=cpc(B_G + (l * 4 + w) * 8 + c), in1=PSt[:, b2, 0:n],
                    op0=ALU.mult, op1=ALU.mult)

        def xsrc(a, b):
            def f(c):
                return Xt[:, c, a - 128:b - 128], bk("X", [c], a, b)
            return f

        def postnorm_residual(l, w, a, b, Ybuf, yname):
            n = b - a

            def ysrc(c):
                return Ybuf[:, c, a - 128:b - 128], bk(yname, [c], a, b)
            b2 = stats_rstd(ysrc, n)
            tms = {}
            for c in range(9):
                if c < 8:
                    tb = nbank()
                    while tb == b2 or tb in [v for k_, v in tms.items() if k_ >= c - 1]:
                        tb = nbank()
                    tms[c] = tb
                    DVE("scalar_tensor_tensor", bk(yname, [c], a, b) + psk(b2) + KC, psk(tb),
                        out=PSt[:, tb, 0:n], in0=Ybuf[:, c, a - 128:b - 128], scalar=cpc(B_G + (l * 4 + w) * 8 + c),
                        in1=PSt[:, b2, 0:n], op0=ALU.mult, op1=ALU.mult)
                if c >= 1:
                    cp_ = c - 1
                    xs = Xt[:, cp_, a - 128:b - 128]
                    DVE("tensor_tensor", psk(tms[cp_]) + bk("X", [cp_], a, b), bk("X", [cp_], a, b),
                        out=xs, in0=xs, in1=PSt[:, tms[cp_], 0:n], op=ALU.add)

        def fm_proj(slot, woff, nk, rhs_fn, a, b, rkeys):
            n = b - a
            bnk = nbank()
            for k in range(nk):
                PE("matmul", [("w", slot)] + rkeys, psk(bnk), inc=(k == nk - 1),
                   out=PSt[:, bnk, 0:n], lhsT=WRt[:, slot, woff + k * 128: woff + (k + 1) * 128],
                   rhs=rhs_fn(k), start=(k == 0), stop=(k == nk - 1))
            return bnk

        class _Stop(Exception):
            pass

        def chk(name):
            if debug and debug.get("_stop") == name:
                raise _Stop()

        def dump(name, ap, keys):
            if name in dbg_t:
                T.dma("sync", "dbg", dict(out=dbg_t[name], in_=ap), r=keys)

        def layer(it, l, KV0, KV1, C0, C1, gbase, after_p9=None):
            kv_tiles = tiles_of(KV0, KV1)
            c_tiles = tiles_of(C0, C1)
            kvb = list(range(KV0 // 128, KV1 // 128))
            cb_ = list(range(C0 // 128, C1 // 128))
            g = gbase

            WORDER = [3, 2, 0, 1]

            def gen_win(pjs):
                for pj in pjs:
                    slot = acquire_piece(g + WORDER.index(pj))
                    nchunks = 4 if pj < 3 else 1
                    for ci in range(nchunks):
                        zi = pj * 4 + ci
                        for (a, b) in kv_tiles:
                            bnk = fm_proj(slot, ci * 1024, 8, lambda k, a=a, b=b: Ht[:, k, a:b], a, b,
                                          bk("H", range(8), a, b))
                            evac_copy(Zc[zi][:, a:b], PSt[:, bnk, 0:b - a], psk(bnk), bk("Z", [zi], a, b))
                            yield
                    if pj == 3:
                        for blk in kvb:
                            t0 = blk * 128
                            inC = blk in cb_
                            ncol = 384 if inC else 128
                            bnk = nbank()
                            for k in range(8):
                                PE("matmul", [("w", slot)] + bk("H", range(8), t0, t0 + 128), psk(bnk), inc=(k == 7),
                                   out=PSt[:, bnk, 0:ncol], lhsT=Ht[:, k, t0:t0 + 128],
                                   rhs=WRt[:, slot, 1024 + k * 384: 1024 + k * 384 + ncol],
                                   start=(k == 0), stop=(k == 7))
                            ACT("activation", psk(bnk), [("VT", blk)], out=VT[:, blk, :], in_=PSt[:, bnk, 0:128], func=AF.Copy)
                            if inC:
                                s1 = nsm()
                                s2 = nsm()
                                DVE("bn_stats", psk(bnk), [("sm", s1)], out=SMt[:, s1, 0:6], in_=PSt[:, bnk, 128:384])
                                DVE("bn_aggr", [("sm", s1)], [("sm", s2)], out=SMt[:, s2, 0:2], in_=SMt[:, s1, 0:6])
                                ACT("activation", [("sm", s2)] + KC, [("sm", s2)], out=SMt[:, s2, 2:3], in_=SMt[:, s2, 1:2],
                                    func=AF.Sqrt, bias=epsc, scale=1.0)
                                DVE("reciprocal", [("sm", s2)], [("sm", s2)], out=SMt[:, s2, 3:4], in_=SMt[:, s2, 2:3])
                                DVE("tensor_scalar", psk(bnk) + [("sm", s2), ("sm", s2)], [("VS", blk)],
                                    out=VS[:, blk - 1, :], in0=PSt[:, bnk, 128:384], scalar1=SMt[:, s2, 0:1],
                                    scalar2=SMt[:, s2, 3:4], op0=ALU.subtract, op1=ALU.mult)
                            yield

            def gen_conv():
                for cc in range(2):
                    for (a, b) in c_tiles:
                        n = b - a
                        U = scr(12, BF16, n + 2)
                        DVE("tensor_tensor", bk("Z", [2 + cc, 4 + cc], a - 1, b + 1), scrk(12),
                            out=U, in0=Zc[2 + cc][:, a - 1:b + 1], in1=Zc[4 + cc][:, a - 1:b + 1], op=ALU.mult)
                        bnk = nbank()
                        for tap in range(3):
                            o = CB_DG + ((l * 3 + tap) * 2 + cc) * 128
                            PE("matmul", scrk(12) + KC, psk(bnk), inc=(tap == 2), out=PSt[:, bnk, 0:n],
                               lhsT=CB[:, o:o + 128], rhs=U[:, tap:tap + n], start=(tap == 0), stop=(tap == 2))
                        DVE("tensor_tensor", psk(bnk) + bk("Z", [cc], a, b), bk("BR", [cc], a, b),
                            out=BR[:, cc, a - 128:b - 128], in0=Zc[cc][:, a:b], in1=PSt[:, bnk, 0:n], op=ALU.mult)
                        yield

            def gen_pool():
                for cc in range(2):
                    zp = 6 + cc
                    for (a, b) in c_tiles:
                        n = b - a
                        T1 = scr(14, BF16)
                        T2 = scr(15, BF16)
                        T3 = scr(16, BF16)
                        T4 = scr(17, BF16)
                        PL = scr(13, BF16, n)
                        zk = bk("Z", [zp], a - 8, b + 8)
                        DVE("tensor_tensor", zk, scrk(14), out=T1[:, 0:n + 14], in0=Zc[zp][:, a - 8:b + 6],
                            in1=Zc[zp][:, a - 7:b + 7], op=ALU.add)
                        DVE("tensor_tensor", scrk(14), scrk(15), out=T2[:, 0:n + 12], in0=T1[:, 0:n + 12],
                            in1=T1[:, 2:n + 14], op=ALU.add)
                        if cc == 0:
                            sel = [(T1, 7, 14), (T2, 6, 15)]
                        else:
                            yield
                            DVE("tensor_tensor", scrk(15), scrk(16), out=T3[:, 0:n + 8], in0=T2[:, 0:n + 8],
                                in1=T2[:, 4:n + 12], op=ALU.add)
                            DVE("tensor_tensor", scrk(16), scrk(17), out=T4[:, 0:n], in0=T3[:, 0:n],
                                in1=T3[:, 8:n + 8], op=ALU.add)
                            sel = [(T3, 4, 16), (T4, 0, 17)]
                        yield
                        for hf in range(2):
                            Ts, off, si = sel[hf]
                            r0 = hf * 64
                            DVE("scalar_tensor_tensor", scrk(si) + bk("Z", [zp], a, b) + KC, scrk(13),
                                out=PL[r0:r0 + 64, :], in0=Ts[r0:r0 + 64, off:off + n],
                                scalar=CP[r0:r0 + 64, B_IW + cc:B_IW + cc + 1],
                                in1=Zc[zp][r0:r0 + 64, a:b], op0=ALU.mult, op1=ALU.subtract)
                            for side, e0 in ((0, 256), (1, 1272)):
                                if a <= e0 and e0 + 8 <= b:
                                    j0 = e0 - a
                                    tb = B_PT + ((it * 2 + side) * 2 + cc) * 8
                                    sm = nsm()
                                    DVE("tensor_tensor", scrk(si) + KC, [("sm", sm)],
                                        out=SMt[r0:r0 + 64, sm, :], in0=Ts[r0:r0 + 64, off + j0:off + j0 + 8],
                                        in1=CP[r0:r0 + 64, tb:tb + 8], op=ALU.mult)
                                    DVE("tensor_tensor", [("sm", sm)] + bk("Z", [zp], e0, e0 + 8), scrk(13),
                                        out=PL[r0:r0 + 64, j0:j0 + 8], in0=SMt[r0:r0 + 64, sm, :],
                                        in1=Zc[zp][r0:r0 + 64, e0:e0 + 8], op=ALU.subtract)
                        bnk = nbank()
                        PE("matmul", scrk(13) + KC, psk(bnk), out=PSt[:, bnk, 0:n],
                           lhsT=CB[:, CB_PW + (l * 2 + cc) * 128: CB_PW + (l * 2 + cc + 1) * 128],
                           rhs=PL, start=True, stop=True)
                        ACT("activation", psk(bnk) + KC, bk("BR", [2 + cc], a, b),
                            out=BR[:, 2 + cc, a - 128:b - 128], in_=PSt[:, bnk, 0:n], func=AF.Copy,
                            scale=cpc(B_PS + l * 2 + cc))
                        yield

            def gen_sg():
                for blk in cb_:
                    t0 = blk * 128
                    for cc in range(2):
                        bnk = nbank()
                        PE("matmul", [("VS", blk)] + KC, psk(bnk), out=PSt[:, bnk, 0:256],
                           lhsT=VS[:, blk - 1, cc * 128:(cc + 1) * 128],
                           rhs=CB[:, CB_SGW + (l * 2 + cc) * 256: CB_SGW + (l * 2 + cc + 1) * 256],
                           start=True, stop=True)
                        for hf in range(2):
                            r0 = hf * 64
                            tm = PSt[r0:r0 + 64, bnk, 256 + hf * 128:256 + (hf + 1) * 128]
                            DVE("scalar_tensor_tensor", psk(bnk) + KC, psk(bnk),
                                out=tm, in0=PSt[r0:r0 + 64, bnk, hf * 128:(hf + 1) * 128],
                                scalar=CP[r0:r0 + 64, B_SN + l * 2 + cc:B_SN + l * 2 + cc + 1],
                                in1=SGB[r0:r0 + 64, l, cc, :], op0=ALU.mult, op1=ALU.add)
                            DVE("tensor_tensor", psk(bnk) + bk("Z", [8 + cc], t0, t0 + 128), bk("BR", [4 + cc], t0, t0 + 128),
                                out=BR[r0:r0 + 64, 4 + cc, t0 - 128:t0], in0=Zc[8 + cc][r0:r0 + 64, t0:t0 + 128],
                                in1=tm, op=ALU.mult)
                        yield

            sinkv = CP[:, B_SK + l * 4:B_SK + l * 4 + 4]
            nsinkv = CP[:, B_NSK + l * 4:B_NSK + l * 4 + 4]

            def attn_unit(blk, hp, u):
                t0 = blk * 128
                k0 = t0 - 128
                sb0 = (u % 6) * 2
                Pb = SCRt[:, sb0 * SCRB:sb0 * SCRB + 1536].bitcast(BF16).rearrange("p (h k) -> p h k", h=2)
                Pk = scrk(sb0)
                PTs = SCRt[:, (sb0 + 1) * SCRB:(sb0 + 1) * SCRB + 1536].bitcast(BF16).rearrange("p (h k) -> p h k", h=2)
                PTk = scrk(sb0 + 1)
                hds = (2 * hp, 2 * hp + 1)
                r0 = hp * 64
                b0 = try_hold(2)
                while b0 is None:
                    yield
                    b0 = try_hold(2)
                Sk = psk(b0) + psk(b0 + 1)
                for i2, hd in enumerate(hds):
                    qc = 10 + (hd % 2)
                    PE("matmul", bk("Z", [qc], t0, t0 + 128) + bk("Z", [12], k0, k0 + 384), psk(b0 + i2), inc=False,
                       out=PSt[:, b0 + i2, 0:384], lhsT=Zc[qc][r0:r0 + 64, t0:t0 + 128],
                       rhs=Zc[12][r0:r0 + 64, k0:k0 + 384], start=True, stop=False)
                    PE("matmul", KC, psk(b0 + i2), out=PSt[:, b0 + i2, 0:384], lhsT=ident, rhs=BIAS[:, hd, :],
                       start=False, stop=True)
                S2 = PSt[:, b0:b0 + 2, 0:384]
                if blk == 2:
                    DVE("tensor_scalar", Sk + KC, Sk, out=PSt[:, b0:b0 + 2, 0:128], in0=PSt[:, b0:b0 + 2, 0:128],
                        scalar1=cpc(B_FL + it * 4 + 0), scalar2=None, op0=ALU.add)
                if blk == 9:
                    DVE("tensor_scalar", Sk + KC, Sk, out=PSt[:, b0:b0 + 2, 256:384], in0=PSt[:, b0:b0 + 2, 256:384],
                        scalar1=cpc(B_FL + it * 4 + 1), scalar2=None, op0=ALU.add)
                yield
                s1 = nsm()
                s2 = nsm()
                s3 = nsm()
                s4 = nsm()
                SMr = SMt[:, s1, :]
                SMq = SMt[:, s2, :]
                SMs = SMt[:, s3, :]
                SMd = SMt[:, s4, :]
                sk2 = sinkv[:, 2 * hp:2 * hp + 2]
                DVE("tensor_reduce", Sk, [("sm", s1)], out=SMr[:, 0:2], in_=S2, axis=AX.X, op=ALU.max, negate=True)
                DVE("scalar_tensor_tensor", [("sm", s1), ("nsk",)] + KC, [("sm", s2)], out=SMq[:, 0:2], in0=SMr[:, 0:2],
                    scalar=0.125, in1=nsinkv[:, 2 * hp:2 * hp + 2], op0=ALU.mult, op1=ALU.min)
                yield
                for i2 in range(2):
                    ACT("activation", psk(b0 + i2) + [("sm", s2)], Pk + [("sm", s3)],
                        out=Pb[:, i2, :], in_=PSt[:, b0 + i2, 0:384], func=AF.Exp, bias=SMq[:, i2:i2 + 1], scale=0.125,
                        accum_out=SMs[:, i2:i2 + 1])
                    ACT("activation", [("sm", s2)] + KC, [("sm", s3)], out=SMs[:, 4 + i2:5 + i2],
                        in_=sinkv[:, 2 * hp + i2:2 * hp + i2 + 1], func=AF.Exp, bias=SMq[:, i2:i2 + 1], scale=1.0)
                release(b0, 2)
                yield
                DVE("tensor_tensor", [("sm", s3), ("sm", s3), ("sm", s3), ("sm", s3)], [("sm", s4)],
                    out=SMd[:, 0:2], in0=SMs[:, 0:2], in1=SMs[:, 4:6], op=ALU.add)
                DVE("reciprocal", [("sm", s4)], [("sm", s4)], out=SMd[:, 4:6], in_=SMd[:, 0:2])
                yield
                for i2 in range(2):
                    ACT("activation", Pk + [("sm", s4)], Pk, out=Pb[:, i2, :], in_=Pb[:, i2, :], func=AF.Copy,
                        scale=SMd[:, 4 + i2:5 + i2])
                yield
                tbk = try_hold(1)
                while tbk is None:
                    yield
                    tbk = try_hold(1)
                ptv = PSt[:, tbk, :].bitcast(BF16)
                for i2 in range(2):
                    for j in range(3):
                        c0 = i2 * 384 + j * 128
                        PE("transpose", Pk + KC, psk(tbk), out=ptv[:, c0:c0 + 128],
                           in_=Pb[:, i2, j * 128:(j + 1) * 128], identity=ident)
                yield
                evac_copy(PTs.rearrange("p h k -> p (h k)"), ptv[:, 0:768], psk(tbk), PTk)
                release(tbk)
                yield
                ob = try_hold(1)
                while ob is None:
                    yield
                    ob = try_hold(1)
                for i2 in range(2):
                    for j in range(3):
                        PE("matmul", PTk + [("VT", blk - 1 + j)], psk(ob), inc=(j == 2),
                           out=PSt[i2 * 64:i2 * 64 + 64, ob, 0:128],
                           lhsT=VT[:, blk - 1 + j, hp * 64:(hp + 1) * 64], rhs=PTs[:, i2, j * 128:(j + 1) * 128],
                           start=(j == 0), stop=(j == 2))
                yield
                ACT("activation", psk(ob), bk("BR", [6 + hp], t0, t0 + 128),
                    out=BR[:, 6 + hp, t0 - 128:t0], in_=PSt[:, ob, 0:128], func=AF.Copy)
                release(ob)

            for _ in gen_win([3, 2]):
                pass
            import itertools
            others = itertools.chain(gen_win([0, 1]), gen_sg(), gen_conv(), gen_pool())
            others_done = False
            units = [(blk, hp) for blk in cb_ for hp in range(2)]
            active = []
            nxt = 0
            while nxt < len(units) or active or not others_done:
                if nxt < len(units) and len(active) < 6:
                    active.append(attn_unit(units[nxt][0], units[nxt][1], nxt))
                    nxt += 1
                for gen in list(active):
                    try:
                        next(gen)
                    except StopIteration:
                        active.remove(gen)
                for _ in range(2):
                    if not others_done:
                        try:
                            next(others)
                        except StopIteration:
                            others_done = True
            g += 4

            if it == 0 and ("BR%d" % l) in dbg_t:
                dump("BR%d" % l, BR, bk("BR", range(8), 128, 1408))

            chk("p3d")
            T.barrier(["pe", "act", "dve"])
            GB = [0, 1, 2]
            PB = [3, 4, 5]
            MB = [6, 7]
            gctr = [0]
            mctr = [0]
            pend = []

            def flush_pend(keep):
                while len(pend) > keep:
                    pend.pop(0)()
            for c in range(8):
                slots2 = [acquire_piece(g, hold_from=g), acquire_piece(g + 1, hold_from=g)]
                g += 2
                for (a, b) in c_tiles:
                    n = b - a
                    mb = MB[mctr[0] % 2]
                    mctr[0] += 1
                    for i in range(4):
                        gslot = slots2[i // 2]
                        i2 = i % 2
                        gb = GB[gctr[0] % 3]
                        pb = PB[gctr[0] % 3]
                        gi = gctr[0] % 2
                        gctr[0] += 1
                        for k in range(8):
                            wo = (i2 * 10 + k) * 128
                            PE("matmul", [("w", gslot)] + bk("H", range(8), a, b), psk(gb), inc=(k == 7),
                               out=PSt[:, gb, 0:n], lhsT=WRt[:, gslot, wo:wo + 128],
                               rhs=Ht[:, k, a:b], start=(k == 0), stop=(k == 7))
                        for kk in range(2):
                            wo = (i2 * 10 + 8 + kk) * 128
                            PE("matmul", [("w", gslot)] + bk("BR", [2 * i, 2 * i + 1], a, b), psk(pb), inc=(kk == 1),
                               out=PSt[:, pb, 0:n], lhsT=WRt[:, gslot, wo:wo + 128],
                               rhs=BR[:, 2 * i + kk, a - 128:b - 128], start=(kk == 0), stop=(kk == 1))
                        gsb = scr(2 * gi, F32, n, nslots=2)
                        tsb = scr(4 + gi, BF16, n)
                        ACT("activation", psk(gb) + KC, scrk(2 * gi, 2), out=gsb, in_=PSt[:, gb, 0:n], func=AF.Sigmoid,
                            bias=cpc(B_BG + (l * 4 + i) * 8 + c), scale=1.0)
                        DVE("tensor_tensor", scrk(2 * gi, 2) + psk(pb), scrk(4 + gi), out=tsb, in0=gsb, in1=PSt[:, pb, 0:n],
                            op=ALU.mult)

                        def acc(i=i, mb=mb, tsb=tsb, n=n, gi=gi, a=a, b=b, c=c):
                            PE("matmul", scrk(4 + gi) + KC, psk(mb), out=PSt[:, mb, 0:n], lhsT=ident, rhs=tsb,
                               start=(i == 0), stop=(i == 3))
                            if i == 3:
                                ACT("activation", psk(mb), bk("M", [c], a, b), out=Mv[:, c, a - 128:b - 128],
                                    in_=PSt[:, mb, 0:n], func=AF.Copy)
                        pend.append(acc)
                        flush_pend(1)
            flush_pend(0)

            if l == 1 and it + 1 < n_items:
                load_halo(it + 1)
            chk("p4")
            wslots = [acquire_piece(g, hold_from=g), acquire_piece(g + 1, hold_from=g)]
            g += 2
            for (a, b) in c_tiles:
                for op_ in range(2):
                    for c4 in range(4):
                        c = op_ * 4 + c4
                        bnk = fm_proj(wslots[op_], c4 * 1024, 8, lambda k, a=a, b=b: Mv[:, k, a - 128:b - 128], a, b,
                                      bk("M", range(8), a, b))
                        evac_copy(Yv[:, c, a - 128:b - 128], PSt[:, bnk, 0:b - a], psk(bnk), bk("Y", [c], a, b))
                postnorm_residual(l, 1, a, b, Yv, "Y")
                prenorm(l, 2, a, b, xsrc(a, b))
            chk("p6")
            for half in range(2):
                for pp in range(6):
                    slot = acquire_piece(g)
                    g += 1
                    for jj in range(2 if pp < 5 else 1):
                        j = pp * 2 + jj
                        for (a, b) in c_tiles:
                            n = b - a
                            ba = fm_proj(slot, (jj * 2 + 0) * 1024, 8, lambda k, a=a, b=b: Ht[:, k, a:b], a, b,
                                         bk("H", range(8), a, b))
                            bg = fm_proj(slot, (jj * 2 + 1) * 1024, 8, lambda k, a=a, b=b: Ht[:, k, a:b], a, b,
                                         bk("H", range(8), a, b))
                            si = j % 2
                            ssb = scr(2 * si, F32, n, nslots=2)
                            ACT("activation", psk(ba), scrk(2 * si, 2), out=ssb, in_=PSt[:, ba, 0:n], func=AF.Silu)
                            DVE("tensor_tensor", scrk(2 * si, 2) + psk(bg), bk("HID", [j], a, b),
                                out=HID[:, j, a - 128:b - 128], in0=ssb, in1=PSt[:, bg, 0:n], op=ALU.mult)
                for pp in range(4):
                    slot = acquire_piece(g)
                    g += 1
                    for c2 in range(2):
                        c = pp * 2 + c2
                        for (a, b) in c_tiles:
                            n = b - a
                            bnk = fm_proj(slot, c2 * 1408, 11, lambda k, a=a, b=b: HID[:, k, a - 128:b - 128], a, b,
                                          bk("HID", range(11), a, b))
                            y2s = Y2[:, c, a - 128:b - 128]
                            if half == 0:
                                evac_copy(y2s, PSt[:, bnk, 0:n], psk(bnk), bk("Y2", [c], a, b))
                            else:
                                DVE("tensor_tensor", psk(bnk) + bk("Y2", [c], a, b), bk("Y2", [c], a, b),
                                    out=y2s, in0=y2s, in1=PSt[:, bnk, 0:n], op=ALU.add)
            chk("p8")
            for ti, (a, b) in enumerate(c_tiles):
                postnorm_residual(l, 3, a, b, Y2, "Y2")
                if after_p9 is not None:
                    after_p9(ti, a, b)
            assert g == gbase + NPIECE
            if it == 0:
                dump("X%d" % l, Xt[:], bk("X", range(8), 128, 1408))

        epsc = EPS
        XH = SCRt[:, 12 * SCRB:12 * SCRB + 8192].bitcast(F32).rearrange("p (c t) -> p c t", c=8)
        def load_halo(it_):
            T.dma("sync", "xh", dict(out=XH[:, :, 0:128], in_=xin[it_, :, :, 0:128]), w=scrk(12, 6))
            T.dma("sync", "xh", dict(out=XH[:, :, 128:256], in_=xin[it_, :, :, 1408:1536]), w=scrk(12, 6))

        def main_body():
            for it in range(n_items):
                for ti, (a, b) in enumerate(tiles_of(128, 1408)):
                    T.dma("sync", "xa%d" % ti, dict(out=Xt[:, :, a - 128:b - 128], in_=xin[it, :, :, a:b]),
                          w=bk("X", range(8), a, b))
                if it == 0:
                    load_halo(0)
                gbase = (it * 2) * NPIECE

                def hsrc0(c):
                    return XH[:, c, 0:128], scrk(12, 6)

                def hsrc1(c):
                    return XH[:, c, 128:256], scrk(12, 6)
                chk("p0")
                tl = tiles_of(128, 1408)
                prenorm(0, 0, 0, 128, hsrc0)
                prenorm(0, 0, tl[0][0], tl[0][1], xsrc(*tl[0]))
                prenorm(0, 0, tl[1][0], tl[1][1], xsrc(*tl[1]))
                prenorm(0, 0, tl[2][0], tl[2][1], xsrc(*tl[2]))
                prenorm(0, 0, 1408, 1536, hsrc1)
                chk("p1")
                layer(it, 0, 0, 1536, 128, 1408, gbase)
                for (a, b) in tiles_of(128, 1408):
                    prenorm(1, 0, a, b, xsrc(a, b))
                hk = bk("H", range(8), 128, 256)
                DVE("tensor_scalar", hk + KC, hk, out=Ht[:, :, 128:256], in0=Ht[:, :, 128:256],
                    scalar1=cpc(B_FL + it * 4 + 2), scalar2=None, op0=ALU.mult)
                hk = bk("H", range(8), 1280, 1408)
                DVE("tensor_scalar", hk + KC, hk, out=Ht[:, :, 1280:1408], in0=Ht[:, :, 1280:1408],
                    scalar1=cpc(B_FL + it * 4 + 3), scalar2=None, op0=ALU.mult)
                def store_tile(ti, a, b, it=it):
                    T.dma("sync", "out%d" % ti, dict(out=yout[it, :, :, a - 256:b - 256], in_=Xt[:, :, a - 128:b - 128]),
                          r=bk("X", range(8), a, b))
                layer(it, 1, 128, 1408, 256, 1280, gbase + NPIECE, after_p9=store_tile)
        try:
            main_body()
        except _Stop:
            pass
        T.barrier(["pe", "act", "dve"])
        T.final_wait("sync", list(T.dcount.keys()))

        with nc.Block() as block:
            @block.sync
            def _(h):
                T.emit("sync", h)

            @block.gpsimd
            def _(h):
                T.emit("gq", h)

            @block.tensor
            def _(h):
                T.emit("pe", h)

            @block.scalar
            def _(h):
                T.emit("act", h)

            @block.vector
            def _(h):
                T.emit("dve", h)
    return nc


def pack_weights(w_in, w_branch, w_gate, w_out, w_ffn_in, w_ffn_out):
    wst = np.zeros((2, NPIECE, 128, PIECE_E), np.float32)

    def fm(Wcols):
        K = Wcols.shape[0] // 128
        return Wcols.reshape(K, 128, 128).transpose(1, 0, 2)
    qperm = np.concatenate([np.arange(1536, 1600), np.arange(1664, 1728), np.arange(1600, 1664), np.arange(1728, 1792)])
    fmcols = [np.arange(0, 128), np.arange(128, 256), np.arange(256, 384), np.arange(384, 512),
              np.arange(512, 640), np.arange(640, 768), np.arange(768, 896), np.arange(896, 1024),
              np.arange(1024, 1152), np.arange(1152, 1280), qperm[:128], qperm[128:], np.arange(1792, 1920)]
    tmcols = np.concatenate([np.arange(1920, 2048), np.arange(1280, 1536)])
    for l in range(2):
        wp = []
        for pj in range(3):
            parts = [fm(w_in[l][:, fmcols[pj * 4 + ci]]) for ci in range(4)]
            wp.append(np.stack(parts, axis=1).reshape(128, 4096))
        kpart = fm(w_in[l][:, fmcols[12]]).reshape(128, 1024)
        tm = w_in[l][:, tmcols].reshape(8, 128, 384).transpose(1, 0, 2).reshape(128, 3072)
        wp.append(np.concatenate([kpart, tm], axis=1))
        for j, pj in enumerate([3, 2, 0, 1]):
            wst[l, j] = wp[pj]
        j = 4
        for c in range(8):
            for ih in range(2):
                arr = np.zeros((128, 2, 10, 128), np.float32)
                for i2 in range(2):
                    i = ih * 2 + i2
                    arr[:, i2, 0:8] = fm(w_gate[l, i][:, c * 128:(c + 1) * 128])
                    arr[:, i2, 8:10] = fm(w_branch[l, i][:, c * 128:(c + 1) * 128])
                wst[l, j, :, :2560] = arr.reshape(128, 2560)
                j += 1
        for op_ in range(2):
            arr = np.stack([fm(w_out[l][:, (op_ * 4 + c4) * 128:(op_ * 4 + c4 + 1) * 128]) for c4 in range(4)], axis=1)
            wst[l, j] = arr.reshape(128, 4096)
            j += 1
        for half in range(2):
            for pp in range(6):
                njj = 2 if pp < 5 else 1
                arr = np.zeros((128, njj, 2, 8, 128), np.float32)
                for jj in range(njj):
                    hj = half * 11 + pp * 2 + jj
                    for ag in range(2):
                        arr[:, jj, ag] = fm(w_ffn_in[l][:, ag * DFF + hj * 128: ag * DFF + (hj + 1) * 128])
                wst[l, j, :, :njj * 2048] = arr.reshape(128, njj * 2048)
                j += 1
            for pp in range(4):
                arr = np.zeros((128, 2, 11, 128), np.float32)
                for c2 in range(2):
                    c = pp * 2 + c2
                    arr[:, c2] = fm(w_ffn_out[l][half * 11 * 128:(half + 1) * 11 * 128, c * 128:(c + 1) * 128])
                wst[l, j, :, :2816] = arr.reshape(128, 2816)
                j += 1
        assert j == NPIECE
    return wst


def item_list(core):
    items = []
    for a in (0, 8, 16, 24):
        items.append(("p", core, a, 32))
    for a in (16 * core, 16 * core + 8):
        items.append(("s", 0, a, 128))
    return items


def build_inputs(inp):
    f32 = np.float32
    x_prompt = np.asarray(inp["x_prompt"], f32)
    x_sample = np.asarray(inp["x_sample"], f32)
    wst = pack_weights(np.asarray(inp["w_in"], f32), np.asarray(inp["w_branch"], f32), np.asarray(inp["w_gate"], f32),
                       np.asarray(inp["w_out"], f32), np.asarray(inp["w_ffn_in"], f32), np.asarray(inp["w_ffn_out"], f32))
    cp0 = np.zeros((128, NCP), f32)
    gains = [inp["norm_mix_pre"], inp["norm_mix_post"], inp["norm_ffn_pre"], inp["norm_ffn_post"]]
    for l in range(2):
        for w in range(4):
            cp0[:, B_G + (l * 4 + w) * 8: B_G + (l * 4 + w) * 8 + 8] = np.asarray(gains[w], f32)[l].reshape(8, 128).T
        for i in range(4):
            cp0[:, B_BG + (l * 4 + i) * 8: B_BG + (l * 4 + i) * 8 + 8] = np.asarray(inp["b_gate"], f32)[l, i].reshape(8, 128).T
        for tap in range(3):
            cp0[:, B_CW + (l * 3 + tap) * 2: B_CW + (l * 3 + tap) * 2 + 2] = np.asarray(inp["conv_w"], f32)[l, tap].reshape(2, 128).T
        cp0[:, B_PS + l * 2: B_PS + l * 2 + 2] = np.asarray(inp["pool_scale"], f32)[l].reshape(2, 128).T
        cp0[:, B_SN + l * 2: B_SN + l * 2 + 2] = np.asarray(inp["sg_norm"], f32)[l].reshape(2, 128).T
        cp0[:, B_SK + l * 4: B_SK + l * 4 + 4] = np.asarray(inp["attn_sink"], f32)[l][None, :]
    wins = np.zeros((128, 2), np.int64)
    wins[:64, 0] = 2
    wins[64:, 0] = 4
    wins[:64, 1] = 8
    wins[64:, 1] = 16
    cp0[:, B_IW:B_IW + 2] = (1.0 / wins).astype(f32)
    cb = np.zeros((128, NCB), f32)
    sg_w = np.asarray(inp["sg_w"], f32)
    pool_w = np.asarray(inp["pool_w"], f32)
    for l in range(2):
        for cc in range(2):
            for jg in range(2):
                o = CB_SGW + (l * 2 + cc) * 256 + jg * 128
                cb[:, o:o + 128] = sg_w[l, 2 * cc + jg].T
            blk = np.zeros((128, 128), f32)
            blk[:64, :64] = pool_w[l, 2 * cc]
            blk[64:, 64:] = pool_w[l, 2 * cc + 1]
            cb[:, CB_PW + (l * 2 + cc) * 128: CB_PW + (l * 2 + cc + 1) * 128] = blk
    conv_w = np.asarray(inp["conv_w"], f32)
    for l in range(2):
        for tap in range(3):
            for cc in range(2):
                o = CB_DG + ((l * 3 + tap) * 2 + cc) * 128
                dg = np.zeros((128, 128), f32)
                dg[np.arange(128), np.arange(128)] = conv_w[l, tap, cc * 128:(cc + 1) * 128]
                cb[:, o:o + 128] = dg
    cb[:, CB_ID:CB_ID + 128] = np.eye(128, dtype=f32)
    cb[:, CB_ON:CB_ON + 128] = 1.0 / 1024.0
    qi = np.arange(128)[:, None]
    kj = np.arange(384)[None, :]
    rel = np.abs(qi - kj + 128).astype(f32)
    slopes = np.exp2(-8.0 * (np.arange(4, dtype=f32) + 1.0) / 4.0).astype(f32)
    abias = np.where(rel[:, None, :] <= 128, -8.0 * slopes[None, :, None] * rel[:, None, :], f32(-8e30)).astype(f32)
    abias = np.ascontiguousarray(abias.reshape(128, 4 * 384))
    sg_b = np.asarray(inp["sg_b"], f32)
    sgb = np.zeros((128, 2, 2, 128), f32)
    for l in range(2):
        for cc in range(2):
            sgb[:64, l, cc, :] = sg_b[l, 2 * cc][None, :]
            sgb[64:, l, cc, :] = sg_b[l, 2 * cc + 1][None, :]
    sgb = sgb.reshape(128, 512)

    in_maps = []
    for core in range(NCORES):
        items = item_list(core)
        xin = np.zeros((NITEM, 128, 8, NTOK_IN), f32)
        cp = cp0.copy()
        for it, (kind, sidx, a, nblk) in enumerate(items):
            src = x_prompt[sidx] if kind == "p" else x_sample[0]
            S = src.shape[0]
            t0 = (a - 2) * 128
            lo = max(t0, 0)
            hi = min(t0 + NTOK_IN, S)
            seg = src[lo:hi]
            xin[it, :, :, lo - t0:hi - t0] = seg.reshape(-1, 8, 128).transpose(2, 1, 0)
            lvalid = a > 0
            rvalid = (a + 8) < nblk
            cp[:, B_FL + it * 4 + 0] = 0.0 if lvalid else -1e30
            cp[:, B_FL + it * 4 + 1] = 0.0 if rvalid else -1e30
            cp[:, B_FL + it * 4 + 2] = 1.0 if lvalid else 0.0
            cp[:, B_FL + it * 4 + 3] = 1.0 if rvalid else 0.0
            for cc in range(2):
                w = wins[:, cc]
                for jx in range(8):
                    cl = w if lvalid else (jx + w // 2 - np.maximum(jx - w // 2, 0))
                    cr = w if rvalid else (np.minimum(w // 2, 8 - jx) + w // 2)
                    cp[:, B_PT + ((it * 2 + 0) * 2 + cc) * 8 + jx] = (1.0 / cl).astype(f32)
                    cp[:, B_PT + ((it * 2 + 1) * 2 + cc) * 8 + jx] = (1.0 / cr).astype(f32)
        in_maps.append({"xin": xin, "wst": wst, "cp": cp, "cb": cb, "abias": abias, "sgb": sgb})
    return in_maps


def assemble(results):
    yp = np.zeros((8, SEQ_P, D), np.float32)
    ys = np.zeros((1, SEQ_S, D), np.float32)
    for core in range(NCORES):
        y = results[core]["yout"]
        for it, (kind, sidx, a, nblk) in enumerate(item_list(core)):
            blk = y[it].transpose(2, 1, 0).reshape(NOUT, D)
            if kind == "p":
                yp[sidx, a * 128:a * 128 + NOUT] = blk
            else:
                ys[0, a * 128:a * 128 + NOUT] = blk
    return yp, ys


_DEBUG = None
_LAST = {}


def kernel(**inputs):
    in_maps = build_inputs(inputs)
    nc = build_program(_DEBUG)
    res = run_bass_kernel_spmd(nc, in_maps, core_ids=list(range(NCORES)))
    _LAST["res"] = res
    yp, ys = assemble(res.results)
    return (yp, ys)
```

```python
import numpy as np
import concourse.bass as bass
import concourse.mybir as mybir
from concourse.bass_utils import run_bass_kernel_spmd

F32 = mybir.dt.float32
BF16 = mybir.dt.bfloat16
U8 = mybir.dt.uint8
AF = mybir.ActivationFunctionType
ALU = mybir.AluOpType
AX = mybir.AxisListType

NCORES = 8
D = 1024
SEQ_P = 4096
SEQ_S = 16384
NITEM = 6
NTOK_IN = 1536
NX = 1280
NOUT = 1024
EPS = 1e-6
DFF = 2816
NPIECE = 42
PIECE_E = 4096
NSLOT = 4

B_G = 0
B_BG = 64
B_CW = 128
B_PS = 140
B_SN = 144
B_IW = 148
B_SK = 150
B_FL = 158
B_PT = 182
B_NSK = 374
NCP = 382
CB_SGW = 0
CB_PW = 1024
CB_ID = 1536
CB_ON = 1664
CB_DG = 1792
NCB = 3328


def piece_sizes():
    s = [4096, 4096, 4096, 4096]
    s += [2560] * 16
    s += [4096, 4096]
    for _ in range(2):
        s += [4096] * 5 + [2048]
        s += [2816] * 4
    assert len(s) == NPIECE
    return s


class Tracker:
    def __init__(self):
        self.streams = {}
        self.sems = {}
        self.count = {}
        self.known = {}
        self.lastw = {}
        self.readers = {}
        self.semh = {}
        self.dcount = {}

    def add_engine(self, name, sem=None):
        self.streams[name] = []
        self.known[name] = {}
        if sem is not None:
            self.sems[name] = sem
            self.semh[name] = sem
            self.count[name] = 0

    def add_dma_sem(self, name, sem):
        self.semh[name] = sem
        self.dcount[name] = 0

    def _waits(self, eng, r, w):
        need = {}

        def req(ev, same_ok):
            sn, val, en = ev
            if en == eng and same_ok and eng == "pe":
                return
            if need.get(sn, 0) < val:
                need[sn] = val
        for k in r:
            ev = self.lastw.get(k)
            if ev is not None:
                req(ev, False)
        for k in w:
            ev = self.lastw.get(k)
            if ev is not None:
                req(ev, True)
            for ev in self.readers.get(k, {}).values():
                req(ev, True)
        kn = self.known[eng]
        out = []
        for sn, val in need.items():
            if kn.get(sn, 0) < val:
                kn[sn] = val
                out.append((sn, val))
        return out

    def _commit(self, ev, r, w):
        for k in w:
            self.lastw[k] = ev
            self.readers[k] = {}
        for k in r:
            self.readers.setdefault(k, {})[ev[0]] = ev

    def op(self, eng, name, kw, r=(), w=(), inc=True):
        w = list(w) + [k for k in r if k[0] == "ps"]
        waits = self._waits(eng, r, w)
        if inc:
            self.count[eng] += 1
            ev = (eng, self.count[eng], eng)
        else:
            ev = (eng, self.count[eng] + 1, eng)
        self.streams[eng].append((waits, name, kw, self.sems[eng] if inc else None, 1))
        self._commit(ev, r, w)

    def dma(self, q, dsem, kw, r=(), w=()):
        waits = self._waits(q, r, w)
        self.dcount[dsem] += 16
        ev = (dsem, self.dcount[dsem], "dma:" + dsem)
        self.streams[q].append((waits, "dma_start", kw, self.semh[dsem], 16))
        self._commit(ev, r, w)

    def barrier(self, engs):
        for e in engs:
            waits = []
            for o in engs:
                if o == e:
                    continue
                val = self.count[o]
                if self.known[e].get(o, 0) < val:
                    self.known[e][o] = val
                    waits.append((o, val))
            self.streams[e].append((waits, None, None, None, 0))

    def final_wait(self, q, semnames):
        vals = [(sn, self.dcount[sn]) for sn in semnames if self.dcount[sn] > 0]
        self.streams[q].append((vals, None, None, None, 0))

    def emit(self, eng, h):
        semh = self.semh
        for waits, name, kw, isem, amt in self.streams[eng]:
            for sn, val in waits:
                h.wait_ge(semh[sn], val)
            if name is not None:
                ins = getattr(h, name)(**kw)
                if isem is not None:
                    ins.then_inc(isem, amt)


def tiles_of(t0, t1, step=512):
    out = []
    a = t0
    while a < t1:
        b = min(a + step, t1)
        out.append((a, b))
        a = b
    return out


def build_program(debug=None):
    nc = bass.Bass("TRN2", target_bir_lowering=False)
    psz = piece_sizes()
    xin = nc.dram_tensor("xin", [NITEM, 128, 8, NTOK_IN], F32, kind="ExternalInput").ap()
    wst = nc.dram_tensor("wst", [2, NPIECE, 128, PIECE_E], F32, kind="ExternalInput").ap()
    cpd = nc.dram_tensor("cp", [128, NCP], F32, kind="ExternalInput").ap()
    cbd = nc.dram_tensor("cb", [128, NCB], F32, kind="ExternalInput").ap()
    biasd = nc.dram_tensor("abias", [128, 4 * 384], F32, kind="ExternalInput").ap()
    sgbd = nc.dram_tensor("sgb", [128, 2 * 2 * 128], F32, kind="ExternalInput").ap()
    yout = nc.dram_tensor("yout", [NITEM, 128, 8, NOUT], F32, kind="ExternalOutput").ap()
    dbg_t = {}
    if debug:
        for name, spec in debug.items():
            if name.startswith("_"):
                continue
            dbg_t[name] = nc.dram_tensor("dbg_" + name, list(spec[0]), spec[1], kind="ExternalOutput").ap()
    n_items = NITEM if not (debug and debug.get("_items")) else 1

    from contextlib import ExitStack
    with ExitStack() as es:
        def sb(name, shape, dt):
            return es.enter_context(nc.sbuf_tensor(name, shape, dt))

        Xt = sb("X", [128, 8, NX], F32)
        Ht = sb("H", [128, 8, NTOK_IN], BF16)
        RSZ = 69632
        Rt = sb("R", [128, RSZ], U8)
        WRt = sb("WR", [128, NSLOT, PIECE_E], BF16)
        CP = sb("CP", [128, NCP], F32)
        CB = sb("CB", [128, NCB], BF16)
        BIAS = sb("BIAS", [128, 4, 384], BF16)
        SGB = sb("SGB", [128, 2, 2, 128], F32)
        NSCR = 18
        SCRB = 1536
        SCRt = sb("SCR", [128, NSCR * SCRB], U8)
        SMt = sb("SM", [128, 32, 8], F32)
        PSt = es.enter_context(nc.psum_tensor("PS", [128, 8, 512], F32))

        def sem(name):
            return es.enter_context(nc.semaphore(name))

        T = Tracker()
        for en in ("pe", "act", "dve"):
            T.add_engine(en, sem("s_" + en))
        T.add_engine("sync")
        T.add_engine("gq")
        for s_ in range(NSLOT):
            T.add_dma_sem("w%d" % s_, sem("s_w%d" % s_))
        for n_ in ("xa0", "xa1", "xa2", "xh", "out0", "out1", "cst", "cstg", "dbg"):
            T.add_dma_sem(n_, sem("s_" + n_))

        def PE(name, r, w, inc=True, **kw):
            T.op("pe", name, kw, r, w, inc)

        def ACT(name, r, w, **kw):
            T.op("act", name, kw, r, w)

        def DVE(name, r, w, **kw):
            T.op("dve", name, kw, r, w)

        def rview(off, nbytes, dt, pat=None, **kw):
            v = Rt[:, off:off + nbytes].bitcast(dt)
            if pat:
                v = v.rearrange(pat, **kw)
            return v
        Zc = {}
        Zc[12] = rview(0, 3072, BF16)
        VT = rview(3072, 3072, BF16, "p (b d) -> p b d", b=12)
        VS = rview(6144, 5120, BF16, "p (b d) -> p b d", b=10)
        for zi_ in range(8, 12):
            Zc[zi_] = rview(11264 + (zi_ - 8) * 3072, 3072, BF16)
        for zi_ in range(8):
            Zc[zi_] = rview(23552 + zi_ * 3072, 3072, BF16)
        BR = rview(48128, 20480, BF16, "p (c t) -> p c t", c=8)
        Mv = rview(0, 20480, BF16, "p (c t) -> p c t", c=8)
        Yv = rview(20480, 40960, F32, "p (c t) -> p c t", c=8)
        HID = rview(0, 28160, BF16, "p (c t) -> p c t", c=11)
        Y2 = rview(28160, 40960, F32, "p (c t) -> p c t", c=8)

        def scr(i, dt, n=None, nslots=1):
            v = SCRt[:, i * SCRB:(i + nslots) * SCRB].bitcast(dt)
            if n is not None:
                v = v[:, 0:n]
            return v

        def scrk(i, nslots=1):
            return [("scr", i + j) for j in range(nslots)]

        def bk(name, chunks, t0, t1):
            return [(name, c, b) for c in chunks for b in range(t0 // 128, (t1 + 127) // 128)]

        def cpc(j):
            return CP[:, j:j + 1]

        KC = [("const",), ("constb",)]
        ident = CB[:, CB_ID:CB_ID + 128]
        onesm = CB[:, CB_ON:CB_ON + 128]

        bank_ctr = [0]

        held = set()

        def nbank():
            for _ in range(8):
                b = bank_ctr[0] % 8
                bank_ctr[0] += 1
                if b not in held:
                    return b
            raise RuntimeError("no free PSUM bank")

        def try_hold(n):
            if len(held) + n > 6:
                return None
            step = 2 if n == 2 else 1
            for b in range(0, 8, step):
                if all((b + i) not in held for i in range(n)):
                    for i in range(n):
                        held.add(b + i)
                    return b
            return None

        def release(b, n=1):
            for i in range(n):
                held.discard(b + i)

        def psk(b):
            return [("ps", b)]

        sm_ctr = [0]

        def nsm():
            i = sm_ctr[0] % 32
            sm_ctr[0] += 1
            return i

        alt = [0]

        def evac_copy(out_ap, in_ap, r, w):
            alt[0] += 1
            if alt[0] % 2 == 0:
                ACT("activation", r, w, out=out_ap, in_=in_ap, func=AF.Copy)
            else:
                DVE("tensor_copy", r, w, out=out_ap, in_=in_ap)

        T.dma("sync", "cst", dict(out=CP[:], in_=cpd[:, :]), w=[("const",)])
        T.dma("sync", "cst", dict(out=SGB[:].rearrange("p l c t -> p (l c t)"), in_=sgbd[:, :]), w=[("const",)])
        T.dma("gq", "cstg", dict(out=CB[:, 0:1792], in_=cbd[:, 0:1792]), w=[("constb",)])
        T.dma("gq", "cstg", dict(out=CB[:, 1792:NCB], in_=cbd[:, 1792:NCB]), w=[("constb",)])
        T.dma("gq", "cstg", dict(out=BIAS[:].rearrange("p h k -> p (h k)"), in_=biasd[:, :]), w=[("constb",)])

        DVE("tensor_scalar", KC, [("nsk",)], out=CP[:, B_NSK:B_NSK + 8], in0=CP[:, B_SK:B_SK + 8], scalar1=-1.0,
            scalar2=None, op0=ALU.mult)

        gp_state = {"next": 0}
        total_pieces = n_items * 2 * NPIECE

        def issue_piece(g):
            l = (g // NPIECE) % 2
            j = g % NPIECE
            slot = g % NSLOT
            n = psz[j]
            sp = {4096: 1024, 2048: 1024, 2816: 1408, 2560: 1280}[n]
            dst = WRt[:, slot, 0:n].rearrange("p (a b) -> p a b", b=sp)
            src = wst[l, j, :, 0:n].rearrange("p (a b) -> p a b", b=sp)
            T.dma("gq", "w%d" % slot, dict(out=dst, in_=src), w=[("w", slot)])

        def acquire_piece(g, hold_from=None):
            if hold_from is None:
                hold_from = g
            while gp_state["next"] < min(hold_from + NSLOT, total_pieces):
                issue_piece(gp_state["next"])
                gp_state["next"] += 1
            assert gp_state["next"] > g
            return g % NSLOT

        def stats_rstd(src_fn, n):
            b1 = nbank()
            for c in range(8):
                ap, keys = src_fn(c)
                si = 6 + (c % 2)
                sq = scr(si, BF16, n)
                ACT("activation", keys, scrk(si), out=sq, in_=ap, func=AF.Square)
                PE("matmul", scrk(si) + KC, psk(b1), out=PSt[:, b1, 0:n], lhsT=onesm, rhs=sq,
                   start=(c == 0), stop=(c == 7))
            sr = scr(8, F32, n, nslots=2)
            ACT("activation", psk(b1) + KC, scrk(8, 2), out=sr, in_=PSt[:, b1, 0:n], func=AF.Sqrt, bias=epsc, scale=1.0)
            b2 = nbank()
            DVE("reciprocal", scrk(8, 2), psk(b2), out=PSt[:, b2, 0:n], in_=sr)
            return b2

        def prenorm(l, w, a, b, src_fn):
            n = b - a
            b2 = stats_rstd(src_fn, n)
            for c in range(8):
                ap, keys = src_fn(c)
                DVE("scalar_tensor_tensor", keys + psk(b2) + KC, bk("H", [c], a, b),
                    out=Ht[:, c, a:b], in0=ap, scalar=cpc(B_G + (l * 4 + w) * 8 + c), in1=PSt[:, b2, 0:n],
                    op0=ALU.mult, op1=ALU.mult)

        def xsrc(a, b):
            def f(c):
                return Xt[:, c, a - 128:b - 128], bk("X", [c], a, b)
            return f

        def postnorm_residual(l, w, a, b, Ybuf, yname):
            n = b - a

            def ysrc(c):
                return Ybuf[:, c, a - 128:b - 128], bk(yname, [c], a, b)
            b2 = stats_rstd(ysrc, n)
            tms = {}
            for c in range(9):
                if c < 8:
                    tb = nbank()
                    while tb == b2 or tb in [v for k_, v in tms.items() if k_ >= c - 1]:
                        tb = nbank()
                    tms[c] = tb
                    DVE("scalar_tensor_tensor", bk(yname, [c], a, b) + psk(b2) + KC, psk(tb),
                        out=PSt[:, tb, 0:n], in0=Ybuf[:, c, a - 128:b - 128], scalar=cpc(B_G + (l * 4 + w) * 8 + c),
                        in1=PSt[:, b2, 0:n], op0=ALU.mult, op1=ALU.mult)
                if c >= 1:
                    cp_ = c - 1
                    xs = Xt[:, cp_, a - 128:b - 128]
                    DVE("tensor_tensor", psk(tms[cp_]) + bk("X", [cp_], a, b), bk("X", [cp_], a, b),
                        out=xs, in0=xs, in1=PSt[:, tms[cp_], 0:n], op=ALU.add)

        def fm_proj(slot, woff, nk, rhs_fn, a, b, rkeys):
            n = b - a
            bnk = nbank()
            for k in range(nk):
                PE("matmul", [("w", slot)] + rkeys, psk(bnk), inc=(k == nk - 1),
                   out=PSt[:, bnk, 0:n], lhsT=WRt[:, slot, woff + k * 128: woff + (k + 1) * 128],
                   rhs=rhs_fn(k), start=(k == 0), stop=(k == nk - 1))
            return bnk

        class _Stop(Exception):
            pass

        def chk(name):
            if debug and debug.get("_stop") == name:
                raise _Stop()

        def dump(name, ap, keys):
            if name in dbg_t:
                T.dma("sync", "dbg", dict(out=dbg_t[name], in_=ap), r=keys)

        def layer(it, l, KV0, KV1, C0, C1, gbase, after_p9=None):
            kv_tiles = tiles_of(KV0, KV1)
            c_tiles = tiles_of(C0, C1)
            kvb = list(range(KV0 // 128, KV1 // 128))
            cb_ = list(range(C0 // 128, C1 // 128))
            g = gbase

            WORDER = [3, 2, 0, 1]

            def gen_win(pjs):
                for pj in pjs:
                    slot = acquire_piece(g + WORDER.index(pj))
                    nchunks = 4 if pj < 3 else 1
                    for ci in range(nchunks):
                        zi = pj * 4 + ci
                        for (a, b) in kv_tiles:
                            bnk = fm_proj(slot, ci * 1024, 8, lambda k, a=a, b=b: Ht[:, k, a:b], a, b,
                                          bk("H", range(8), a, b))
                            evac_copy(Zc[zi][:, a:b], PSt[:, bnk, 0:b - a], psk(bnk), bk("Z", [zi], a, b))
                            yield
                    if pj == 3:
                        for blk in kvb:
                            t0 = blk * 128
                            inC = blk in cb_
                            ncol = 384 if inC else 128
                            bnk = nbank()
                            for k in range(8):
                                PE("matmul", [("w", slot)] + bk("H", range(8), t0, t0 + 128), psk(bnk), inc=(k == 7),
                                   out=PSt[:, bnk, 0:ncol], lhsT=Ht[:, k, t0:t0 + 128],
                                   rhs=WRt[:, slot, 1024 + k * 384: 1024 + k * 384 + ncol],
                                   start=(k == 0), stop=(k == 7))
                            ACT("activation", psk(bnk), [("VT", blk)], out=VT[:, blk, :], in_=PSt[:, bnk, 0:128], func=AF.Copy)
                            if inC:
                                s1 = nsm()
                                s2 = nsm()
                                DVE("bn_stats", psk(bnk), [("sm", s1)], out=SMt[:, s1, 0:6], in_=PSt[:, bnk, 128:384])
                                DVE("bn_aggr", [("sm", s1)], [("sm", s2)], out=SMt[:, s2, 0:2], in_=SMt[:, s1, 0:6])
                                ACT("activation", [("sm", s2)] + KC, [("sm", s2)], out=SMt[:, s2, 2:3], in_=SMt[:, s2, 1:2],
                                    func=AF.Sqrt, bias=epsc, scale=1.0)
                                DVE("reciprocal", [("sm", s2)], [("sm", s2)], out=SMt[:, s2, 3:4], in_=SMt[:, s2, 2:3])
                                DVE("tensor_scalar", psk(bnk) + [("sm", s2), ("sm", s2)], [("VS", blk)],
                                    out=VS[:, blk - 1, :], in0=PSt[:, bnk, 128:384], scalar1=SMt[:, s2, 0:1],
                                    scalar2=SMt[:, s2, 3:4], op0=ALU.subtract, op1=ALU.mult)
                            yield

            def gen_conv():
                for cc in range(2):
                    for (a, b) in c_tiles:
                        n = b - a
                        U = scr(12, BF16, n + 2)
                        DVE("tensor_tensor", bk("Z", [2 + cc, 4 + cc], a - 1, b + 1), scrk(12),
                            out=U, in0=Zc[2 + cc][:, a - 1:b + 1], in1=Zc[4 + cc][:, a - 1:b + 1], op=ALU.mult)
                        bnk = nbank()
                        for tap in range(3):
                            o = CB_DG + ((l * 3 + tap) * 2 + cc) * 128
                            PE("matmul", scrk(12) + KC, psk(bnk), inc=(tap == 2), out=PSt[:, bnk, 0:n],
                               lhsT=CB[:, o:o + 128], rhs=U[:, tap:tap + n], start=(tap == 0), stop=(tap == 2))
                        DVE("tensor_tensor", psk(bnk) + bk("Z", [cc], a, b), bk("BR", [cc], a, b),
                            out=BR[:, cc, a - 128:b - 128], in0=Zc[cc][:, a:b], in1=PSt[:, bnk, 0:n], op=ALU.mult)
                        yield

            def gen_pool():
                for cc in range(2):
                    zp = 6 + cc
                    for (a, b) in c_tiles:
                        n = b - a
                        T1 = scr(14, BF16)
                        T2 = scr(15, BF16)
                        T3 = scr(16, BF16)
                        T4 = scr(17, BF16)
                        PL = scr(13, BF16, n)
                        zk = bk("Z", [zp], a - 8, b + 8)
                        DVE("tensor_tensor", zk, scrk(14), out=T1[:, 0:n + 14], in0=Zc[zp][:, a - 8:b + 6],
                            in1=Zc[zp][:, a - 7:b + 7], op=ALU.add)
                        DVE("tensor_tensor", scrk(14), scrk(15), out=T2[:, 0:n + 12], in0=T1[:, 0:n + 12],
                            in1=T1[:, 2:n + 14], op=ALU.add)
                        if cc == 0:
                            sel = [(T1, 7, 14), (T2, 6, 15)]
                        else:
                            yield
                            DVE("tensor_tensor", scrk(15), scrk(16), out=T3[:, 0:n + 8], in0=T2[:, 0:n + 8],
                                in1=T2[:, 4:n + 12], op=ALU.add)
                            DVE("tensor_tensor", scrk(16), scrk(17), out=T4[:, 0:n], in0=T3[:, 0:n],
                                in1=T3[:, 8:n + 8], op=ALU.add)
                            sel = [(T3, 4, 16), (T4, 0, 17)]
                        yield
                        for hf in range(2):
                            Ts, off, si = sel[hf]
                            r0 = hf * 64
                            DVE("scalar_tensor_tensor", scrk(si) + bk("Z", [zp], a, b) + KC, scrk(13),
                                out=PL[r0:r0 + 64, :], in0=Ts[r0:r0 + 64, off:off + n],
                                scalar=CP[r0:r0 + 64, B_IW + cc:B_IW + cc + 1],
                                in1=Zc[zp][r0:r0 + 64, a:b], op0=ALU.mult, op1=ALU.subtract)
                            for side, e0 in ((0, 256), (1, 1272)):
                                if a <= e0 and e0 + 8 <= b:
                                    j0 = e0 - a
                                    tb = B_PT + ((it * 2 + side) * 2 + cc) * 8
                                    sm = nsm()
                                    DVE("tensor_tensor", scrk(si) + KC, [("sm", sm)],
                                        out=SMt[r0:r0 + 64, sm, :], in0=Ts[r0:r0 + 64, off + j0:off + j0 + 8],
                                        in1=CP[r0:r0 + 64, tb:tb + 8], op=ALU.mult)
                                    DVE("tensor_tensor", [("sm", sm)] + bk("Z", [zp], e0, e0 + 8), scrk(13),
                                        out=PL[r0:r0 + 64, j0:j0 + 8], in0=SMt[r0:r0 + 64, sm, :],
                                        in1=Zc[zp][r0:r0 + 64, e0:e0 + 8], op=ALU.subtract)
                        bnk = nbank()
                        PE("matmul", scrk(13) + KC, psk(bnk), out=PSt[:, bnk, 0:n],
                           lhsT=CB[:, CB_PW + (l * 2 + cc) * 128: CB_PW + (l * 2 + cc + 1) * 128],
                           rhs=PL, start=True, stop=True)
                        ACT("activation", psk(bnk) + KC, bk("BR", [2 + cc], a, b),
                            out=BR[:, 2 + cc, a - 128:b - 128], in_=PSt[:, bnk, 0:n], func=AF.Copy,
                            scale=cpc(B_PS + l * 2 + cc))
                        yield

            def gen_sg():
                for blk in cb_:
                    t0 = blk * 128
                    for cc in range(2):
                        bnk = nbank()
                        PE("matmul", [("VS", blk)] + KC, psk(bnk), out=PSt[:, bnk, 0:256],
                           lhsT=VS[:, blk - 1, cc * 128:(cc + 1) * 128],
                           rhs=CB[:, CB_SGW + (l * 2 + cc) * 256: CB_SGW + (l * 2 + cc + 1) * 256],
                           start=True, stop=True)
                        for hf in range(2):
                            r0 = hf * 64
                            tm = PSt[r0:r0 + 64, bnk, 256 + hf * 128:256 + (hf + 1) * 128]
                            DVE("scalar_tensor_tensor", psk(bnk) + KC, psk(bnk),
                                out=tm, in0=PSt[r0:r0 + 64, bnk, hf * 128:(hf + 1) * 128],
                                scalar=CP[r0:r0 + 64, B_SN + l * 2 + cc:B_SN + l * 2 + cc + 1],
                                in1=SGB[r0:r0 + 64, l, cc, :], op0=ALU.mult, op1=ALU.add)
                            DVE("tensor_tensor", psk(bnk) + bk("Z", [8 + cc], t0, t0 + 128), bk("BR", [4 + cc], t0, t0 + 128),
                                out=BR[r0:r0 + 64, 4 + cc, t0 - 128:t0], in0=Zc[8 + cc][r0:r0 + 64, t0:t0 + 128],
                                in1=tm, op=ALU.mult)
                        yield

            sinkv = CP[:, B_SK + l * 4:B_SK + l * 4 + 4]
            nsinkv = CP[:, B_NSK + l * 4:B_NSK + l * 4 + 4]

            def attn_unit(blk, hp, u):
                t0 = blk * 128
                k0 = t0 - 128
                sb0 = (u % 6) * 2
                Pb = SCRt[:, sb0 * SCRB:sb0 * SCRB + 1536].bitcast(BF16).rearrange("p (h k) -> p h k", h=2)
                Pk = scrk(sb0)
                PTs = SCRt[:, (sb0 + 1) * SCRB:(sb0 + 1) * SCRB + 1536].bitcast(BF16).rearrange("p (h k) -> p h k", h=2)
                PTk = scrk(sb0 + 1)
                hds = (2 * hp, 2 * hp + 1)
                r0 = hp * 64
                b0 = try_hold(2)
                while b0 is None:
                    yield
                    b0 = try_hold(2)
                Sk = psk(b0) + psk(b0 + 1)
                for i2, hd in enumerate(hds):
                    qc = 10 + (hd % 2)
                    PE("matmul", bk("Z", [qc], t0, t0 + 128) + bk("Z", [12], k0, k0 + 384), psk(b0 + i2), inc=False,
                       out=PSt[:, b0 + i2, 0:384], lhsT=Zc[qc][r0:r0 + 64, t0:t0 + 128],
                       rhs=Zc[12][r0:r0 + 64, k0:k0 + 384], start=True, stop=False)
                    PE("matmul", KC, psk(b0 + i2), out=PSt[:, b0 + i2, 0:384], lhsT=ident, rhs=BIAS[:, hd, :],
                       start=False, stop=True)
                S2 = PSt[:, b0:b0 + 2, 0:384]
                if blk == 2:
                    DVE("tensor_scalar", Sk + KC, Sk, out=PSt[:, b0:b0 + 2, 0:128], in0=PSt[:, b0:b0 + 2, 0:128],
                        scalar1=cpc(B_FL + it * 4 + 0), scalar2=None, op0=ALU.add)
                if blk == 9:
                    DVE("tensor_scalar", Sk + KC, Sk, out=PSt[:, b0:b0 + 2, 256:384], in0=PSt[:, b0:b0 + 2, 256:384],
                        scalar1=cpc(B_FL + it * 4 + 1), scalar2=None, op0=ALU.add)
                yield
                s1 = nsm()
                s2 = nsm()
                s3 = nsm()
                s4 = nsm()
                SMr = SMt[:, s1, :]
                SMq = SMt[:, s2, :]
                SMs = SMt[:, s3, :]
                SMd = SMt[:, s4, :]
                sk2 = sinkv[:, 2 * hp:2 * hp + 2]
                DVE("tensor_reduce", Sk, [("sm", s1)], out=SMr[:, 0:2], in_=S2, axis=AX.X, op=ALU.max, negate=True)
                DVE("scalar_tensor_tensor", [("sm", s1), ("nsk",)] + KC, [("sm", s2)], out=SMq[:, 0:2], in0=SMr[:, 0:2],
                    scalar=0.125, in1=nsinkv[:, 2 * hp:2 * hp + 2], op0=ALU.mult, op1=ALU.min)
                yield
                for i2 in range(2):
                    ACT("activation", psk(b0 + i2) + [("sm", s2)], Pk + [("sm", s3)],
                        out=Pb[:, i2, :], in_=PSt[:, b0 + i2, 0:384], func=AF.Exp, bias=SMq[:, i2:i2 + 1], scale=0.125,
                        accum_out=SMs[:, i2:i2 + 1])
                    ACT("activation", [("sm", s2)] + KC, [("sm", s3)], out=SMs[:, 4 + i2:5 + i2],
                        in_=sinkv[:, 2 * hp + i2:2 * hp + i2 + 1], func=AF.Exp, bias=SMq[:, i2:i2 + 1], scale=1.0)
                release(b0, 2)
                yield
                DVE("tensor_tensor", [("sm", s3), ("sm", s3), ("sm", s3), ("sm", s3)], [("sm", s4)],
                    out=SMd[:, 0:2], in0=SMs[:, 0:2], in1=SMs[:, 4:6], op=ALU.add)
                DVE("reciprocal", [("sm", s4)], [("sm", s4)], out=SMd[:, 4:6], in_=SMd[:, 0:2])
                yield
                for i2 in range(2):
                    ACT("activation", Pk + [("sm", s4)], Pk, out=Pb[:, i2, :], in_=Pb[:, i2, :], func=AF.Copy,
                        scale=SMd[:, 4 + i2:5 + i2])
                yield
                tbk = try_hold(1)
                while tbk is None:
                    yield
                    tbk = try_hold(1)
                ptv = PSt[:, tbk, :].bitcast(BF16)
                for i2 in range(2):
                    for j in range(3):
                        c0 = i2 * 384 + j * 128
                        PE("transpose", Pk + KC, psk(tbk), out=ptv[:, c0:c0 + 128],
                           in_=Pb[:, i2, j * 128:(j + 1) * 128], identity=ident)
                yield
                evac_copy(PTs.rearrange("p h k -> p (h k)"), ptv[:, 0:768], psk(tbk), PTk)
                release(tbk)
                yield
                ob = try_hold(1)
                while ob is None:
                    yield
                    ob = try_hold(1)
                for i2 in range(2):
                    for j in range(3):
                        PE("matmul", PTk + [("VT", blk - 1 + j)], psk(ob), inc=(j == 2),
                           out=PSt[i2 * 64:i2 * 64 + 64, ob, 0:128],
                           lhsT=VT[:, blk - 1 + j, hp * 64:(hp + 1) * 64], rhs=PTs[:, i2, j * 128:(j + 1) * 128],
                           start=(j == 0), stop=(j == 2))
                yield
                ACT("activation", psk(ob), bk("BR", [6 + hp], t0, t0 + 128),
                    out=BR[:, 6 + hp, t0 - 128:t0], in_=PSt[:, ob, 0:128], func=AF.Copy)
                release(ob)

            for _ in gen_win([3, 2]):
                pass
            import itertools
            others = itertools.chain(gen_win([0, 1]), gen_sg(), gen_conv(), gen_pool())
            others_done = False
            units = [(blk, hp) for blk in cb_ for hp in range(2)]
            active = []
            nxt = 0
            while nxt < len(units) or active or not others_done:
                if nxt < len(units) and len(active) < 6:
                    active.append(attn_unit(units[nxt][0], units[nxt][1], nxt))
                    nxt += 1
                for gen in list(active):
                    try:
                        next(gen)
                    except StopIteration:
                        active.remove(gen)
                for _ in range(2):
                    if not others_done:
                        try:
                            next(others)
                        except StopIteration:
                            others_done = True
            g += 4

            if it == 0 and ("BR%d" % l) in dbg_t:
                dump("BR%d" % l, BR, bk("BR", range(8), 128, 1408))

            chk("p3d")
            T.barrier(["pe", "act", "dve"])
            GB = [0, 1, 2]
            PB = [3, 4, 5]
            MB = [6, 7]
            gctr = [0]
            mctr = [0]
            actr = [0]
            for c in range(8):
                slots2 = [acquire_piece(g, hold_from=g), acquire_piece(g + 1, hold_from=g)]
                g += 2
                for (a, b) in c_tiles:
                    n = b - a
                    ai = actr[0] % 2
                    actr[0] += 1
                    accb = scr(6 + 2 * ai, F32, n, nslots=2)
                    acck = scrk(6 + 2 * ai, 2)
                    for i in range(4):
                        gslot = slots2[i // 2]
                        i2 = i % 2
                        gb = GB[gctr[0] % 3]
                        pb = PB[gctr[0] % 3]
                        gi = gctr[0] % 2
                        gctr[0] += 1
                        for k in range(8):
                            wo = (i2 * 10 + k) * 128
                            PE("matmul", [("w", gslot)] + bk("H", range(8), a, b), psk(gb), inc=(k == 7),
                               out=PSt[:, gb, 0:n], lhsT=WRt[:, gslot, wo:wo + 128],
                               rhs=Ht[:, k, a:b], start=(k == 0), stop=(k == 7))
                        for kk in range(2):
                            wo = (i2 * 10 + 8 + kk) * 128
                            PE("matmul", [("w", gslot)] + bk("BR", [2 * i, 2 * i + 1], a, b), psk(pb), inc=(kk == 1),
                               out=PSt[:, pb, 0:n], lhsT=WRt[:, gslot, wo:wo + 128],
                               rhs=BR[:, 2 * i + kk, a - 128:b - 128], start=(kk == 0), stop=(kk == 1))
                        gsb = scr(2 * gi, F32, n, nslots=2)
                        ACT("activation", psk(gb) + KC, scrk(2 * gi, 2), out=gsb, in_=PSt[:, gb, 0:n], func=AF.Sigmoid,
                            bias=cpc(B_BG + (l * 4 + i) * 8 + c), scale=1.0)
                        if i == 0:
                            DVE("tensor_tensor", scrk(2 * gi, 2) + psk(pb), acck, out=accb, in0=gsb, in1=PSt[:, pb, 0:n],
                                op=ALU.mult)
                        else:
                            tb = MB[mctr[0] % 2]
                            mctr[0] += 1
                            DVE("tensor_tensor", scrk(2 * gi, 2) + psk(pb), psk(tb), out=PSt[:, tb, 0:n], in0=gsb,
                                in1=PSt[:, pb, 0:n], op=ALU.mult)
                            if i < 3:
                                DVE("tensor_tensor", acck + psk(tb), acck, out=accb, in0=accb, in1=PSt[:, tb, 0:n],
                                    op=ALU.add)
                            else:
                                DVE("tensor_tensor", acck + psk(tb), bk("M", [c], a, b), out=Mv[:, c, a - 128:b - 128],
                                    in0=accb, in1=PSt[:, tb, 0:n], op=ALU.add)

            if l == 1 and it + 1 < n_items:
                load_halo(it + 1)
            chk("p4")
            wslots = [acquire_piece(g, hold_from=g), acquire_piece(g + 1, hold_from=g)]
            g += 2
            for (a, b) in c_tiles:
                for op_ in range(2):
                    for c4 in range(4):
                        c = op_ * 4 + c4
                        bnk = fm_proj(wslots[op_], c4 * 1024, 8, lambda k, a=a, b=b: Mv[:, k, a - 128:b - 128], a, b,
                                      bk("M", range(8), a, b))
                        evac_copy(Yv[:, c, a - 128:b - 128], PSt[:, bnk, 0:b - a], psk(bnk), bk("Y", [c], a, b))
                postnorm_residual(l, 1, a, b, Yv, "Y")
                prenorm(l, 2, a, b, xsrc(a, b))
            chk("p6")
            for half in range(2):
                for pp in range(6):
                    slot = acquire_piece(g)
                    g += 1
                    for jj in range(2 if pp < 5 else 1):
                        j = pp * 2 + jj
                        for (a, b) in c_tiles:
                            n = b - a
                            ba = fm_proj(slot, (jj * 2 + 0) * 1024, 8, lambda k, a=a, b=b: Ht[:, k, a:b], a, b,
                                         bk("H", range(8), a, b))
                            bg = fm_proj(slot, (jj * 2 + 1) * 1024, 8, lambda k, a=a, b=b: Ht[:, k, a:b], a, b,
                                         bk("H", range(8), a, b))
                            si = j % 2
                            ssb = scr(2 * si, F32, n, nslots=2)
                            ACT("activation", psk(ba), scrk(2 * si, 2), out=ssb, in_=PSt[:, ba, 0:n], func=AF.Silu)
                            DVE("tensor_tensor", scrk(2 * si, 2) + psk(bg), bk("HID", [j], a, b),
                                out=HID[:, j, a - 128:b - 128], in0=ssb, in1=PSt[:, bg, 0:n], op=ALU.mult)
                for pp in range(4):
                    slot = acquire_piece(g)
                    g += 1
                    for c2 in range(2):
                        c = pp * 2 + c2
                        for (a, b) in c_tiles:
                            n = b - a
                            bnk = fm_proj(slot, c2 * 1408, 11, lambda k, a=a, b=b: HID[:, k, a - 128:b - 128], a, b,
                                          bk("HID", range(11), a, b))
                            y2s = Y2[:, c, a - 128:b - 128]
                            if half == 0:
                                evac_copy(y2s, PSt[:, bnk, 0:n], psk(bnk), bk("Y2", [c], a, b))
                            else:
                                DVE("tensor_tensor", psk(bnk) + bk("Y2", [c], a, b), bk("Y2", [c], a, b),
                                    out=y2s, in0=y2s, in1=PSt[:, bnk, 0:n], op=ALU.add)
            chk("p8")
            for ti, (a, b) in enumerate(c_tiles):
                postnorm_residual(l, 3, a, b, Y2, "Y2")
                if after_p9 is not None:
                    after_p9(ti, a, b)
            assert g == gbase + NPIECE
            if it == 0:
                dump("X%d" % l, Xt[:], bk("X", range(8), 128, 1408))

        epsc = EPS
        XH = SCRt[:, 12 * SCRB:12 * SCRB + 8192].bitcast(F32).rearrange("p (c t) -> p c t", c=8)
        def load_halo(it_):
            T.dma("sync", "xh", dict(out=XH[:, :, 0:128], in_=xin[it_, :, :, 0:128]), w=scrk(12, 6))
            T.dma("sync", "xh", dict(out=XH[:, :, 128:256], in_=xin[it_, :, :, 1408:1536]), w=scrk(12, 6))

        def main_body():
            for it in range(n_items):
                for ti, (a, b) in enumerate(tiles_of(128, 1408)):
                    T.dma("sync", "xa%d" % ti, dict(out=Xt[:, :, a - 128:b - 128], in_=xin[it, :, :, a:b]),
                          w=bk("X", range(8), a, b))
                if it == 0:
                    load_halo(0)
                gbase = (it * 2) * NPIECE

                def hsrc0(c):
                    return XH[:, c, 0:128], scrk(12, 6)

                def hsrc1(c):
                    return XH[:, c, 128:256], scrk(12, 6)
                chk("p0")
                tl = tiles_of(128, 1408)
                prenorm(0, 0, 0, 128, hsrc0)
                prenorm(0, 0, tl[0][0], tl[0][1], xsrc(*tl[0]))
                prenorm(0, 0, tl[1][0], tl[1][1], xsrc(*tl[1]))
                prenorm(0, 0, tl[2][0], tl[2][1], xsrc(*tl[2]))
                prenorm(0, 0, 1408, 1536, hsrc1)
                chk("p1")
                layer(it, 0, 0, 1536, 128, 1408, gbase)
                for (a, b) in tiles_of(128, 1408):
                    prenorm(1, 0, a, b, xsrc(a, b))
                hk = bk("H", range(8), 128, 256)
                DVE("tensor_scalar", hk + KC, hk, out=Ht[:, :, 128:256], in0=Ht[:, :, 128:256],
                    scalar1=cpc(B_FL + it * 4 + 2), scalar2=None, op0=ALU.mult)
                hk = bk("H", range(8), 1280, 1408)
                DVE("tensor_scalar", hk + KC, hk, out=Ht[:, :, 1280:1408], in0=Ht[:, :, 1280:1408],
                    scalar1=cpc(B_FL + it * 4 + 3), scalar2=None, op0=ALU.mult)
                def store_tile(ti, a, b, it=it):
                    T.dma("sync", "out%d" % ti, dict(out=yout[it, :, :, a - 256:b - 256], in_=Xt[:, :, a - 128:b - 128]),
                          r=bk("X", range(8), a, b))
                layer(it, 1, 128, 1408, 256, 1280, gbase + NPIECE, after_p9=store_tile)
        try:
            main_body()
        except _Stop:
            pass
        T.barrier(["pe", "act", "dve"])
        T.final_wait("sync", list(T.dcount.keys()))

        with nc.Block() as block:
            @block.sync
            def _(h):
                T.emit("sync", h)

            @block.gpsimd
            def _(h):
                T.emit("gq", h)

            @block.tensor
            def _(h):
                T.emit("pe", h)

            @block.scalar
            def _(h):
                T.emit("act", h)

            @block.vector
            def _(h):
                T.emit("dve", h)
    return nc


def pack_weights(w_in, w_branch, w_gate, w_out, w_ffn_in, w_ffn_out):
    wst = np.zeros((2, NPIECE, 128, PIECE_E), np.float32)

    def fm(Wcols):
        K = Wcols.shape[0] // 128
        return Wcols.reshape(K, 128, 128).transpose(1, 0, 2)
    qperm = np.concatenate([np.arange(1536, 1600), np.arange(1664, 1728), np.arange(1600, 1664), np.arange(1728, 1792)])
    fmcols = [np.arange(0, 128), np.arange(128, 256), np.arange(256, 384), np.arange(384, 512),
              np.arange(512, 640), np.arange(640, 768), np.arange(768, 896), np.arange(896, 1024),
              np.arange(1024, 1152), np.arange(1152, 1280), qperm[:128], qperm[128:], np.arange(1792, 1920)]
    tmcols = np.concatenate([np.arange(1920, 2048), np.arange(1280, 1536)])
    for l in range(2):
        wp = []
        for pj in range(3):
            parts = [fm(w_in[l][:, fmcols[pj * 4 + ci]]) for ci in range(4)]
            wp.append(np.stack(parts, axis=1).reshape(128, 4096))
        kpart = fm(w_in[l][:, fmcols[12]]).reshape(128, 1024)
        tm = w_in[l][:, tmcols].reshape(8, 128, 384).transpose(1, 0, 2).reshape(128, 3072)
        wp.append(np.concatenate([kpart, tm], axis=1))
        for j, pj in enumerate([3, 2, 0, 1]):
            wst[l, j] = wp[pj]
        j = 4
        for c in range(8):
            for ih in range(2):
                arr = np.zeros((128, 2, 10, 128), np.float32)
                for i2 in range(2):
                    i = ih * 2 + i2
                    arr[:, i2, 0:8] = fm(w_gate[l, i][:, c * 128:(c + 1) * 128])
                    arr[:, i2, 8:10] = fm(w_branch[l, i][:, c * 128:(c + 1) * 128])
                wst[l, j, :, :2560] = arr.reshape(128, 2560)
                j += 1
        for op_ in range(2):
            arr = np.stack([fm(w_out[l][:, (op_ * 4 + c4) * 128:(op_ * 4 + c4 + 1) * 128]) for c4 in range(4)], axis=1)
            wst[l, j] = arr.reshape(128, 4096)
            j += 1
        for half in range(2):
            for pp in range(6):
                njj = 2 if pp < 5 else 1
                arr = np.zeros((128, njj, 2, 8, 128), np.float32)
                for jj in range(njj):
                    hj = half * 11 + pp * 2 + jj
                    for ag in range(2):
                        arr[:, jj, ag] = fm(w_ffn_in[l][:, ag * DFF + hj * 128: ag * DFF + (hj + 1) * 128])
                wst[l, j, :, :njj * 2048] = arr.reshape(128, njj * 2048)
                j += 1
            for pp in range(4):
                arr = np.zeros((128, 2, 11, 128), np.float32)
                for c2 in range(2):
                    c = pp * 2 + c2
                    arr[:, c2] = fm(w_ffn_out[l][half * 11 * 128:(half + 1) * 11 * 128, c * 128:(c + 1) * 128])
                wst[l, j, :, :2816] = arr.reshape(128, 2816)
                j += 1
        assert j == NPIECE
    return wst


def item_list(core):
    items = []
    for a in (0, 8, 16, 24):
        items.append(("p", core, a, 32))
    for a in (16 * core, 16 * core + 8):
        items.append(("s", 0, a, 128))
    return items


def build_inputs(inp):
    f32 = np.float32
    x_prompt = np.asarray(inp["x_prompt"], f32)
    x_sample = np.asarray(inp["x_sample"], f32)
    wst = pack_weights(np.asarray(inp["w_in"], f32), np.asarray(inp["w_branch"], f32), np.asarray(inp["w_gate"], f32),
                       np.asarray(inp["w_out"], f32), np.asarray(inp["w_ffn_in"], f32), np.asarray(inp["w_ffn_out"], f32))
    cp0 = np.zeros((128, NCP), f32)
    gains = [inp["norm_mix_pre"], inp["norm_mix_post"], inp["norm_ffn_pre"], inp["norm_ffn_post"]]
    for l in range(2):
        for w in range(4):
            cp0[:, B_G + (l * 4 + w) * 8: B_G + (l * 4 + w) * 8 + 8] = np.asarray(gains[w], f32)[l].reshape(8, 128).T
        for i in range(4):
            cp0[:, B_BG + (l * 4 + i) * 8: B_BG + (l * 4 + i) * 8 + 8] = np.asarray(inp["b_gate"], f32)[l, i].reshape(8, 128).T
        for tap in range(3):
            cp0[:, B_CW + (l * 3 + tap) * 2: B_CW + (l * 3 + tap) * 2 + 2] = np.asarray(inp["conv_w"], f32)[l, tap].reshape(2, 128).T
        cp0[:, B_PS + l * 2: B_PS + l * 2 + 2] = np.asarray(inp["pool_scale"], f32)[l].reshape(2, 128).T
        cp0[:, B_SN + l * 2: B_SN + l * 2 + 2] = np.asarray(inp["sg_norm"], f32)[l].reshape(2, 128).T
        cp0[:, B_SK + l * 4: B_SK + l * 4 + 4] = np.asarray(inp["attn_sink"], f32)[l][None, :]
    wins = np.zeros((128, 2), np.int64)
    wins[:64, 0] = 2
    wins[64:, 0] = 4
    wins[:64, 1] = 8
    wins[64:, 1] = 16
    cp0[:, B_IW:B_IW + 2] = (1.0 / wins).astype(f32)
    cb = np.zeros((128, NCB), f32)
    sg_w = np.asarray(inp["sg_w"], f32)
    pool_w = np.asarray(inp["pool_w"], f32)
    for l in range(2):
        for cc in range(2):
            for jg in range(2):
                o = CB_SGW + (l * 2 + cc) * 256 + jg * 128
                cb[:, o:o + 128] = sg_w[l, 2 * cc + jg].T
            blk = np.zeros((128, 128), f32)
            blk[:64, :64] = pool_w[l, 2 * cc]
            blk[64:, 64:] = pool_w[l, 2 * cc + 1]
            cb[:, CB_PW + (l * 2 + cc) * 128: CB_PW + (l * 2 + cc + 1) * 128] = blk
    conv_w = np.asarray(inp["conv_w"], f32)
    for l in range(2):
        for tap in range(3):
            for cc in range(2):
                o = CB_DG + ((l * 3 + tap) * 2 + cc) * 128
                dg = np.zeros((128, 128), f32)
                dg[np.arange(128), np.arange(128)] = conv_w[l, tap, cc * 128:(cc + 1) * 128]
                cb[:, o:o + 128] = dg
    cb[:, CB_ID:CB_ID + 128] = np.eye(128, dtype=f32)
    cb[:, CB_ON:CB_ON + 128] = 1.0 / 1024.0
    qi = np.arange(128)[:, None]
    kj = np.arange(384)[None, :]
    rel = np.abs(qi - kj + 128).astype(f32)
    slopes = np.exp2(-8.0 * (np.arange(4, dtype=f32) + 1.0) / 4.0).astype(f32)
    abias = np.where(rel[:, None, :] <= 128, -8.0 * slopes[None, :, None] * rel[:, None, :], f32(-8e30)).astype(f32)
    abias = np.ascontiguousarray(abias.reshape(128, 4 * 384))
    sg_b = np.asarray(inp["sg_b"], f32)
    sgb = np.zeros((128, 2, 2, 128), f32)
    for l in range(2):
        for cc in range(2):
            sgb[:64, l, cc, :] = sg_b[l, 2 * cc][None, :]
            sgb[64:, l, cc, :] = sg_b[l, 2 * cc + 1][None, :]
    sgb = sgb.reshape(128, 512)

    in_maps = []
    for core in range(NCORES):
        items = item_list(core)
        xin = np.zeros((NITEM, 128, 8, NTOK_IN), f32)
        cp = cp0.copy()
        for it, (kind, sidx, a, nblk) in enumerate(items):
            src = x_prompt[sidx] if kind == "p" else x_sample[0]
            S = src.shape[0]
            t0 = (a - 2) * 128
            lo = max(t0, 0)
            hi = min(t0 + NTOK_IN, S)
            seg = src[lo:hi]
            xin[it, :, :, lo - t0:hi - t0] = seg.reshape(-1, 8, 128).transpose(2, 1, 0)
            lvalid = a > 0
            rvalid = (a + 8) < nblk
            cp[:, B_FL + it * 4 + 0] = 0.0 if lvalid else -1e30
            cp[:, B_FL + it * 4 + 1] = 0.0 if rvalid else -1e30
            cp[:, B_FL + it * 4 + 2] = 1.0 if lvalid else 0.0
            cp[:, B_FL + it * 4 + 3] = 1.0 if rvalid else 0.0
            for cc in range(2):
                w = wins[:, cc]
                for jx in range(8):
                    cl = w if lvalid else (jx + w // 2 - np.maximum(jx - w // 2, 0))
                    cr = w if rvalid else (np.minimum(w // 2, 8 - jx) + w // 2)
                    cp[:, B_PT + ((it * 2 + 0) * 2 + cc) * 8 + jx] = (1.0 / cl).astype(f32)
                    cp[:, B_PT + ((it * 2 + 1) * 2 + cc) * 8 + jx] = (1.0 / cr).astype(f32)
        in_maps.append({"xin": xin, "wst": wst, "cp": cp, "cb": cb, "abias": abias, "sgb": sgb})
    return in_maps


def assemble(results):
    yp = np.zeros((8, SEQ_P, D), np.float32)
    ys = np.zeros((1, SEQ_S, D), np.float32)
    for core in range(NCORES):
        y = results[core]["yout"]
        for it, (kind, sidx, a, nblk) in enumerate(item_list(core)):
            blk = y[it].transpose(2, 1, 0).reshape(NOUT, D)
            if kind == "p":
                yp[sidx, a * 128:a * 128 + NOUT] = blk
            else:
                ys[0, a * 128:a * 128 + NOUT] = blk
    return yp, ys


_DEBUG = None
_LAST = {}


def kernel(**inputs):
    in_maps = build_inputs(inputs)
    nc = build_program(_DEBUG)
    res = run_bass_kernel_spmd(nc, in_maps, core_ids=list(range(NCORES)))
    _LAST["res"] = res
    yp, ys = assemble(res.results)
    return (yp, ys)
```

```python
import numpy as np
import concourse.bass as bass
import concourse.mybir as mybir
from concourse.bass_utils import run_bass_kernel_spmd

F32 = mybir.dt.float32
BF16 = mybir.dt.bfloat16
U8 = mybir.dt.uint8
AF = mybir.ActivationFunctionType
ALU = mybir.AluOpType
AX = mybir.AxisListType

NCORES = 8
D = 1024
SEQ_P = 4096
SEQ_S = 16384
NITEM = 6
NTOK_IN = 1536
NX = 1280
NOUT = 1024
EPS = 1e-6
DFF = 2816
NPIECE = 42
PIECE_E = 4096
NSLOT = 4

B_G = 0
B_BG = 64
B_CW = 128
B_PS = 140
B_SN = 144
B_IW = 148
B_SK = 150
B_FL = 158
B_PT = 182
B_NSK = 374
NCP = 382
CB_SGW = 0
CB_PW = 1024
CB_ID = 1536
CB_ON = 1664
CB_DG = 1792
NCB = 3328


def piece_sizes():
    s = [4096, 4096, 4096, 4096]
    s += [2560] * 16
    s += [4096, 4096]
    for _ in range(2):
        s += [4096] * 5 + [2048]
        s += [2816] * 4
    assert len(s) == NPIECE
    return s


class Tracker:
    def __init__(self):
        self.streams = {}
        self.sems = {}
        self.count = {}
        self.known = {}
        self.lastw = {}
        self.readers = {}
        self.semh = {}
        self.dcount = {}

    def add_engine(self, name, sem=None):
        self.streams[name] = []
        self.known[name] = {}
        if sem is not None:
            self.sems[name] = sem
            self.semh[name] = sem
            self.count[name] = 0

    def add_dma_sem(self, name, sem):
        self.semh[name] = sem
        self.dcount[name] = 0

    def _waits(self, eng, r, w):
        need = {}

        def req(ev, same_ok):
            sn, val, en = ev
            if en == eng and same_ok and eng == "pe":
                return
            if need.get(sn, 0) < val:
                need[sn] = val
        for k in r:
            ev = self.lastw.get(k)
            if ev is not None:
                req(ev, False)
        for k in w:
            ev = self.lastw.get(k)
            if ev is not None:
                req(ev, True)
            for ev in self.readers.get(k, {}).values():
                req(ev, True)
        kn = self.known[eng]
        out = []
        for sn, val in need.items():
            if kn.get(sn, 0) < val:
                kn[sn] = val
                out.append((sn, val))
        return out

    def _commit(self, ev, r, w):
        for k in w:
            self.lastw[k] = ev
            self.readers[k] = {}
        for k in r:
            self.readers.setdefault(k, {})[ev[0]] = ev

    def op(self, eng, name, kw, r=(), w=(), inc=True):
        w = list(w) + [k for k in r if k[0] == "ps"]
        waits = self._waits(eng, r, w)
        if inc:
            self.count[eng] += 1
            ev = (eng, self.count[eng], eng)
        else:
            ev = (eng, self.count[eng] + 1, eng)
        self.streams[eng].append((waits, name, kw, self.sems[eng] if inc else None, 1))
        self._commit(ev, r, w)

    def dma(self, q, dsem, kw, r=(), w=()):
        waits = self._waits(q, r, w)
        self.dcount[dsem] += 16
        ev = (dsem, self.dcount[dsem], "dma:" + dsem)
        self.streams[q].append((waits, "dma_start", kw, self.semh[dsem], 16))
        self._commit(ev, r, w)

    def barrier(self, engs):
        for e in engs:
            waits = []
            for o in engs:
                if o == e:
                    continue
                val = self.count[o]
                if self.known[e].get(o, 0) < val:
                    self.known[e][o] = val
                    waits.append((o, val))
            self.streams[e].append((waits, None, None, None, 0))

    def final_wait(self, q, semnames):
        vals = [(sn, self.dcount[sn]) for sn in semnames if self.dcount[sn] > 0]
        self.streams[q].append((vals, None, None, None, 0))

    def emit(self, eng, h):
        semh = self.semh
        for waits, name, kw, isem, amt in self.streams[eng]:
            for sn, val in waits:
                h.wait_ge(semh[sn], val)
            if name is not None:
                ins = getattr(h, name)(**kw)
                if isem is not None:
                    ins.then_inc(isem, amt)


def tiles_of(t0, t1, step=512):
    out = []
    a = t0
    while a < t1:
        b = min(a + step, t1)
        out.append((a, b))
        a = b
    return out


def build_program(debug=None):
    nc = bass.Bass("TRN2", target_bir_lowering=False)
    psz = piece_sizes()
    xin = nc.dram_tensor("xin", [NITEM, 128, 8, NTOK_IN], F32, kind="ExternalInput").ap()
    wst = nc.dram_tensor("wst", [2, NPIECE, 128, PIECE_E], F32, kind="ExternalInput").ap()
    cpd = nc.dram_tensor("cp", [128, NCP], F32, kind="ExternalInput").ap()
    cbd = nc.dram_tensor("cb", [128, NCB], F32, kind="ExternalInput").ap()
    biasd = nc.dram_tensor("abias", [128, 4 * 384], F32, kind="ExternalInput").ap()
    sgbd = nc.dram_tensor("sgb", [128, 2 * 2 * 128], F32, kind="ExternalInput").ap()
    yout = nc.dram_tensor("yout", [NITEM, 128, 8, NOUT], F32, kind="ExternalOutput").ap()
    dbg_t = {}
    if debug:
        for name, spec in debug.items():
            if name.startswith("_"):
                continue
            dbg_t[name] = nc.dram_tensor("dbg_" + name, list(spec[0]), spec[1], kind="ExternalOutput").ap()
    n_items = NITEM if not (debug and debug.get("_items")) else 1

    from contextlib import ExitStack
    with ExitStack() as es:
        def sb(name, shape, dt):
            return es.enter_context(nc.sbuf_tensor(name, shape, dt))

        Xt = sb("X", [128, 8, NX], F32)
        Ht = sb("H", [128, 8, NTOK_IN], BF16)
        RSZ = 69632
        Rt = sb("R", [128, RSZ], U8)
        WRt = sb("WR", [128, NSLOT, PIECE_E], BF16)
        CP = sb("CP", [128, NCP], F32)
        CB = sb("CB", [128, NCB], BF16)
        BIAS = sb("BIAS", [128, 4, 384], BF16)
        SGB = sb("SGB", [128, 2, 2, 128], F32)
        NSCR = 18
        SCRB = 1536
        SCRt = sb("SCR", [128, NSCR * SCRB], U8)
        SMt = sb("SM", [128, 32, 8], F32)
        PSt = es.enter_context(nc.psum_tensor("PS", [128, 8, 512], F32))

        def sem(name):
            return es.enter_context(nc.semaphore(name))

        T = Tracker()
        for en in ("pe", "act", "dve"):
            T.add_engine(en, sem("s_" + en))
        T.add_engine("sync")
        T.add_engine("gq")
        for s_ in range(NSLOT):
            T.add_dma_sem("w%d" % s_, sem("s_w%d" % s_))
        for n_ in ("xa0", "xa1", "xa2", "xh", "out0", "out1", "cst", "cstg", "dbg"):
            T.add_dma_sem(n_, sem("s_" + n_))

        def PE(name, r, w, inc=True, **kw):
            T.op("pe", name, kw, r, w, inc)

        def ACT(name, r, w, **kw):
            T.op("act", name, kw, r, w)

        def DVE(name, r, w, **kw):
            T.op("dve", name, kw, r, w)

        def rview(off, nbytes, dt, pat=None, **kw):
            v = Rt[:, off:off + nbytes].bitcast(dt)
            if pat:
                v = v.rearrange(pat, **kw)
            return v
        Zc = {}
        Zc[12] = rview(0, 3072, BF16)
        VT = rview(3072, 3072, BF16, "p (b d) -> p b d", b=12)
        VS = rview(6144, 5120, BF16, "p (b d) -> p b d", b=10)
        for zi_ in range(8, 12):
            Zc[zi_] = rview(11264 + (zi_ - 8) * 3072, 3072, BF16)
        for zi_ in range(8):
            Zc[zi_] = rview(23552 + zi_ * 3072, 3072, BF16)
        BR = rview(48128, 20480, BF16, "p (c t) -> p c t", c=8)
        Mv = rview(0, 20480, BF16, "p (c t) -> p c t", c=8)
        Yv = rview(20480, 40960, F32, "p (c t) -> p c t", c=8)
        HID = rview(0, 28160, BF16, "p (c t) -> p c t", c=11)
        Y2 = rview(28160, 40960, F32, "p (c t) -> p c t", c=8)

        def scr(i, dt, n=None, nslots=1):
            v = SCRt[:, i * SCRB:(i + nslots) * SCRB].bitcast(dt)
            if n is not None:
                v = v[:, 0:n]
            return v

        def scrk(i, nslots=1):
            return [("scr", i + j) for j in range(nslots)]

        def bk(name, chunks, t0, t1):
            return [(name, c, b) for c in chunks for b in range(t0 // 128, (t1 + 127) // 128)]

        def cpc(j):
            return CP[:, j:j + 1]

        KC = [("const",), ("constb",)]
        ident = CB[:, CB_ID:CB_ID + 128]
        onesm = CB[:, CB_ON:CB_ON + 128]

        bank_ctr = [0]

        held = set()

        def nbank():
            for _ in range(8):
                b = bank_ctr[0] % 8
                bank_ctr[0] += 1
                if b not in held:
                    return b
            raise RuntimeError("no free PSUM bank")

        def try_hold(n):
            if len(held) + n > 6:
                return None
            step = 2 if n == 2 else 1
            for b in range(0, 8, step):
                if all((b + i) not in held for i in range(n)):
                    for i in range(n):
                        held.add(b + i)
                    return b
            return None

        def release(b, n=1):
            for i in range(n):
                held.discard(b + i)

        def psk(b):
            return [("ps", b)]

        sm_ctr = [0]

        def nsm():
            i = sm_ctr[0] % 32
            sm_ctr[0] += 1
            return i

        alt = [0]

        def evac_copy(out_ap, in_ap, r, w):
            alt[0] += 1
            if alt[0] % 2 == 0:
                ACT("activation", r, w, out=out_ap, in_=in_ap, func=AF.Copy)
            else:
                DVE("tensor_copy", r, w, out=out_ap, in_=in_ap)

        T.dma("sync", "cst", dict(out=CP[:], in_=cpd[:, :]), w=[("const",)])
        T.dma("sync", "cst", dict(out=SGB[:].rearrange("p l c t -> p (l c t)"), in_=sgbd[:, :]), w=[("const",)])
        T.dma("gq", "cstg", dict(out=CB[:, 0:1792], in_=cbd[:, 0:1792]), w=[("constb",)])
        T.dma("gq", "cstg", dict(out=CB[:, 1792:NCB], in_=cbd[:, 1792:NCB]), w=[("constb",)])
        T.dma("gq", "cstg", dict(out=BIAS[:].rearrange("p h k -> p (h k)"), in_=biasd[:, :]), w=[("constb",)])

        DVE("tensor_scalar", KC, [("nsk",)], out=CP[:, B_NSK:B_NSK + 8], in0=CP[:, B_SK:B_SK + 8], scalar1=-1.0,
            scalar2=None, op0=ALU.mult)

        gp_state = {"next": 0}
        total_pieces = n_items * 2 * NPIECE

        def issue_piece(g):
            l = (g // NPIECE) % 2
            j = g % NPIECE
            slot = g % NSLOT
            n = psz[j]
            sp = {4096: 1024, 2048: 1024, 2816: 1408, 2560: 1280}[n]
            dst = WRt[:, slot, 0:n].rearrange("p (a b) -> p a b", b=sp)
            src = wst[l, j, :, 0:n].rearrange("p (a b) -> p a b", b=sp)
            T.dma("gq", "w%d" % slot, dict(out=dst, in_=src), w=[("w", slot)])

        def acquire_piece(g, hold_from=None):
            if hold_from is None:
                hold_from = g
            while gp_state["next"] < min(hold_from + NSLOT, total_pieces):
                issue_piece(gp_state["next"])
                gp_state["next"] += 1
            assert gp_state["next"] > g
            return g % NSLOT

        def stats_rstd(src_fn, n):
            b1 = nbank()
            for c in range(8):
                ap, keys = src_fn(c)
                si = 6 + (c % 2)
                sq = scr(si, BF16, n)
                ACT("activation", keys, scrk(si), out=sq, in_=ap, func=AF.Square)
                PE("matmul", scrk(si) + KC, psk(b1), out=PSt[:, b1, 0:n], lhsT=onesm, rhs=sq,
                   start=(c == 0), stop=(c == 7))
            sr = scr(8, F32, n, nslots=2)
            ACT("activation", psk(b1) + KC, scrk(8, 2), out=sr, in_=PSt[:, b1, 0:n], func=AF.Sqrt, bias=epsc, scale=1.0)
            b2 = nbank()
            DVE("reciprocal", scrk(8, 2), psk(b2), out=PSt[:, b2, 0:n], in_=sr)
            return b2

        def prenorm(l, w, a, b, src_fn):
            n = b - a
            b2 = stats_rstd(src_fn, n)
            for c in range(8):
                ap, keys = src_fn(c)
                DVE("scalar_tensor_tensor", keys + psk(b2) + KC, bk("H", [c], a, b),
                    out=Ht[:, c, a:b], in0=ap, scalar=cpc(B_G + (l * 4 + w) * 8 + c), in1=PSt[:, b2, 0:n],
                    op0=ALU.mult, op1=ALU.mult)

        def xsrc(a, b):
            def f(c):
                return Xt[:, c, a - 128:b - 128], bk("X", [c], a, b)
            return f

        def postnorm_residual(l, w, a, b, Ybuf, yname):
            n = b - a

            def ysrc(c):
                return Ybuf[:, c, a - 128:b - 128], bk(yname, [c], a, b)
            b2 = stats_rstd(ysrc, n)
            tms = {}
            for c in range(9):
                if c < 8:
                    tb = nbank()
                    while tb == b2 or tb in [v for k_, v in tms.items() if k_ >= c - 1]:
                        tb = nbank()
                    tms[c] = tb
                    DVE("scalar_tensor_tensor", bk(yname, [c], a, b) + psk(b2) + KC, psk(tb),
                        out=PSt[:, tb, 0:n], in0=Ybuf[:, c, a - 128:b - 128], scalar=cpc(B_G + (l * 4 + w) * 8 + c),
                        in1=PSt[:, b2, 0:n], op0=ALU.mult, op1=ALU.mult)
                if c >= 1:
                    cp_ = c - 1
                    xs = Xt[:, cp_, a - 128:b - 128]
                    DVE("tensor_tensor", psk(tms[cp_]) + bk("X", [cp_], a, b), bk("X", [cp_], a, b),
                        out=xs, in0=xs, in1=PSt[:, tms[cp_], 0:n], op=ALU.add)

        def fm_proj(slot, woff, nk, rhs_fn, a, b, rkeys):
            n = b - a
            bnk = nbank()
            for k in range(nk):
                PE("matmul", [("w", slot)] + rkeys, psk(bnk), inc=(k == nk - 1),
                   out=PSt[:, bnk, 0:n], lhsT=WRt[:, slot, woff + k * 128: woff + (k + 1) * 128],
                   rhs=rhs_fn(k), start=(k == 0), stop=(k == nk - 1))
            return bnk

        class _Stop(Exception):
            pass

        def chk(name):
            if debug and debug.get("_stop") == name:
                raise _Stop()

        def dump(name, ap, keys):
            if name in dbg_t:
                T.dma("sync", "dbg", dict(out=dbg_t[name], in_=ap), r=keys)

        def layer(it, l, KV0, KV1, C0, C1, gbase, after_p9=None):
            kv_tiles = tiles_of(KV0, KV1)
            c_tiles = tiles_of(C0, C1)
            kvb = list(range(KV0 // 128, KV1 // 128))
            cb_ = list(range(C0 // 128, C1 // 128))
            g = gbase

            WORDER = [3, 2, 0, 1]

            def gen_win(pjs):
                for pj in pjs:
                    slot = acquire_piece(g + WORDER.index(pj))
                    nchunks = 4 if pj < 3 else 1
                    for ci in range(nchunks):
                        zi = pj * 4 + ci
                        for (a, b) in (c_tiles if zi in (0, 1, 8, 9, 10, 11) else kv_tiles):
                            bnk = fm_proj(slot, ci * 1024, 8, lambda k, a=a, b=b: Ht[:, k, a:b], a, b,
                                          bk("H", range(8), a, b))
                            evac_copy(Zc[zi][:, a:b], PSt[:, bnk, 0:b - a], psk(bnk), bk("Z", [zi], a, b))
                            yield
                    if pj == 3:
                        for blk in kvb:
                            t0 = blk * 128
                            inC = blk in cb_
                            ncol = 384 if inC else 128
                            bnk = nbank()
                            for k in range(8):
                                PE("matmul", [("w", slot)] + bk("H", range(8), t0, t0 + 128), psk(bnk), inc=(k == 7),
                                   out=PSt[:, bnk, 0:ncol], lhsT=Ht[:, k, t0:t0 + 128],
                                   rhs=WRt[:, slot, 1024 + k * 384: 1024 + k * 384 + ncol],
                                   start=(k == 0), stop=(k == 7))
                            ACT("activation", psk(bnk), [("VT", blk)], out=VT[:, blk, :], in_=PSt[:, bnk, 0:128], func=AF.Copy)
                            if inC:
                                s1 = nsm()
                                s2 = nsm()
                                DVE("bn_stats", psk(bnk), [("sm", s1)], out=SMt[:, s1, 0:6], in_=PSt[:, bnk, 128:384])
                                DVE("bn_aggr", [("sm", s1)], [("sm", s2)], out=SMt[:, s2, 0:2], in_=SMt[:, s1, 0:6])
                                ACT("activation", [("sm", s2)] + KC, [("sm", s2)], out=SMt[:, s2, 2:3], in_=SMt[:, s2, 1:2],
                                    func=AF.Sqrt, bias=epsc, scale=1.0)
                                DVE("reciprocal", [("sm", s2)], [("sm", s2)], out=SMt[:, s2, 3:4], in_=SMt[:, s2, 2:3])
                                DVE("tensor_scalar", psk(bnk) + [("sm", s2), ("sm", s2)], [("VS", blk)],
                                    out=VS[:, blk - 1, :], in0=PSt[:, bnk, 128:384], scalar1=SMt[:, s2, 0:1],
                                    scalar2=SMt[:, s2, 3:4], op0=ALU.subtract, op1=ALU.mult)
                            yield

            def gen_conv():
                for cc in range(2):
                    for (a, b) in c_tiles:
                        n = b - a
                        U = scr(12, BF16, n + 2)
                        DVE("tensor_tensor", bk("Z", [2 + cc, 4 + cc], a - 1, b + 1), scrk(12),
                            out=U, in0=Zc[2 + cc][:, a - 1:b + 1], in1=Zc[4 + cc][:, a - 1:b + 1], op=ALU.mult)
                        bnk = nbank()
                        for tap in range(3):
                            o = CB_DG + ((l * 3 + tap) * 2 + cc) * 128
                            PE("matmul", scrk(12) + KC, psk(bnk), inc=(tap == 2), out=PSt[:, bnk, 0:n],
                               lhsT=CB[:, o:o + 128], rhs=U[:, tap:tap + n], start=(tap == 0), stop=(tap == 2))
                        DVE("tensor_tensor", psk(bnk) + bk("Z", [cc], a, b), bk("BR", [cc], a, b),
                            out=BR[:, cc, a - 128:b - 128], in0=Zc[cc][:, a:b], in1=PSt[:, bnk, 0:n], op=ALU.mult)
                        yield

            def gen_pool():
                for cc in range(2):
                    zp = 6 + cc
                    for (a, b) in c_tiles:
                        n = b - a
                        T1 = scr(14, BF16)
                        T2 = scr(15, BF16)
                        T3 = scr(16, BF16)
                        T4 = scr(17, BF16)
                        PL = scr(13, BF16, n)
                        zk = bk("Z", [zp], a - 8, b + 8)
                        DVE("tensor_tensor", zk, scrk(14), out=T1[:, 0:n + 14], in0=Zc[zp][:, a - 8:b + 6],
                            in1=Zc[zp][:, a - 7:b + 7], op=ALU.add)
                        DVE("tensor_tensor", scrk(14), scrk(15), out=T2[:, 0:n + 12], in0=T1[:, 0:n + 12],
                            in1=T1[:, 2:n + 14], op=ALU.add)
                        if cc == 0:
                            sel = [(T1, 7, 14), (T2, 6, 15)]
                        else:
                            yield
                            DVE("tensor_tensor", scrk(15), scrk(16), out=T3[:, 0:n + 8], in0=T2[:, 0:n + 8],
                                in1=T2[:, 4:n + 12], op=ALU.add)
                            DVE("tensor_tensor", scrk(16), scrk(17), out=T4[:, 0:n], in0=T3[:, 0:n],
                                in1=T3[:, 8:n + 8], op=ALU.add)
                            sel = [(T3, 4, 16), (T4, 0, 17)]
                        yield
                        for hf in range(2):
                            Ts, off, si = sel[hf]
                            r0 = hf * 64
                            DVE("scalar_tensor_tensor", scrk(si) + bk("Z", [zp], a, b) + KC, scrk(13),
                                out=PL[r0:r0 + 64, :], in0=Ts[r0:r0 + 64, off:off + n],
                                scalar=CP[r0:r0 + 64, B_IW + cc:B_IW + cc + 1],
                                in1=Zc[zp][r0:r0 + 64, a:b], op0=ALU.mult, op1=ALU.subtract)
                            for side, e0 in ((0, 256), (1, 1272)):
                                if a <= e0 and e0 + 8 <= b:
                                    j0 = e0 - a
                                    tb = B_PT + ((it * 2 + side) * 2 + cc) * 8
                                    sm = nsm()
                                    DVE("tensor_tensor", scrk(si) + KC, [("sm", sm)],
                                        out=SMt[r0:r0 + 64, sm, :], in0=Ts[r0:r0 + 64, off + j0:off + j0 + 8],
                                        in1=CP[r0:r0 + 64, tb:tb + 8], op=ALU.mult)
                                    DVE("tensor_tensor", [("sm", sm)] + bk("Z", [zp], e0, e0 + 8), scrk(13),
                                        out=PL[r0:r0 + 64, j0:j0 + 8], in0=SMt[r0:r0 + 64, sm, :],
                                        in1=Zc[zp][r0:r0 + 64, e0:e0 + 8], op=ALU.subtract)
                        bnk = nbank()
                        PE("matmul", scrk(13) + KC, psk(bnk), out=PSt[:, bnk, 0:n],
                           lhsT=CB[:, CB_PW + (l * 2 + cc) * 128: CB_PW + (l * 2 + cc + 1) * 128],
                           rhs=PL, start=True, stop=True)
                        ACT("activation", psk(bnk) + KC, bk("BR", [2 + cc], a, b),
                            out=BR[:, 2 + cc, a - 128:b - 128], in_=PSt[:, bnk, 0:n], func=AF.Copy,
                            scale=cpc(B_PS + l * 2 + cc))
                        yield

            def gen_sg():
                for blk in cb_:
                    t0 = blk * 128
                    for cc in range(2):
                        bnk = nbank()
                        PE("matmul", [("VS", blk)] + KC, psk(bnk), out=PSt[:, bnk, 0:256],
                           lhsT=VS[:, blk - 1, cc * 128:(cc + 1) * 128],
                           rhs=CB[:, CB_SGW + (l * 2 + cc) * 256: CB_SGW + (l * 2 + cc + 1) * 256],
                           start=True, stop=True)
                        for hf in range(2):
                            r0 = hf * 64
                            tm = PSt[r0:r0 + 64, bnk, 256 + hf * 128:256 + (hf + 1) * 128]
                            DVE("scalar_tensor_tensor", psk(bnk) + KC, psk(bnk),
                                out=tm, in0=PSt[r0:r0 + 64, bnk, hf * 128:(hf + 1) * 128],
                                scalar=CP[r0:r0 + 64, B_SN + l * 2 + cc:B_SN + l * 2 + cc + 1],
                                in1=SGB[r0:r0 + 64, l, cc, :], op0=ALU.mult, op1=ALU.add)
                            DVE("tensor_tensor", psk(bnk) + bk("Z", [8 + cc], t0, t0 + 128), bk("BR", [4 + cc], t0, t0 + 128),
                                out=BR[r0:r0 + 64, 4 + cc, t0 - 128:t0], in0=Zc[8 + cc][r0:r0 + 64, t0:t0 + 128],
                                in1=tm, op=ALU.mult)
                        yield

            sinkv = CP[:, B_SK + l * 4:B_SK + l * 4 + 4]
            nsinkv = CP[:, B_NSK + l * 4:B_NSK + l * 4 + 4]

            def attn_unit(blk, hp, u):
                t0 = blk * 128
                k0 = t0 - 128
                sb0 = (u % 6) * 2
                Pb = SCRt[:, sb0 * SCRB:sb0 * SCRB + 1536].bitcast(BF16).rearrange("p (h k) -> p h k", h=2)
                Pk = scrk(sb0)
                PTs = SCRt[:, (sb0 + 1) * SCRB:(sb0 + 1) * SCRB + 1536].bitcast(BF16).rearrange("p (h k) -> p h k", h=2)
                PTk = scrk(sb0 + 1)
                hds = (2 * hp, 2 * hp + 1)
                r0 = hp * 64
                b0 = try_hold(2)
                while b0 is None:
                    yield
                    b0 = try_hold(2)
                Sk = psk(b0) + psk(b0 + 1)
                for i2, hd in enumerate(hds):
                    qc = 10 + (hd % 2)
                    PE("matmul", bk("Z", [qc], t0, t0 + 128) + bk("Z", [12], k0, k0 + 384), psk(b0 + i2), inc=False,
                       out=PSt[:, b0 + i2, 0:384], lhsT=Zc[qc][r0:r0 + 64, t0:t0 + 128],
                       rhs=Zc[12][r0:r0 + 64, k0:k0 + 384], start=True, stop=False)
                    PE("matmul", KC, psk(b0 + i2), out=PSt[:, b0 + i2, 0:384], lhsT=ident, rhs=BIAS[:, hd, :],
                       start=False, stop=True)
                S2 = PSt[:, b0:b0 + 2, 0:384]
                if blk == 2:
                    DVE("tensor_scalar", Sk + KC, Sk, out=PSt[:, b0:b0 + 2, 0:128], in0=PSt[:, b0:b0 + 2, 0:128],
                        scalar1=cpc(B_FL + it * 4 + 0), scalar2=None, op0=ALU.add)
                if blk == 9:
                    DVE("tensor_scalar", Sk + KC, Sk, out=PSt[:, b0:b0 + 2, 256:384], in0=PSt[:, b0:b0 + 2, 256:384],
                        scalar1=cpc(B_FL + it * 4 + 1), scalar2=None, op0=ALU.add)
                yield
                s1 = nsm()
                s2 = nsm()
                s3 = nsm()
                s4 = nsm()
                SMr = SMt[:, s1, :]
                SMq = SMt[:, s2, :]
                SMs = SMt[:, s3, :]
                SMd = SMt[:, s4, :]
                sk2 = sinkv[:, 2 * hp:2 * hp + 2]
                DVE("tensor_reduce", Sk, [("sm", s1)], out=SMr[:, 0:2], in_=S2, axis=AX.X, op=ALU.max, negate=True)
                DVE("scalar_tensor_tensor", [("sm", s1), ("nsk",)] + KC, [("sm", s2)], out=SMq[:, 0:2], in0=SMr[:, 0:2],
                    scalar=0.125, in1=nsinkv[:, 2 * hp:2 * hp + 2], op0=ALU.mult, op1=ALU.min)
                yield
                for i2 in range(2):
                    ACT("activation", psk(b0 + i2) + [("sm", s2)], Pk + [("sm", s3)],
                        out=Pb[:, i2, :], in_=PSt[:, b0 + i2, 0:384], func=AF.Exp, bias=SMq[:, i2:i2 + 1], scale=0.125,
                        accum_out=SMs[:, i2:i2 + 1])
                    ACT("activation", [("sm", s2)] + KC, [("sm", s3)], out=SMs[:, 4 + i2:5 + i2],
                        in_=sinkv[:, 2 * hp + i2:2 * hp + i2 + 1], func=AF.Exp, bias=SMq[:, i2:i2 + 1], scale=1.0)
                release(b0, 2)
                yield
                DVE("tensor_tensor", [("sm", s3), ("sm", s3), ("sm", s3), ("sm", s3)], [("sm", s4)],
                    out=SMd[:, 0:2], in0=SMs[:, 0:2], in1=SMs[:, 4:6], op=ALU.add)
                DVE("reciprocal", [("sm", s4)], [("sm", s4)], out=SMd[:, 4:6], in_=SMd[:, 0:2])
                yield
                for i2 in range(2):
                    ACT("activation", Pk + [("sm", s4)], Pk, out=Pb[:, i2, :], in_=Pb[:, i2, :], func=AF.Copy,
                        scale=SMd[:, 4 + i2:5 + i2])
                yield
                tbk = try_hold(1)
                while tbk is None:
                    yield
                    tbk = try_hold(1)
                ptv = PSt[:, tbk, :].bitcast(BF16)
                for i2 in range(2):
                    for j in range(3):
                        c0 = i2 * 384 + j * 128
                        PE("transpose", Pk + KC, psk(tbk), out=ptv[:, c0:c0 + 128],
                           in_=Pb[:, i2, j * 128:(j + 1) * 128], identity=ident)
                yield
                evac_copy(PTs.rearrange("p h k -> p (h k)"), ptv[:, 0:768], psk(tbk), PTk)
                release(tbk)
                yield
                ob = try_hold(1)
                while ob is None:
                    yield
                    ob = try_hold(1)
                for i2 in range(2):
                    for j in range(3):
                        PE("matmul", PTk + [("VT", blk - 1 + j)], psk(ob), inc=(j == 2),
                           out=PSt[i2 * 64:i2 * 64 + 64, ob, 0:128],
                           lhsT=VT[:, blk - 1 + j, hp * 64:(hp + 1) * 64], rhs=PTs[:, i2, j * 128:(j + 1) * 128],
                           start=(j == 0), stop=(j == 2))
                yield
                ACT("activation", psk(ob), bk("BR", [6 + hp], t0, t0 + 128),
                    out=BR[:, 6 + hp, t0 - 128:t0], in_=PSt[:, ob, 0:128], func=AF.Copy)
                release(ob)

            for _ in gen_win([3, 2]):
                pass
            import itertools
            others = itertools.chain(gen_win([0, 1]), gen_sg(), gen_conv(), gen_pool())
            others_done = False
            units = [(blk, hp) for blk in cb_ for hp in range(2)]
            active = []
            nxt = 0
            while nxt < len(units) or active or not others_done:
                if nxt < len(units) and len(active) < 6:
                    active.append(attn_unit(units[nxt][0], units[nxt][1], nxt))
                    nxt += 1
                for gen in list(active):
                    try:
                        next(gen)
                    except StopIteration:
                        active.remove(gen)
                for _ in range(2):
                    if not others_done:
                        try:
                            next(others)
                        except StopIteration:
                            others_done = True
            g += 4

            if it == 0 and ("BR%d" % l) in dbg_t:
                dump("BR%d" % l, BR, bk("BR", range(8), 128, 1408))

            chk("p3d")
            T.barrier(["pe", "act", "dve"])
            GB = [0, 1, 2]
            PB = [3, 4, 5]
            MB = [6, 7]
            gctr = [0]
            mctr = [0]
            actr = [0]
            for c in range(8):
                slots2 = [acquire_piece(g, hold_from=g), acquire_piece(g + 1, hold_from=g)]
                g += 2
                for (a, b) in c_tiles:
                    n = b - a
                    ai = actr[0] % 2
                    actr[0] += 1
                    accb = scr(6 + 2 * ai, F32, n, nslots=2)
                    acck = scrk(6 + 2 * ai, 2)
                    for i in range(4):
                        gslot = slots2[i // 2]
                        i2 = i % 2
                        gb = GB[gctr[0] % 3]
                        pb = PB[gctr[0] % 3]
                        gi = gctr[0] % 2
                        gctr[0] += 1
                        for k in range(8):
                            wo = (i2 * 10 + k) * 128
                            PE("matmul", [("w", gslot)] + bk("H", range(8), a, b), psk(gb), inc=(k == 7),
                               out=PSt[:, gb, 0:n], lhsT=WRt[:, gslot, wo:wo + 128],
                               rhs=Ht[:, k, a:b], start=(k == 0), stop=(k == 7))
                        for kk in range(2):
                            wo = (i2 * 10 + 8 + kk) * 128
                            PE("matmul", [("w", gslot)] + bk("BR", [2 * i, 2 * i + 1], a, b), psk(pb), inc=(kk == 1),
                               out=PSt[:, pb, 0:n], lhsT=WRt[:, gslot, wo:wo + 128],
                               rhs=BR[:, 2 * i + kk, a - 128:b - 128], start=(kk == 0), stop=(kk == 1))
                        gsb = scr(2 * gi, F32, n, nslots=2)
                        ACT("activation", psk(gb) + KC, scrk(2 * gi, 2), out=gsb, in_=PSt[:, gb, 0:n], func=AF.Sigmoid,
                            bias=cpc(B_BG + (l * 4 + i) * 8 + c), scale=1.0)
                        if i == 0:
                            DVE("tensor_tensor", scrk(2 * gi, 2) + psk(pb), acck, out=accb, in0=gsb, in1=PSt[:, pb, 0:n],
                                op=ALU.mult)
                        else:
                            tb = MB[mctr[0] % 2]
                            mctr[0] += 1
                            DVE("tensor_tensor", scrk(2 * gi, 2) + psk(pb), psk(tb), out=PSt[:, tb, 0:n], in0=gsb,
                                in1=PSt[:, pb, 0:n], op=ALU.mult)
                            if i < 3:
                                DVE("tensor_tensor", acck + psk(tb), acck, out=accb, in0=accb, in1=PSt[:, tb, 0:n],
                                    op=ALU.add)
                            else:
                                DVE("tensor_tensor", acck + psk(tb), bk("M", [c], a, b), out=Mv[:, c, a - 128:b - 128],
                                    in0=accb, in1=PSt[:, tb, 0:n], op=ALU.add)

            if l == 1 and it + 1 < n_items:
                load_halo(it + 1)
            chk("p4")
            wslots = [acquire_piece(g, hold_from=g), acquire_piece(g + 1, hold_from=g)]
            g += 2
            for (a, b) in c_tiles:
                for op_ in range(2):
                    for c4 in range(4):
                        c = op_ * 4 + c4
                        bnk = fm_proj(wslots[op_], c4 * 1024, 8, lambda k, a=a, b=b: Mv[:, k, a - 128:b - 128], a, b,
                                      bk("M", range(8), a, b))
                        evac_copy(Yv[:, c, a - 128:b - 128], PSt[:, bnk, 0:b - a], psk(bnk), bk("Y", [c], a, b))
                postnorm_residual(l, 1, a, b, Yv, "Y")
                prenorm(l, 2, a, b, xsrc(a, b))
            chk("p6")
            for half in range(2):
                for pp in range(6):
                    slot = acquire_piece(g)
                    g += 1
                    for jj in range(2 if pp < 5 else 1):
                        j = pp * 2 + jj
                        for (a, b) in c_tiles:
                            n = b - a
                            ba = fm_proj(slot, (jj * 2 + 0) * 1024, 8, lambda k, a=a, b=b: Ht[:, k, a:b], a, b,
                                         bk("H", range(8), a, b))
                            bg = fm_proj(slot, (jj * 2 + 1) * 1024, 8, lambda k, a=a, b=b: Ht[:, k, a:b], a, b,
                                         bk("H", range(8), a, b))
                            si = j % 2
                            ssb = scr(2 * si, F32, n, nslots=2)
                            ACT("activation", psk(ba), scrk(2 * si, 2), out=ssb, in_=PSt[:, ba, 0:n], func=AF.Silu)
                            DVE("tensor_tensor", scrk(2 * si, 2) + psk(bg), bk("HID", [j], a, b),
                                out=HID[:, j, a - 128:b - 128], in0=ssb, in1=PSt[:, bg, 0:n], op=ALU.mult)
                for pp in range(4):
                    slot = acquire_piece(g)
                    g += 1
                    for c2 in range(2):
                        c = pp * 2 + c2
                        for (a, b) in c_tiles:
                            n = b - a
                            bnk = fm_proj(slot, c2 * 1408, 11, lambda k, a=a, b=b: HID[:, k, a - 128:b - 128], a, b,
                                          bk("HID", range(11), a, b))
                            y2s = Y2[:, c, a - 128:b - 128]
                            if half == 0:
                                evac_copy(y2s, PSt[:, bnk, 0:n], psk(bnk), bk("Y2", [c], a, b))
                            else:
                                DVE("tensor_tensor", psk(bnk) + bk("Y2", [c], a, b), bk("Y2", [c], a, b),
                                    out=y2s, in0=y2s, in1=PSt[:, bnk, 0:n], op=ALU.add)
            chk("p8")
            for ti, (a, b) in enumerate(c_tiles):
                postnorm_residual(l, 3, a, b, Y2, "Y2")
                if after_p9 is not None:
                    after_p9(ti, a, b)
            assert g == gbase + NPIECE
            if it == 0:
                dump("X%d" % l, Xt[:], bk("X", range(8), 128, 1408))

        epsc = EPS
        XH = SCRt[:, 12 * SCRB:12 * SCRB + 8192].bitcast(F32).rearrange("p (c t) -> p c t", c=8)
        def load_halo(it_):
            T.dma("sync", "xh", dict(out=XH[:, :, 0:128], in_=xin[it_, :, :, 0:128]), w=scrk(12, 6))
            T.dma("sync", "xh", dict(out=XH[:, :, 128:256], in_=xin[it_, :, :, 1408:1536]), w=scrk(12, 6))

        def main_body():
            for it in range(n_items):
                for ti, (a, b) in enumerate(tiles_of(128, 1408)):
                    T.dma("sync", "xa%d" % ti, dict(out=Xt[:, :, a - 128:b - 128], in_=xin[it, :, :, a:b]),
                          w=bk("X", range(8), a, b))
                if it == 0:
                    load_halo(0)
                gbase = (it * 2) * NPIECE

                def hsrc0(c):
                    return XH[:, c, 0:128], scrk(12, 6)

                def hsrc1(c):
                    return XH[:, c, 128:256], scrk(12, 6)
                chk("p0")
                tl = tiles_of(128, 1408)
                prenorm(0, 0, 0, 128, hsrc0)
                prenorm(0, 0, tl[0][0], tl[0][1], xsrc(*tl[0]))
                prenorm(0, 0, tl[1][0], tl[1][1], xsrc(*tl[1]))
                prenorm(0, 0, tl[2][0], tl[2][1], xsrc(*tl[2]))
                prenorm(0, 0, 1408, 1536, hsrc1)
                chk("p1")
                layer(it, 0, 0, 1536, 128, 1408, gbase)
                for (a, b) in tiles_of(128, 1408):
                    prenorm(1, 0, a, b, xsrc(a, b))
                hk = bk("H", range(8), 128, 256)
                DVE("tensor_scalar", hk + KC, hk, out=Ht[:, :, 128:256], in0=Ht[:, :, 128:256],
                    scalar1=cpc(B_FL + it * 4 + 2), scalar2=None, op0=ALU.mult)
                hk = bk("H", range(8), 1280, 1408)
                DVE("tensor_scalar", hk + KC, hk, out=Ht[:, :, 1280:1408], in0=Ht[:, :, 1280:1408],
                    scalar1=cpc(B_FL + it * 4 + 3), scalar2=None, op0=ALU.mult)
                def store_tile(ti, a, b, it=it):
                    T.dma("sync", "out%d" % ti, dict(out=yout[it, :, :, a - 256:b - 256], in_=Xt[:, :, a - 128:b - 128]),
                          r=bk("X", range(8), a, b))
                layer(it, 1, 128, 1408, 256, 1280, gbase + NPIECE, after_p9=store_tile)
        try:
            main_body()
        except _Stop:
            pass
        T.barrier(["pe", "act", "dve"])
        T.final_wait("sync", list(T.dcount.keys()))

        with nc.Block() as block:
            @block.sync
            def _(h):
                T.emit("sync", h)

            @block.gpsimd
            def _(h):
                T.emit("gq", h)

            @block.tensor
            def _(h):
                T.emit("pe", h)

            @block.scalar
            def _(h):
                T.emit("act", h)

            @block.vector
            def _(h):
                T.emit("dve", h)
    return nc


def pack_weights(w_in, w_branch, w_gate, w_out, w_ffn_in, w_ffn_out):
    wst = np.zeros((2, NPIECE, 128, PIECE_E), np.float32)

    def fm(Wcols):
        K = Wcols.shape[0] // 128
        return Wcols.reshape(K, 128, 128).transpose(1, 0, 2)
    qperm = np.concatenate([np.arange(1536, 1600), np.arange(1664, 1728), np.arange(1600, 1664), np.arange(1728, 1792)])
    fmcols = [np.arange(0, 128), np.arange(128, 256), np.arange(256, 384), np.arange(384, 512),
              np.arange(512, 640), np.arange(640, 768), np.arange(768, 896), np.arange(896, 1024),
              np.arange(1024, 1152), np.arange(1152, 1280), qperm[:128], qperm[128:], np.arange(1792, 1920)]
    tmcols = np.concatenate([np.arange(1920, 2048), np.arange(1280, 1536)])
    for l in range(2):
        wp = []
        for pj in range(3):
            parts = [fm(w_in[l][:, fmcols[pj * 4 + ci]]) for ci in range(4)]
            wp.append(np.stack(parts, axis=1).reshape(128, 4096))
        kpart = fm(w_in[l][:, fmcols[12]]).reshape(128, 1024)
        tm = w_in[l][:, tmcols].reshape(8, 128, 384).transpose(1, 0, 2).reshape(128, 3072)
        wp.append(np.concatenate([kpart, tm], axis=1))
        for j, pj in enumerate([3, 2, 0, 1]):
            wst[l, j] = wp[pj]
        j = 4
        for c in range(8):
            for ih in range(2):
                arr = np.zeros((128, 2, 10, 128), np.float32)
                for i2 in range(2):
                    i = ih * 2 + i2
                    arr[:, i2, 0:8] = fm(w_gate[l, i][:, c * 128:(c + 1) * 128])
                    arr[:, i2, 8:10] = fm(w_branch[l, i][:, c * 128:(c + 1) * 128])
                wst[l, j, :, :2560] = arr.reshape(128, 2560)
                j += 1
        for op_ in range(2):
            arr = np.stack([fm(w_out[l][:, (op_ * 4 + c4) * 128:(op_ * 4 + c4 + 1) * 128]) for c4 in range(4)], axis=1)
            wst[l, j] = arr.reshape(128, 4096)
            j += 1
        for half in range(2):
            for pp in range(6):
                njj = 2 if pp < 5 else 1
                arr = np.zeros((128, njj, 2, 8, 128), np.float32)
                for jj in range(njj):
                    hj = half * 11 + pp * 2 + jj
                    for ag in range(2):
                        arr[:, jj, ag] = fm(w_ffn_in[l][:, ag * DFF + hj * 128: ag * DFF + (hj + 1) * 128])
                wst[l, j, :, :njj * 2048] = arr.reshape(128, njj * 2048)
                j += 1
            for pp in range(4):
                arr = np.zeros((128, 2, 11, 128), np.float32)
                for c2 in range(2):
                    c = pp * 2 + c2
                    arr[:, c2] = fm(w_ffn_out[l][half * 11 * 128:(half + 1) * 11 * 128, c * 128:(c + 1) * 128])
                wst[l, j, :, :2816] = arr.reshape(128, 2816)
                j += 1
        assert j == NPIECE
    return wst


def item_list(core):
    items = []
    for a in (0, 8, 16, 24):
        items.append(("p", core, a, 32))
    for a in (16 * core, 16 * core + 8):
        items.append(("s", 0, a, 128))
    return items


def build_inputs(inp):
    f32 = np.float32
    x_prompt = np.asarray(inp["x_prompt"], f32)
    x_sample = np.asarray(inp["x_sample"], f32)
    wst = pack_weights(np.asarray(inp["w_in"], f32), np.asarray(inp["w_branch"], f32), np.asarray(inp["w_gate"], f32),
                       np.asarray(inp["w_out"], f32), np.asarray(inp["w_ffn_in"], f32), np.asarray(inp["w_ffn_out"], f32))
    cp0 = np.zeros((128, NCP), f32)
    gains = [inp["norm_mix_pre"], inp["norm_mix_post"], inp["norm_ffn_pre"], inp["norm_ffn_post"]]
    for l in range(2):
        for w in range(4):
            cp0[:, B_G + (l * 4 + w) * 8: B_G + (l * 4 + w) * 8 + 8] = np.asarray(gains[w], f32)[l].reshape(8, 128).T
        for i in range(4):
            cp0[:, B_BG + (l * 4 + i) * 8: B_BG + (l * 4 + i) * 8 + 8] = np.asarray(inp["b_gate"], f32)[l, i].reshape(8, 128).T
        for tap in range(3):
            cp0[:, B_CW + (l * 3 + tap) * 2: B_CW + (l * 3 + tap) * 2 + 2] = np.asarray(inp["conv_w"], f32)[l, tap].reshape(2, 128).T
        cp0[:, B_PS + l * 2: B_PS + l * 2 + 2] = np.asarray(inp["pool_scale"], f32)[l].reshape(2, 128).T
        cp0[:, B_SN + l * 2: B_SN + l * 2 + 2] = np.asarray(inp["sg_norm"], f32)[l].reshape(2, 128).T
        cp0[:, B_SK + l * 4: B_SK + l * 4 + 4] = np.asarray(inp["attn_sink"], f32)[l][None, :]
    wins = np.zeros((128, 2), np.int64)
    wins[:64, 0] = 2
    wins[64:, 0] = 4
    wins[:64, 1] = 8
    wins[64:, 1] = 16
    cp0[:, B_IW:B_IW + 2] = (1.0 / wins).astype(f32)
    cb = np.zeros((128, NCB), f32)
    sg_w = np.asarray(inp["sg_w"], f32)
    pool_w = np.asarray(inp["pool_w"], f32)
    for l in range(2):
        for cc in range(2):
            for jg in range(2):
                o = CB_SGW + (l * 2 + cc) * 256 + jg * 128
                cb[:, o:o + 128] = sg_w[l, 2 * cc + jg].T
            blk = np.zeros((128, 128), f32)
            blk[:64, :64] = pool_w[l, 2 * cc]
            blk[64:, 64:] = pool_w[l, 2 * cc + 1]
            cb[:, CB_PW + (l * 2 + cc) * 128: CB_PW + (l * 2 + cc + 1) * 128] = blk
    conv_w = np.asarray(inp["conv_w"], f32)
    for l in range(2):
        for tap in range(3):
            for cc in range(2):
                o = CB_DG + ((l * 3 + tap) * 2 + cc) * 128
                dg = np.zeros((128, 128), f32)
                dg[np.arange(128), np.arange(128)] = conv_w[l, tap, cc * 128:(cc + 1) * 128]
                cb[:, o:o + 128] = dg
    cb[:, CB_ID:CB_ID + 128] = np.eye(128, dtype=f32)
    cb[:, CB_ON:CB_ON + 128] = 1.0 / 1024.0
    qi = np.arange(128)[:, None]
    kj = np.arange(384)[None, :]
    rel = np.abs(qi - kj + 128).astype(f32)
    slopes = np.exp2(-8.0 * (np.arange(4, dtype=f32) + 1.0) / 4.0).astype(f32)
    abias = np.where(rel[:, None, :] <= 128, -8.0 * slopes[None, :, None] * rel[:, None, :], f32(-8e30)).astype(f32)
    abias = np.ascontiguousarray(abias.reshape(128, 4 * 384))
    sg_b = np.asarray(inp["sg_b"], f32)
    sgb = np.zeros((128, 2, 2, 128), f32)
    for l in range(2):
        for cc in range(2):
            sgb[:64, l, cc, :] = sg_b[l, 2 * cc][None, :]
            sgb[64:, l, cc, :] = sg_b[l, 2 * cc + 1][None, :]
    sgb = sgb.reshape(128, 512)

    in_maps = []
    for core in range(NCORES):
        items = item_list(core)
        xin = np.zeros((NITEM, 128, 8, NTOK_IN), f32)
        cp = cp0.copy()
        for it, (kind, sidx, a, nblk) in enumerate(items):
            src = x_prompt[sidx] if kind == "p" else x_sample[0]
            S = src.shape[0]
            t0 = (a - 2) * 128
            lo = max(t0, 0)
            hi = min(t0 + NTOK_IN, S)
            seg = src[lo:hi]
            xin[it, :, :, lo - t0:hi - t0] = seg.reshape(-1, 8, 128).transpose(2, 1, 0)
            lvalid = a > 0
            rvalid = (a + 8) < nblk
            cp[:, B_FL + it * 4 + 0] = 0.0 if lvalid else -1e30
            cp[:, B_FL + it * 4 + 1] = 0.0 if rvalid else -1e30
            cp[:, B_FL + it * 4 + 2] = 1.0 if lvalid else 0.0
            cp[:, B_FL + it * 4 + 3] = 1.0 if rvalid else 0.0
            for cc in range(2):
                w = wins[:, cc]
                for jx in range(8):
                    cl = w if lvalid else (jx + w // 2 - np.maximum(jx - w // 2, 0))
                    cr = w if rvalid else (np.minimum(w // 2, 8 - jx) + w // 2)
                    cp[:, B_PT + ((it * 2 + 0) * 2 + cc) * 8 + jx] = (1.0 / cl).astype(f32)
                    cp[:, B_PT + ((it * 2 + 1) * 2 + cc) * 8 + jx] = (1.0 / cr).astype(f32)
        in_maps.append({"xin": xin, "wst": wst, "cp": cp, "cb": cb, "abias": abias, "sgb": sgb})
    return in_maps


def assemble(results):
    yp = np.zeros((8, SEQ_P, D), np.float32)
    ys = np.zeros((1, SEQ_S, D), np.float32)
    for core in range(NCORES):
        y = results[core]["yout"]
        for it, (kind, sidx, a, nblk) in enumerate(item_list(core)):
            blk = y[it].transpose(2, 1, 0).reshape(NOUT, D)
            if kind == "p":
                yp[sidx, a * 128:a * 128 + NOUT] = blk
            else:
                ys[0, a * 128:a * 128 + NOUT] = blk
    return yp, ys


_DEBUG = None
_LAST = {}


def kernel(**inputs):
    in_maps = build_inputs(inputs)
    nc = build_program(_DEBUG)
    res = run_bass_kernel_spmd(nc, in_maps, core_ids=list(range(NCORES)))
    _LAST["res"] = res
    yp, ys = assemble(res.results)
    return (yp, ys)
```
